# Optimizing a Trainium2 kernel written in Bass

```python
import math
import jax, jax.numpy as jnp
from jax import lax
import numpy as np

D_MODEL = 1024
BATCH = 8
SEQ = 8192
DEPTH = 1
DEC_BATCH = 8
DEC_SEQ = 32
PAST_LEN = 1024

CHUNK = 64
Q_BLOCK = 128
N_HEADS = 8
HEAD_DIM = 64
V_DIM = 2 * HEAD_DIM
ATTN_DIM = N_HEADS * V_DIM
CONV_DIM = D_MODEL
CONV_WIDTH = 3
D_FF = -(-8 * D_MODEL // (3 * 256)) * 256
EPS = 1e-6
IN_SIZES = (N_HEADS * 2 * HEAD_DIM,
            N_HEADS * 2 * HEAD_DIM,
            ATTN_DIM,
            CONV_DIM,
            CONV_DIM,
            CONV_DIM,
            D_MODEL,
            D_MODEL)
IN_COLS = sum(IN_SIZES)
IN_SPLITS = tuple(int(s) for s in np.cumsum(IN_SIZES)[:-1])

kernel_name = "hybrid_diffattn_shortconv_streaming_step"


def _rms_norm(x, g):
    xf = x.astype(jnp.float32)
    y = xf * lax.rsqrt(jnp.mean(xf * xf, axis=-1, keepdims=True) + EPS)
    return (y * g.astype(jnp.float32)).astype(x.dtype)


def _lambda_init(layer_idx):
    return 0.8 - 0.6 * math.exp(-0.3 * layer_idx)


def _diff_core(q, k, v, mask, lam):
    s = jnp.einsum('bqhmd,bkhmd->bhmqk', q, k,
                   preferred_element_type=jnp.float32) * (HEAD_DIM ** -0.5)
    if mask is not None:
        s = jnp.where(mask, s, -jnp.inf)
    p = jax.nn.softmax(s, axis=-1)
    a = p[:, :, 0] - lam * p[:, :, 1]
    return jnp.einsum('bhqk,bkhd->bqhd', a.astype(v.dtype), v)


def _attend_prompt(q, k, v, lam):
    B, S = q.shape[0], q.shape[1]
    nb = S // Q_BLOCK
    qb = q.reshape(B, nb, Q_BLOCK, N_HEADS, 2, HEAD_DIM).swapaxes(0, 1)
    kpos = jnp.arange(S)

    def one(args):
        i, qi = args
        qpos = i * Q_BLOCK + jnp.arange(Q_BLOCK)
        limit = (qpos // CHUNK + 1) * CHUNK
        mask = kpos[None, :] < limit[:, None]
        return _diff_core(qi, k, v, mask, lam)

    out = lax.map(one, (jnp.arange(nb), qb))
    return out.swapaxes(0, 1).reshape(B, S, N_HEADS, V_DIM)


def _short_conv(u, prev, w):
    T = u.shape[1]
    up = jnp.concatenate([prev, u], axis=1)
    y = up[:, 0:T] * w[0]
    for j in range(1, CONV_WIDTH):
        y = y + up[:, j:j + T] * w[j]
    return y, up[:, -(CONV_WIDTH - 1):]


def _layer(x, c, past_k, past_v, conv_prev, lambda_init,
           w_ada, b_ada, norm1_g, norm2_g, w_in, q_norm_g, k_norm_g,
           lambda_q1, lambda_k1, lambda_q2, lambda_k2, sub_norm_g,
           w_attn_out, conv_w, w_conv_out, w_out, w_gate_up, w_down):
    B, T = x.shape[0], x.shape[1]
    mod = (jax.nn.silu(c) @ w_ada + b_ada).reshape(B, 6, 1, D_MODEL)
    shift1, scale1, gate1, shift2, scale2, gate2 = [mod[:, i] for i in range(6)]

    h = _rms_norm(x, norm1_g) * (1 + scale1) + shift1
    proj = h @ w_in
    q, k, v, xin, gb, gc, ga_br, gb_br = jnp.split(proj, IN_SPLITS, axis=-1)

    q = _rms_norm(q.reshape(B, T, N_HEADS, 2, HEAD_DIM), q_norm_g)
    k = _rms_norm(k.reshape(B, T, N_HEADS, 2, HEAD_DIM), k_norm_g)
    v = v.reshape(B, T, N_HEADS, V_DIM)
    lam = (jnp.exp(jnp.sum(lambda_q1.astype(jnp.float32) * lambda_k1.astype(jnp.float32)))
           - jnp.exp(jnp.sum(lambda_q2.astype(jnp.float32) * lambda_k2.astype(jnp.float32)))
           + lambda_init)
    if past_k is None:
        o = _attend_prompt(q, k, v, lam)
    else:
        k_all = jnp.concatenate([past_k, k], axis=1)
        v_all = jnp.concatenate([past_v, v], axis=1)
        o = _diff_core(q, k_all, v_all, None, lam)
    o = _rms_norm(o, sub_norm_g) * (1.0 - lambda_init)
    y_a = o.reshape(B, T, ATTN_DIM) @ w_attn_out

    u = gc * xin
    cv, conv_tail = _short_conv(u, conv_prev, conv_w)
    y_b = (gb * cv) @ w_conv_out

    m = jax.nn.sigmoid(ga_br) * y_a + jax.nn.sigmoid(gb_br) * y_b
    x = x + gate1 * (m @ w_out)

    h2 = _rms_norm(x, norm2_g) * (1 + scale2) + shift2
    g, up = jnp.split(h2 @ w_gate_up, 2, axis=-1)
    x = x + gate2 * ((jax.nn.silu(g) * up) @ w_down)
    return x, k, v, conv_tail


def setup_inputs(seed: int = 0) -> dict:
    key = jax.random.key(seed)
    ks = jax.random.split(key, 28)
    f32 = jnp.float32

    def nrm(k, shape, scale):
        return jax.random.normal(k, shape, f32) * scale

    return {
        "x_prompt": nrm(ks[0], (BATCH, SEQ, D_MODEL), 1.0),
        "x_sample": nrm(ks[1], (DEC_BATCH, DEC_SEQ, D_MODEL), 1.0),
        "cache_k": nrm(ks[2], (DEPTH, DEC_BATCH, PAST_LEN, N_HEADS, 2, HEAD_DIM), 1.0),
        "cache_v": nrm(ks[3], (DEPTH, DEC_BATCH, PAST_LEN, N_HEADS, V_DIM), 1.0),
        "state_conv": nrm(ks[4], (DEPTH, DEC_BATCH, CONV_WIDTH - 1, CONV_DIM), 1.0),
        "c_prompt": nrm(ks[5], (BATCH, D_MODEL), 1.0),
        "c_sample": nrm(ks[6], (DEC_BATCH, D_MODEL), 1.0),
        "w_ada": nrm(ks[7], (DEPTH, D_MODEL, 6 * D_MODEL), 0.5 * D_MODEL ** -0.5),
        "b_ada": nrm(ks[8], (DEPTH, 6 * D_MODEL), 0.02),
        "norm1_g": 1.0 + nrm(ks[9], (DEPTH, D_MODEL), 0.02),
        "norm2_g": 1.0 + nrm(ks[10], (DEPTH, D_MODEL), 0.02),
        "w_in": nrm(ks[11], (DEPTH, D_MODEL, IN_COLS), D_MODEL ** -0.5),
        "q_norm_g": 1.0 + nrm(ks[12], (DEPTH, HEAD_DIM), 0.02),
        "k_norm_g": 1.0 + nrm(ks[13], (DEPTH, HEAD_DIM), 0.02),
        "lambda_q1": nrm(ks[14], (DEPTH, HEAD_DIM), 0.1),
        "lambda_k1": nrm(ks[15], (DEPTH, HEAD_DIM), 0.1),
        "lambda_q2": nrm(ks[16], (DEPTH, HEAD_DIM), 0.1),
        "lambda_k2": nrm(ks[17], (DEPTH, HEAD_DIM), 0.1),
        "sub_norm_g": 1.0 + nrm(ks[18], (DEPTH, V_DIM), 0.02),
        "w_attn_out": nrm(ks[19], (DEPTH, ATTN_DIM, D_MODEL), ATTN_DIM ** -0.5),
        "conv_w": nrm(ks[20], (DEPTH, CONV_WIDTH, CONV_DIM), CONV_WIDTH ** -0.5),
        "w_conv_out": nrm(ks[21], (DEPTH, CONV_DIM, D_MODEL), CONV_DIM ** -0.5),
        "w_out": nrm(ks[22], (DEPTH, D_MODEL, D_MODEL), D_MODEL ** -0.5),
        "w_gate_up": nrm(ks[23], (DEPTH, D_MODEL, 2 * D_FF), D_MODEL ** -0.5),
        "w_down": nrm(ks[24], (DEPTH, D_FF, D_MODEL), D_FF ** -0.5),
    }


def reference(x_prompt, x_sample, cache_k, cache_v, state_conv, c_prompt, c_sample,
              w_ada, b_ada, norm1_g, norm2_g, w_in, q_norm_g, k_norm_g,
              lambda_q1, lambda_k1, lambda_q2, lambda_k2, sub_norm_g,
              w_attn_out, conv_w, w_conv_out, w_out, w_gate_up, w_down):
    xp, xs = x_prompt, x_sample
    kp_l, vp_l, cp_l, ksl, vsl, csl = [], [], [], [], [], []
    for l in range(DEPTH):
        lam0 = _lambda_init(l)
        wl = (w_ada[l], b_ada[l], norm1_g[l], norm2_g[l], w_in[l], q_norm_g[l], k_norm_g[l],
              lambda_q1[l], lambda_k1[l], lambda_q2[l], lambda_k2[l], sub_norm_g[l],
              w_attn_out[l], conv_w[l], w_conv_out[l], w_out[l], w_gate_up[l], w_down[l])
        zeros_prev = jnp.zeros((xp.shape[0], CONV_WIDTH - 1, CONV_DIM), xp.dtype)
        xp, kp, vp, cp = _layer(xp, c_prompt, None, None, zeros_prev, lam0, *wl)
        xs, ksn, vsn, csn = _layer(xs, c_sample, cache_k[l], cache_v[l], state_conv[l], lam0, *wl)
        kp_l.append(kp); vp_l.append(vp); cp_l.append(cp)
        ksl.append(ksn); vsl.append(vsn); csl.append(csn)
    k_prompt = jnp.stack(kp_l)
    v_prompt = jnp.stack(vp_l)
    conv_prompt = jnp.stack(cp_l)
    k_sample = jnp.stack(ksl)
    v_sample = jnp.stack(vsl)
    conv_sample = jnp.stack(csl)
    return (xp, xs, k_prompt, v_prompt, conv_prompt, k_sample, v_sample, conv_sample)
```

```python
import numpy as np
import concourse.bass as bass
import concourse.mybir as mybir
from concourse.bass_utils import run_bass_kernel_spmd

F32 = mybir.dt.float32
BF = mybir.dt.bfloat16
AF = mybir.ActivationFunctionType
ALU = mybir.AluOpType
AX = mybir.AxisListType

D = 1024
KC = 8
NH = 8
DFF = 2816
NF = 22
EPS = 1e-6
LAM_INIT = 0.2
PAST = 1024
TS = 32
SAME_ENG_WINDOW = 8
NCHAN = 16


class Buf:
    __slots__ = ("name", "w", "r")

    def __init__(self, name=""):
        self.name = name
        self.w = None
        self.r = []


class Op:
    __slots__ = ("eng", "fn", "deps", "is_dma", "needs_inc", "sem", "val", "pos", "noself")

    def __init__(self, eng, fn, is_dma):
        self.eng = eng
        self.fn = fn
        self.deps = []
        self.is_dma = is_dma
        self.needs_inc = False
        self.sem = None
        self.val = 0
        self.pos = 0
        self.noself = False


class Sched:
    ENGS = ("pe", "act", "dve", "pool", "sp")

    def __init__(self):
        self.ops = []
        self.per_eng = {e: [] for e in self.ENGS}
        self.dma_hist = {e: [] for e in self.ENGS}
        self.barrier_deps = {e: [] for e in self.ENGS}

    def add(self, eng, fn, R=(), W=(), dma=False, noself=False):
        op = Op(eng, fn, dma)
        op.noself = noself
        deps = set()
        for b in R:
            if b.w is not None:
                deps.add(b.w)
        for b in W:
            if b.w is not None:
                deps.add(b.w)
            for r in b.r:
                deps.add(r)
        for d in self.barrier_deps[eng]:
            deps.add(d)
        self.barrier_deps[eng] = []
        if dma:
            h = self.dma_hist[eng]
            if len(h) >= NCHAN:
                deps.add(h[-NCHAN])
            h.append(op)
        deps.discard(op)
        op.deps = list(deps)
        for b in R:
            b.r.append(op)
        for b in W:
            b.w = op
            b.r = []
        op.pos = len(self.per_eng[eng])
        self.per_eng[eng].append(op)
        self.ops.append(op)
        return op

    def barrier(self):
        last = []
        for e in self.ENGS:
            ops = self.per_eng[e]
            nd = [o for o in ops[-200:] if not o.is_dma]
            if nd:
                last.append(nd[-1])
            last.extend(self.dma_hist[e][-NCHAN:])
        for e in self.ENGS:
            self.barrier_deps[e] = list(last)

    def emit(self, nc):
        for op in self.ops:
            real = []
            for d in op.deps:
                if not d.is_dma and d.eng == op.eng:
                    if op.eng == "pe" and not op.is_dma:
                        continue
                    if op.pos - d.pos >= SAME_ENG_WINDOW or op.noself:
                        continue
                real.append(d)
                d.needs_inc = True
            op.deps = real
        from contextlib import ExitStack
        with ExitStack() as es:
            csem = {e: es.enter_context(nc.semaphore("c_" + e)) for e in ("pe", "act", "dve", "pool")}
            dsem = {}
            for e in ("sp", "pool", "act"):
                dsem[e] = [es.enter_context(nc.semaphore("d_%s%d" % (e, i))) for i in range(NCHAN)]
            cnt = {e: 0 for e in csem}
            dcnt = {e: [0] * NCHAN for e in dsem}
            dn = {e: 0 for e in dsem}
            for op in self.ops:
                if op.is_dma:
                    i = dn[op.eng] % NCHAN
                    dn[op.eng] += 1
                    dcnt[op.eng][i] += 16
                    op.sem = dsem[op.eng][i]
                    op.val = dcnt[op.eng][i]
                elif op.needs_inc:
                    cnt[op.eng] += 1
                    op.sem = csem[op.eng]
                    op.val = cnt[op.eng]
            block = es.enter_context(nc.Block())

            def run_stream(ename, e):
                waited = {}
                for op in self.per_eng[ename]:
                    need = {}
                    for d in op.deps:
                        k = id(d.sem)
                        if waited.get(k, 0) >= d.val:
                            continue
                        if k not in need or need[k][1] < d.val:
                            need[k] = (d.sem, d.val)
                    for k, (s, v) in need.items():
                        e.wait_ge(s, v)
                        waited[k] = v
                    ins = op.fn(e)
                    if op.is_dma:
                        ins.then_inc(op.sem, 16)
                    elif op.needs_inc:
                        ins.then_inc(op.sem, 1)
                if ename in dsem:
                    for i in range(NCHAN):
                        if dcnt[ename][i] > 0:
                            e.wait_ge(dsem[ename][i], dcnt[ename][i])

            @block.sync
            def _(e):
                run_stream("sp", e)

            @block.tensor
            def _(e):
                run_stream("pe", e)

            @block.vector
            def _(e):
                run_stream("dve", e)

            @block.scalar
            def _(e):
                run_stream("act", e)

            @block.gpsimd
            def _(e):
                run_stream("pool", e)


class T:
    def __init__(self, nc, name, shape, dtype, psum=False, es=None):
        name = "t_" + name
        if psum:
            self.t = nc.alloc_psum_tensor(name, list(shape), dtype)
        elif es is not None:
            self.t = es.enter_context(nc.sbuf_tensor(name, list(shape), dtype))
        else:
            self.t = nc.alloc_sbuf_tensor(name, list(shape), dtype)
        self.b = Buf(name)

    def __getitem__(self, k):
        return self.t[k]


class Ring:
    def __init__(self, items):
        self.items = items
        self.i = 0

    def next(self):
        it = self.items[self.i % len(self.items)]
        self.i += 1
        return it


def build(S):
    assert S % 512 == 0
    NB = S // 512
    nc = bass.Bass("TRN2", target_bir_lowering=False)
    sc = Sched()

    def din(name, shape):
        return nc.dram_tensor(name, list(shape), F32, kind="ExternalInput").ap()

    def dout(name, shape):
        return nc.dram_tensor(name, list(shape), F32, kind="ExternalOutput").ap()

    def dscr(name, shape):
        return nc.dram_tensor(name, list(shape), BF, kind="Internal").ap()

    x_d = din("x", [S, D]); xs_d = din("xs", [TS, D])
    ck_d = din("ck", [PAST, D]); cv_d = din("cv", [PAST, D])
    sconvT_d = din("sconvT", [128, 8, 2]); cT_d = din("cT", [128, 8, 2])
    wada_d = din("w_ada", [D, 6 * D]); bada_d = din("b_ada", [1, 6 * D]); badaT_d = din("b_adaT", [128, 48])
    n1g_d = din("n1g", [128, 8]); n2g_d = din("n2g", [128, 8])
    win_d = din("w_in", [D, 8 * D])
    qg2_d = din("qg2", [128, 1]); kg2_d = din("kg2", [128, 1]); kgrow_d = din("kgrow", [1, D])
    lam4_d = din("lam4", [1, 256]); subg_d = din("subg", [128, 1])
    wao_d = din("w_attn_out", [D, D]); cwT_d = din("cwT", [128, 8, 3])
    wco_d = din("w_conv_out", [D, D]); wo_d = din("w_out", [D, D])
    wgu_d = din("w_gate_up", [D, 2 * DFF]); wd_d = din("w_down", [DFF, D])
    consts_d = din("consts", [128, 256])

    y_d = dout("y", [S, D]); ys_d = dout("ys", [TS, D])
    ko_d = dout("ko", [S, D]); vo_d = dout("vo", [S, D]); co_d = dout("co", [2, D])
    kso_d = dout("kso", [TS, D]); vso_d = dout("vso", [TS, D]); cso_d = dout("cso", [2, D])

    Wq = dscr("Wq", [2, 128, 8, 512]); Wk = dscr("Wk", [2, 128, 8, 512]); Wv = dscr("Wv", [2, 128, 8, 512])
    Wcv = dscr("Wcv", [8, 128, 8, 384]); Wmg = dscr("Wmg", [8, 128, 8, 512]); Wo = dscr("Wo", [2, 128, 8, 512])
    Wgu = dscr("Wgu", [NF, 128, 8, 256]); Wd = dscr("Wd", [2, 128, NF, 512])
    KVp = dscr("KVp", [NH, NB, 128, 1024]); KVs = dscr("KVs", [NH, 3, 128, 1024])
    dbuf = {}

    def DB(key):
        if key not in dbuf:
            dbuf[key] = Buf(str(key))
        return dbuf[key]

    def P(name, shape, dt=F32):
        return T(nc, name, shape, dt)

    cst = P("cst", [128, 256]); ident = P("ident", [128, 128], BF); bones = P("bones", [128, 128], BF)
    ones = P("ones", [128, 128], BF); ones128 = P("ones128", [128, 128], BF)
    identf = cst
    epsT = P("epsT", [128, 1])
    modT = P("modT", [128, 48, 2]); g1p = P("g1p", [128, 2, 8]); g2p = P("g2p", [128, 2, 8])
    n1g = P("n1g", [128, 8]); n2g = P("n2g", [128, 8])
    qg2 = P("qg2", [128, 1]); kg2 = P("kg2", [128, 1]); subg = P("subg", [128, 1]); subg2 = P("subg2", [128, 1])
    neglam = P("neglam", [128, 1]); lamb = P("lamb", [128, 256]); lprod = P("lprod", [128, 128]); lsum = P("lsum", [128, 2])
    lexp = P("lexp", [128, 2]); lam1 = P("lam1", [128, 1])
    cw = P("cw", [128, 8, 3]); utail = [P("utail0", [128, 8, 2]), P("utail1", [128, 8, 2])]
    gateb = [P("gateb0", [128, 2048]), P("gateb1", [128, 2048])]
    kgb = P("kgb", [128, D])
    badaT = P("badaT", [128, 48]); cT = P("cT", [128, 8, 2]); scT = P("scT", [128, 8, 2])
    ss = P("ss", [128, 4]); rs = P("rs", [128, 4]); ssk = P("ssk", [128, 8])

    pall = nc.alloc_psum_tensor("t_pall", [128, 4096], F32)

    class BankV:
        def __init__(self, i):
            self.ap = pall[:, i * 512:(i + 1) * 512]
            self.b = Buf("bank%d" % i)

        def __getitem__(self, k):
            return self.ap[k]

    banks = [BankV(i) for i in range(8)]
    bring = Ring(banks)
    onesf = P("onesf", [128, 128])

    def dma(q, out, in_, R, W, **kw):
        return sc.add(q, lambda e: e.dma_start(out=out, in_=in_, **kw), R, W, dma=True)

    def mm(out, lhsT, rhs, start, stop, R, W):
        return sc.add("pe", lambda e: e.matmul(out, lhsT=lhsT, rhs=rhs, start=start, stop=stop), R, W)

    def tr(out, in_, idn, R, W):
        return sc.add("pe", lambda e: e.transpose(out, in_, idn), R, W)

    def act(out, in_, func, R, W, **kw):
        return sc.add("act", lambda e: e.activation(out=out, in_=in_, func=func, **kw), R, W)

    def ts(eng, out, in0, s1, s2, op0, op1, R, W):
        if op1 is None:
            return sc.add(eng, lambda e: e.tensor_scalar(out=out, in0=in0, scalar1=s1, scalar2=None, op0=op0), R, W)
        return sc.add(eng, lambda e: e.tensor_scalar(out=out, in0=in0, scalar1=s1, scalar2=s2, op0=op0, op1=op1), R, W)

    def tt(eng, out, in0, in1, op, R, W):
        return sc.add(eng, lambda e: e.tensor_tensor(out=out, in0=in0, in1=in1, op=op), R, W)

    def stt(out, in0, scalar, in1, op0, op1, R, W):
        return sc.add("dve", lambda e: e.scalar_tensor_tensor(out=out, in0=in0, scalar=scalar, in1=in1, op0=op0, op1=op1), R, W)

    def cp(eng, out, in_, R, W):
        return sc.add(eng, lambda e: e.tensor_copy(out=out, in_=in_), R, W)

    def recip(out, in_, R, W):
        act(out, in_, AF.Ln, R, W)
        return act(out, out, AF.Exp, W, W, scale=-1.0)

    def mset(eng, ap, val, W):
        return sc.add(eng, lambda e: e.memset(ap, val), (), W)

    def rsqrt_inplace(ap, b):
        act(ap, ap, AF.Ln, [b], [b])
        act(ap, ap, AF.Exp, [b], [b], scale=-0.5)

    from contextlib import ExitStack
    dma("sp", cst[:], consts_d[:, :], [], [cst.b])
    cp("dve", ident[:], cst[:, 0:128], [cst.b], [ident.b])
    cp("dve", bones[:], cst[:, 128:256], [cst.b], [bones.b])
    mset("dve", ones[:], 1.0, [ones.b])
    mset("dve", ones128[:], 1.0 / 128, [ones128.b])
    mset("dve", onesf[:], 1.0, [onesf.b])
    mset("dve", epsT[:], EPS, [epsT.b])
    mset("dve", utail[0][:], 0.0, [utail[0].b])
    for (t_, d_) in ((n1g, n1g_d), (n2g, n2g_d), (qg2, qg2_d), (kg2, kg2_d), (subg, subg_d), (badaT, badaT_d)):
        dma("sp", t_[:], d_[:, :], [], [t_.b])
    dma("sp", cw[:], cwT_d[:, :, :], [], [cw.b])
    dma("sp", cT[:], cT_d[:, :, :], [], [cT.b])
    dma("sp", utail[1][:], sconvT_d[:, :, :], [], [utail[1].b])
    dma("sp", lamb[:], lam4_d[0].partition_broadcast(128), [], [lamb.b])
    dma("sp", kgb[:], kgrow_d[0].partition_broadcast(128), [], [kgb.b])
    tt("dve", lprod[:, 0:64], lamb[:, 0:64], lamb[:, 64:128], ALU.mult, [lamb.b], [lprod.b])
    tt("dve", lprod[:, 64:128], lamb[:, 128:192], lamb[:, 192:256], ALU.mult, [lamb.b, lprod.b], [lprod.b])
    sc.add("dve", lambda e: e.tensor_reduce(out=lsum[:, 0:2], in_=lprod[:].rearrange("p (g d) -> p g d", d=64),
                                            axis=AX.X, op=ALU.add), [lprod.b], [lsum.b])
    act(lexp[:], lsum[:], AF.Exp, [lsum.b], [lexp.b])
    tt("dve", lam1[:], lexp[:, 1:2], lexp[:, 0:1], ALU.subtract, [lexp.b], [lam1.b])
    ts("dve", neglam[:], lam1[:], -LAM_INIT, None, ALU.add, None, [lam1.b], [neglam.b])
    ts("dve", subg2[:], subg[:], 1.0 - LAM_INIT, None, ALU.mult, None, [subg.b], [subg2.b])
    act(scT[:], cT[:], AF.Silu, [cT.b], [scT.b])

    es = ExitStack()

    def PT_(name, shape, dt=F32):
        return T(nc, name, shape, dt, es=es)

    wslots = Ring([PT_("wadas%d" % i, [128, 8, 512]) for i in range(2)])
    scb = [PT_("scb%d" % r, [128, 8, 128]) for r in range(2)]
    badab = PT_("badab", [128, 2048])
    for r in range(2):
        cp("dve", scb[r][:], scT[:, :, r:r + 1].to_broadcast([128, 8, 128]), [scT.b], [scb[r].b])
    dma("sp", badab[:, 0:1024], bada_d[0, 2048:3072].partition_broadcast(128), [], [badab.b])
    dma("sp", badab[:, 1024:2048], bada_d[0, 5120:6144].partition_broadcast(128), [badab.b], [badab.b])
    modbank = bring.next()
    wada_v = wada_d.rearrange("(k p) c -> p k c", p=128)

    def mod_block(blk):
        wsl = wslots.next()
        dma("sp", wsl[:], wada_v[:, :, blk * 512:(blk + 1) * 512], [], [wsl.b])
        for m4 in range(4):
            m = blk * 4 + m4
            for kc in range(KC):
                mm(modbank[:, 2 * m:2 * m + 2], wsl[:, kc, m4 * 128:(m4 + 1) * 128], scT[:, kc, :],
                   kc == 0, kc == KC - 1, [wsl.b, scT.b], [modbank.b])
        if blk in (4, 5, 10, 11):
            gi = 0 if blk < 6 else 1
            cgi = blk % 2
            for r in range(2):
                bk = bring.next()
                if bk is modbank:
                    bk = bring.next()
                for kc in range(KC):
                    mm(bk[:, :], scb[r][:, kc, :], wsl[:, kc, :], kc == 0, kc == KC - 1, [wsl.b, scb[r].b], [bk.b])
                off = gi * 1024 + cgi * 512
                tt("dve", gateb[r][:, off:off + 512], bk[:, :], badab[:, off:off + 512], ALU.add,
                   [bk.b, badab.b], [gateb[r].b])

    prep_jobs = []
    stf = Ring([PT_("stf%d" % i, [128, 2048]) for i in range(6)])
    stb = Ring([PT_("stb%d" % i, [128, 2048], BF) for i in range(8)])
    casters = Ring(["dve", "act", "pool", "dve", "act", "dve"])

    def prep_piece(src, kc, c0, c1, mapping):
        f = stf.next(); b = stb.next()
        dma("sp", f[:, 0:c1 - c0], src[kc * 128:(kc + 1) * 128, c0:c1], [], [f.b])
        ce = casters.next()
        if ce == "act":
            act(b[:, 0:c1 - c0], f[:, 0:c1 - c0], AF.Copy, [f.b], [b.b])
        else:
            cp(ce, b[:, 0:c1 - c0], f[:, 0:c1 - c0], [f.b], [b.b])
        for (dst, srcfn) in mapping(kc, c0, c1):
            dma("act", dst, srcfn(b, c0), [b.b], [])

    def prep(src, K, N, mapping):
        nkc = K // 128
        for kc in range(nkc):
            for c0 in range(0, N, 2048):
                c1 = min(N, c0 + 2048)
                prep_jobs.append((src, kc, c0, c1, mapping))

    def map_simple(stream, sname, gw, col_lo, col_hi, src_off=0):
        def f(kc, c0, c1):
            out = []
            a = max(c0, col_lo); bnd = min(c1, col_hi)
            c = a
            while c < bnd:
                g = (c - col_lo) // gw
                e_ = min(bnd, col_lo + (g + 1) * gw)
                o = c - col_lo - g * gw
                out.append((stream[g, :, kc, o:o + (e_ - c)], lambda b, c0, c=c, e_=e_: b[:, c - c0:e_ - c0]))
                c = e_
            return out
        return f

    def map_chunks(specs):
        def f(kc, c0, c1):
            out = []
            for (col_lo, nch, stream, off) in specs:
                i0 = max(0, -(-(c0 - col_lo) // 128))
                i1 = min(nch, (c1 - col_lo) // 128)
                if i1 <= i0:
                    continue
                a0 = col_lo + i0 * 128; a1 = col_lo + i1 * 128
                out.append((stream[i0:i1, :, kc, off:off + 128].rearrange("g p c -> p g c"),
                            lambda b, c0, a0=a0, a1=a1: b[:, a0 - c0:a1 - c0].rearrange("p (g c) -> p g c", c=128)))
            return out
        return f

    def map_multi(fs):
        def f(kc, c0, c1):
            out = []
            for g in fs:
                out.extend(g(kc, c0, c1))
            return out
        return f

    prep(win_d, D, 8 * D, map_multi([
        map_simple(Wq, "Wq", 512, 0, 1024), map_simple(Wk, "Wk", 512, 1024, 2048), map_simple(Wv, "Wv", 512, 2048, 3072),
        map_chunks([(3072, 8, Wcv, 0), (4096, 8, Wcv, 128), (5120, 8, Wcv, 256),
                    (6144, 8, Wmg, 0), (7168, 8, Wmg, 128)])]))
    prep(wao_d, D, D, map_chunks([(0, 8, Wmg, 256)]))
    prep(wco_d, D, D, map_chunks([(0, 8, Wmg, 384)]))
    prep(wo_d, D, D, map_simple(Wo, "Wo", 512, 0, 1024))
    prep(wgu_d, D, 2 * DFF, map_chunks([(0, NF, Wgu, 0), (DFF, NF, Wgu, 128)]))
    prep(wd_d, DFF, D, map_simple(Wd, "Wd", 512, 0, 1024))
    nj = len(prep_jobs)
    per = max(1, nj // 12)
    mb = 0
    for ji, job in enumerate(prep_jobs):
        if ji % per == 0 and mb < 12:
            mod_block(mb); mb += 1
        prep_piece(*job)
    while mb < 12:
        mod_block(mb); mb += 1
    tt("dve", modT[:], modbank[:, 0:96].rearrange("p (m r) -> p m r", r=2),
       badaT[:].unsqueeze(2).to_broadcast([128, 48, 2]), ALU.add, [modbank.b, badaT.b], [modT.b])
    for r in range(2):
        stt(g1p[:, r, :], modT[:, 8:16, r], 1.0, n1g[:], ALU.add, ALU.mult, [modT.b, n1g.b], [g1p.b])
        stt(g2p[:, r, :], modT[:, 32:40, r], 1.0, n2g[:], ALU.add, ALU.mult, [modT.b, n2g.b], [g2p.b])


    ckf = Ring([PT_("ckf%d" % i, [128, D]) for i in range(2)])
    ckb = Ring([PT_("ckb%d" % i, [128, D], BF) for i in range(2)])
    kts_s = PT_("kts_s", [128, NH, 512], BF)
    vs_s = PT_("vs_s", [128, NH, 4, 128], BF)
    for kb in range(2):
        for j in range(4):
            t0 = (kb * 4 + j) * 128
            f = ckf.next(); b = ckb.next()
            dma("sp", f[:], ck_d[t0:t0 + 128, :], [], [f.b])
            cp("dve", b[:], f[:], [f.b], [b.b])
            bk = bring.next()
            bkb = bk[:].bitcast(BF)
            for h in range(NH):
                tr(bkb[:, h * 128:(h + 1) * 128], b[:, h * 128:(h + 1) * 128], ident[:], [b.b, ident.b], [bk.b])
            cp("dve", kts_s[:, :, j * 128:(j + 1) * 128], bkb[:, :].rearrange("p (h c) -> p h c", c=128), [bk.b], [kts_s.b])
            f = ckf.next()
            dma("sp", f[:], cv_d[t0:t0 + 128, :], [], [f.b])
            cp("pool", vs_s[:, :, j, :], f[:].rearrange("p (h d) -> p h d", d=128), [f.b], [vs_s.b])
        dma("pool", KVs[:, kb, :, 0:512].rearrange("h p c -> p h c"), kts_s[:], [kts_s.b], [DB(("KVs", kb, "k"))])
        dma("pool", KVs[:, kb, :, 512:1024].rearrange("h p (t d) -> p h t d", d=128), vs_s[:], [vs_s.b], [DB(("KVs", kb, "v"))])

    sc.barrier()
    es.close()

    xts = [P("xt0", [128, 4, D]), P("xt1", [128, 4, D])]; xn = P("xn", [128, 4, D], BF); sqj = xn
    hT = P("hT", [128, 8, 512], BF)
    QT = P("QT", [128, NH, 1, 512], BF); QTb = [Buf("QT%d" % h) for h in range(NH)]
    KTs = P("KTs", [128, NH, 512], BF); Vs = P("Vs", [128, NH, 4, 128], BF)
    big = P("big", [128, 3 * 8 * 512], BF)

    class View:
        def __init__(self, ap, name):
            self.ap = ap
            self.b = Buf(name)

        def __getitem__(self, k):
            return self.ap[k]

    zT = View(big[:, 0:4096].rearrange("p (c t) -> p c t", t=512), "zT")
    OT = View(big[:, 4096:8192].rearrange("p (c t) -> p c t", t=512), "OT"); OTb = [Buf("OT%d" % h) for h in range(NH)]
    mT = View(big[:, 8192:12288].rearrange("p (c t) -> p c t", t=512), "mT")
    actT = View(big[:, 0:NF * 512].rearrange("p (c t) -> p c t", t=512), "actT")
    ALIAS = [zT.b, mT.b] + OTb
    wring = Ring([P("wr%d" % i, [128, 8, 512], BF) for i in range(4)])
    kvring = Ring([P("kv%d" % i, [128, 1024], BF) for i in range(4)])
    ptring = Ring([P("pt%d" % i, [128, 2, 512], BF) for i in range(5)])
    accs = [P("acc%d" % i, [128, 512]) for i in range(2)]
    o1s = [P("o1s%d" % i, [128, 512]) for i in range(2)]
    o2s = [P("o2s%d" % i, [128, 512]) for i in range(2)]
    l2s = [P("l2s%d" % i, [128, 512]) for i in range(2)]
    accD = [Buf("accD%d" % i) for i in range(2)]
    accPb = [Buf("accP%d" % i) for i in range(2)]
    CS = 352
    tf = Ring([P("tf%d" % i, [128, 512]) for i in range(6)])
    tb = Ring([P("tb%d" % i, [128, 512], BF) for i in range(3)])
    utr = Ring([P("ut%d" % i, [128, 514]) for i in range(2)])
    print("sbuf bytes remaining:", nc.sbuf_bytes_remaining)
    mset("pool", QT[:], 0.0, [QT.b] + QTb)

    def frontA(xt, tiles):
        for i, (r0, R) in enumerate(tiles):
            act(sqj[0:R, i, :], xt[0:R, i, :], AF.Square, [xt.b], [xn.b, ss.b], accum_out=ss[0:R, i:i + 1])
            ts("dve", rs[0:R, i:i + 1], ss[0:R, i:i + 1], 1.0 / D, EPS, ALU.mult, ALU.add, [ss.b], [rs.b])
        rsqrt_inplace(rs[:, 0:len(tiles)] if tiles[0][1] == 128 else rs[0:tiles[0][1], 0:1], rs.b)
        for i, (r0, R) in enumerate(tiles):
            ts("dve", xn[0:R, i, :], xt[0:R, i, :], rs[0:R, i:i + 1], None, ALU.mult, None, [xt.b, rs.b], [xn.b])

    def frontB(path, TB, tiles, gp, sh_lo):
        usedbanks = {}
        for i, (r0, R) in enumerate(tiles):
            for kc in range(KC):
                off = kc * TB + i * 128
                bi = off // 1024
                if bi not in usedbanks:
                    usedbanks[bi] = bring.next()
                bk = usedbanks[bi]
                o = off % 1024
                tr(bk[:].bitcast(BF)[:, o:o + R], xn[0:R, i, kc * 128:(kc + 1) * 128], ident[0:R, 0:R],
                   [xn.b, ident.b], [bk.b])
        for kc in range(KC):
            off = kc * TB
            bk = usedbanks[off // 1024]
            o = off % 1024
            ts("dve", hT[:, kc, 0:TB], bk[:].bitcast(BF)[:, o:o + TB], gp[:, path, kc:kc + 1],
               modT[:, sh_lo + kc, path:path + 1], ALU.mult, ALU.add, [bk.b, gp.b, modT.b], [hT.b])

    def blk_params(path, b):
        prompt = path == 0
        TB = 512 if prompt else TS
        tiles = [(i * 128, 128) for i in range(4)] if prompt else [(0, TS)]
        xsrc = x_d if prompt else xs_d
        row0 = b * 512 if prompt else 0
        slot = (b % 2) if prompt else (NB % 2)
        return prompt, TB, tiles, xsrc, row0, xts[slot]

    def load_x(path, b):
        prompt, TB, tiles, xsrc, row0, xt = blk_params(path, b)
        for i, (r0, R) in enumerate(tiles):
            dma("sp", xt[0:R, i, :], xsrc[row0 + r0:row0 + r0 + R, :], [], [xt.b])

    def load_w(stream, sname, g, shape_kc, ncols, kc0=0):
        w = wring.next()
        dma("sp", w[:, 0:shape_kc, 0:ncols], stream[g, :, kc0:kc0 + shape_kc, :], [], [w.b])
        return w

    def headnorm_fm(bk, TB, outs):
        sq = tb.next()
        act(sq[:, 0:TB], bk[:, 0:TB], AF.Square, [bk.b], [sq.b])

        def part2():
            nb = bring.next()
            mm(nb[:, 0:TB], bones[:], sq[:, 0:TB], True, True, [sq.b, bones.b], [nb.b])
            sd = tf.next()
            act(sd[:, 0:TB], nb[:, 0:TB], AF.Ln, [nb.b, epsT.b], [sd.b], bias=epsT[:, 0:1], scale=1.0)
            act(sd[:, 0:TB], sd[:, 0:TB], AF.Exp, [sd.b], [sd.b], scale=-0.5)
            for (oap, p0, p1, g, wb) in outs:
                stt(oap, bk[p0:p1, 0:TB], g[p0:p1, 0:1], sd[p0:p1, 0:TB], ALU.mult, ALU.mult, [bk.b, sd.b, g.b], wb)
        return part2

    def do_block(path, b, nextblk):
        prompt, TB, tiles, xsrc, row0, xt = blk_params(path, b)
        KV = KVp if prompt else KVs
        kvname = "KVp" if prompt else "KVs"
        kb_new = b if prompt else 2
        yk, yv, yy = (ko_d, vo_d, y_d) if prompt else (kso_d, vso_d, ys_d)
        if prompt and b == 0:
            load_x(path, b)
            frontA(xt, tiles)
            frontB(path, TB, tiles, g1p, 0)
        pend = None
        for hh in range(2 * NH):
            isq = hh < NH
            h = hh % NH
            if h % 4 == 0:
                w = load_w(Wq if isq else Wk, "W", h // 4, 8, 512)
            bk = bring.next()
            for kc in range(KC):
                mm(bk[:, 0:TB], w[:, kc, (h % 4) * 128:(h % 4 + 1) * 128], hT[:, kc, 0:TB], kc == 0, kc == KC - 1,
                   [w.b, hT.b], [bk.b])
            if isq:
                nxt = headnorm_fm(bk, TB, [(QT[:, h, 0, 0:TB], 0, 128, qg2, [QTb[h]])])
            else:
                nxt = headnorm_fm(bk, TB, [(KTs[:, h, 0:TB], 0, 128, kg2, [KTs.b])])
            if pend is not None:
                pend()
            pend = nxt
        pend()
        dma("pool", KV[:, kb_new, :, 0:TB].rearrange("h p c -> p h c"), KTs[:, :, 0:TB], [KTs.b],
            [DB((kvname, kb_new, "k"))])
        if nextblk is not None:
            load_x(*nextblk)
        for cg in range(2):
            w = load_w(Wk, "Wk", cg, 8, 512)
            for i, (r0, R) in enumerate(tiles):
                bk = bring.next()
                for kc in range(KC):
                    mm(bk[0:R, :], hT[:, kc, r0:r0 + R], w[:, kc, :], kc == 0, kc == KC - 1, [w.b, hT.b], [bk.b])
                sq = tf.next()
                act(sq[0:R, :], bk[0:R, :], AF.Square, [bk.b], [sq.b])
                sc.add("dve", lambda e, sq=sq, R=R: e.tensor_reduce(
                    out=ssk[0:R, 0:8], in_=sq[0:R, :].rearrange("p (g d) -> p g d", d=64), axis=AX.X, op=ALU.add),
                    [sq.b], [ssk.b])
                ts("dve", ssk[0:R, :], ssk[0:R, :], 1.0 / 64, EPS, ALU.mult, ALU.add, [ssk.b], [ssk.b])
                rsqrt_inplace(ssk[0:R, :], ssk.b)
                ko = tf.next()
                tt("dve", ko[0:R, :].rearrange("p (g d) -> p g d", d=64), bk[0:R, :].rearrange("p (g d) -> p g d", d=64),
                   ssk[0:R, 0:8].unsqueeze(2).to_broadcast([R, 8, 64]), ALU.mult, [bk.b, ssk.b], [ko.b])
                tt("dve", ko[0:R, :], ko[0:R, :], kgb[0:R, cg * 512:(cg + 1) * 512], ALU.mult, [ko.b, kgb.b], [ko.b])
                dma("pool", yk[row0 + r0:row0 + r0 + R, cg * 512:(cg + 1) * 512], ko[0:R, :], [ko.b], [])
        for cg in range(2):
            w = load_w(Wv, "Wv", cg, 8, 512)
            for i, (r0, R) in enumerate(tiles):
                bk = bring.next()
                for kc in range(KC):
                    mm(bk[0:R, :], hT[:, kc, r0:r0 + R], w[:, kc, :], kc == 0, kc == KC - 1, [w.b, hT.b], [bk.b])
                vo = tf.next()
                act(vo[0:R, :], bk[0:R, :], AF.Copy, [bk.b], [vo.b])
                act(Vs[0:R, 4 * cg:4 * cg + 4, i, :], bk[0:R, :].rearrange("p (h d) -> p h d", d=128), AF.Copy, [bk.b], [Vs.b])
                dma("pool", yv[row0 + r0:row0 + r0 + R, cg * 512:(cg + 1) * 512], vo[0:R, :], [vo.b], [])
        if prompt:
            dma("pool", KV[:, kb_new, :, 512:1024].rearrange("h p (t d) -> p h t d", d=128), Vs[:], [Vs.b],
                [DB((kvname, kb_new, "v"))])
        else:
            dma("pool", KV[:, kb_new, 0:TS, 512:640].rearrange("h p d -> p h d"), Vs[0:TS, :, 0, :], [Vs.b],
                [DB((kvname, kb_new, "v"))])
        ut_ = utail[path]
        for c in range(8):
            w = load_w(Wcv, "Wcv", c, 8, 384)
            bx = bring.next(); bgb = bring.next(); bgc = bring.next()
            for (bk, q) in ((bx, 0), (bgb, 1), (bgc, 2)):
                for kc in range(KC):
                    mm(bk[:, 0:TB], w[:, kc, q * 128:(q + 1) * 128], hT[:, kc, 0:TB], kc == 0, kc == KC - 1, [w.b, hT.b], [bk.b])
            xsf = tf.next()
            act(xsf[:, 0:TB], bx[:, 0:TB], AF.Copy, [bx.b], [xsf.b])
            u = utr.next()
            cp("pool", u[:, 0:2], ut_[:, c, :], [ut_.b], [u.b])
            tt("dve", u[:, 2:2 + TB], bgc[:, 0:TB], xsf[:, 0:TB], ALU.mult, [bgc.b, xsf.b, u.b], [u.b])
            cp("pool", ut_[:, c, :], u[:, TB:TB + 2], [u.b], [ut_.b])
            cvt = tf.next()
            ts("dve", cvt[:, 0:TB], u[:, 2:2 + TB], cw[:, c, 2:3], None, ALU.mult, None, [u.b, cw.b], [cvt.b])
            stt(cvt[:, 0:TB], u[:, 1:1 + TB], cw[:, c, 1:2], cvt[:, 0:TB], ALU.mult, ALU.add, [u.b, cw.b, cvt.b], [cvt.b])
            stt(cvt[:, 0:TB], u[:, 0:TB], cw[:, c, 0:1], cvt[:, 0:TB], ALU.mult, ALU.add, [u.b, cw.b, cvt.b], [cvt.b])
            tt("dve", zT[:, c, 0:TB], bgb[:, 0:TB], cvt[:, 0:TB], ALU.mult, [bgb.b, cvt.b], [zT.b, actT.b])
        nkb = (b + 1) if prompt else 3
        Sp = [(banks[0], banks[1]), (banks[2], banks[3])]
        O1, O2, LB, B7 = banks[4], banks[5], banks[6], banks[7]
        jobs = []
        for h in range(NH):
            hj = []
            for kb in range(nkb):
                diag = prompt and kb == b
                ntile = 1 if (not prompt and kb == 2) else 4
                for j in range(ntile):
                    KR = TS if (not prompt and kb == 2) else 128
                    hj.append((kb, j, KR, 128 * j if diag else 0, diag))
            for n, (kb, j, KR, q0, diag) in enumerate(hj):
                jobs.append((h, n, len(hj), kb, j, KR, q0, diag))
        G = len(jobs)
        njh = G // NH
        dA = max(1, min(6, njh - 2))
        dB = max(1, min(5, njh - 1))
        kvrec = {}
        ptof = {}
        st = {"spair": 0}
        deferred = []

        def next_pair():
            p = st["spair"] % 2
            st["spair"] += 1
            return p

        def emit_S(g):
            (h, n, nj, kb, j, KR, q0, diag) = jobs[g]
            if (h, kb) not in kvrec:
                kv = kvring.next()
                dma("sp", kv[:], KV[h, kb, :, :], [DB((kvname, kb, "k")), DB((kvname, kb, "v"))], [kv.b])
                kvrec[(h, kb)] = kv
            kv = kvrec[(h, kb)]
            NQ = TB - q0
            p = next_pair()
            sa, sb_ = Sp[p]
            mm(sa[0:KR, 0:NQ], kv[0:64, j * 128:j * 128 + KR], QT[0:64, h, 0, q0:TB], True, True, [kv.b, QTb[h]], [sa.b])
            mm(sb_[0:KR, 0:NQ], kv[64:128, j * 128:j * 128 + KR], QT[64:128, h, 0, q0:TB], True, True, [kv.b, QTb[h]], [sb_.b])
            pt = ptring.next()
            ptof[g] = pt
            act(pt[0:KR, :, q0:TB], pall[0:KR, p * 1024:(p + 1) * 1024].rearrange("p (m c) -> p m c", m=2)[:, :, 0:NQ],
                AF.Exp, [sa.b, sb_.b], [pt.b], scale=0.125)
            if diag:
                act(pt[64:128, :, q0:q0 + 64], pt[64:128, :, q0:q0 + 64], AF.Copy, [pt.b], [pt.b], scale=0.0)
            acc = accs[h % 2]
            ab = accD[h % 2]
            if n == 0:
                sc.add("dve", lambda e, a=acc[0:KR, q0:TB], p_=pt[0:KR, 0, q0:TB]: e.tensor_copy(out=a, in_=p_), [pt.b], [ab])
            else:
                sc.add("dve", lambda e, a=acc[0:KR, q0:TB], p_=pt[0:KR, 0, q0:TB]: e.tensor_tensor(out=a, in0=a, in1=p_, op=ALU.add),
                       [pt.b, ab], [ab])

        def epiB(h, ta, rb, sq):
            nb = B7
            mm(nb[:, 0:TB], ones128[:], sq[:, 0:TB], True, True, [sq.b, ones128.b], [nb.b])
            act(rb[:, 0:TB], nb[:, 0:TB], AF.Ln, [nb.b, epsT.b], [rb.b], bias=epsT[:, 0:1], scale=1.0)
            act(rb[:, 0:TB], rb[:, 0:TB], AF.Exp, [rb.b], [rb.b], scale=-0.5)
            stt(OT[:, h, 0:TB], ta[:, 0:TB], subg2[:, 0:1], rb[:, 0:TB], ALU.mult, ALU.mult, [ta.b, rb.b, subg2.b], [OTb[h], actT.b])

        def epiA(h):
            par = h % 2
            acc = accs[par]
            mm(B7[:, 0:TB], onesf[:], acc[:, 0:TB], True, True, [accD[par], onesf.b], [B7.b])
            ra = tf.next(); rb = tf.next(); ta = tf.next(); tb_ = tf.next()
            recip(ra[:, 0:TB], B7[:, 0:TB], [B7.b], [ra.b])
            act(rb[:, 0:TB], l2s[par][:, 0:TB], AF.Exp, [l2s[par].b], [rb.b], scale=-1.0)
            tt("dve", ta[:, 0:TB], o1s[par][:, 0:TB], ra[:, 0:TB], ALU.mult, [o1s[par].b, ra.b], [ta.b])
            tt("dve", tb_[:, 0:TB], o2s[par][:, 0:TB], rb[:, 0:TB], ALU.mult, [o2s[par].b, rb.b], [tb_.b])
            stt(ta[:, 0:TB], tb_[:, 0:TB], neglam[:, 0:1], ta[:, 0:TB], ALU.mult, ALU.add, [tb_.b, ta.b, neglam.b], [ta.b])
            sq = tb.next()
            tt("pool", sq[:, 0:TB], ta[:, 0:TB], ta[:, 0:TB], ALU.mult, [ta.b], [sq.b])
            deferred.append([dB, lambda: epiB(h, ta, rb, sq)])

        def emit_PV(g):
            (h, n, nj, kb, j, KR, q0, diag) = jobs[g]
            kv = kvrec[(h, kb)]
            pt = ptof.pop(g)
            first = n == 0
            last = n == nj - 1
            vt = kv[0:KR, 512 + j * 128:512 + (j + 1) * 128]
            mm(O1[:, q0:TB], vt, pt[0:KR, 0, q0:TB], first, last, [kv.b, pt.b], [O1.b])
            mm(O2[:, q0:TB], vt, pt[0:KR, 1, q0:TB], first, last, [kv.b, pt.b], [O2.b])
            mm(LB[:, q0:TB], ones[0:KR, :], pt[0:KR, 1, q0:TB], first, last, [ones.b, pt.b], [LB.b])
            if last:
                par = h % 2
                act(o1s[par][:, 0:TB], O1[:, 0:TB], AF.Copy, [O1.b], [o1s[par].b])
                act(o2s[par][:, 0:TB], O2[:, 0:TB], AF.Copy, [O2.b], [o2s[par].b])
                act(l2s[par][:, 0:TB], LB[:, 0:TB], AF.Ln, [LB.b], [l2s[par].b])
                deferred.append([dA, lambda: epiA(h)])

        def tick():
            for d in deferred:
                d[0] -= 1
            while deferred and deferred[0][0] <= 0:
                deferred.pop(0)[1]()

        emit_S(0)
        if G > 1:
            emit_S(1)
        for g in range(G):
            if g + 2 < G:
                emit_S(g + 2)
            emit_PV(g)
            tick()
        while deferred:
            deferred.pop(0)[1]()
        for j in range(8):
            w = load_w(Wmg, "Wmg", j, 8, 512)
            bga = bring.next(); bgb = bring.next(); bya = bring.next(); byb = bring.next()
            for kc in range(KC):
                mm(bga[:, 0:TB], w[:, kc, 0:128], hT[:, kc, 0:TB], kc == 0, kc == KC - 1, [w.b, hT.b], [bga.b])
            for kc in range(KC):
                mm(bgb[:, 0:TB], w[:, kc, 128:256], hT[:, kc, 0:TB], kc == 0, kc == KC - 1, [w.b, hT.b], [bgb.b])
            for kc in range(KC):
                mm(bya[:, 0:TB], w[:, kc, 256:384], OT[:, kc, 0:TB], kc == 0, kc == KC - 1, [w.b, OTb[kc]], [bya.b])
            for kc in range(KC):
                mm(byb[:, 0:TB], w[:, kc, 384:512], zT[:, kc, 0:TB], kc == 0, kc == KC - 1, [w.b, zT.b], [byb.b])
            sga = tf.next(); sgb = tf.next()
            act(sga[:, 0:TB], bga[:, 0:TB], AF.Sigmoid, [bga.b], [sga.b])
            act(sgb[:, 0:TB], bgb[:, 0:TB], AF.Sigmoid, [bgb.b], [sgb.b])
            tt("dve", sga[:, 0:TB], bya[:, 0:TB], sga[:, 0:TB], ALU.mult, [bya.b, sga.b], [sga.b])
            tt("dve", sgb[:, 0:TB], byb[:, 0:TB], sgb[:, 0:TB], ALU.mult, [byb.b, sgb.b], [sgb.b])
            tt("pool", mT[:, j, 0:TB], sga[:, 0:TB], sgb[:, 0:TB], ALU.add, [sga.b, sgb.b], [mT.b, actT.b])
        for cg in range(2):
            w = load_w(Wo, "Wo", cg, 8, 512)
            for i, (r0, R) in enumerate(tiles):
                bk = bring.next()
                for kc in range(KC):
                    mm(bk[0:R, :], mT[:, kc, r0:r0 + R], w[:, kc, :], kc == 0, kc == KC - 1, [w.b, mT.b], [bk.b])
                t_ = tf.next()
                tt("dve", t_[0:R, :], bk[0:R, :], gateb[path][0:R, cg * 512:(cg + 1) * 512], ALU.mult,
                   [bk.b, gateb[path].b], [t_.b])
                tt("pool", xt[0:R, i, cg * 512:(cg + 1) * 512], xt[0:R, i, cg * 512:(cg + 1) * 512], t_[0:R, :], ALU.add,
                   [xt.b, t_.b], [xt.b])
        frontA(xt, tiles)
        frontB(path, TB, tiles, g2p, 24)
        if nextblk is not None:
            nprompt, nTB, ntiles, _, _, nxt_xt = blk_params(*nextblk)
            frontA(nxt_xt, ntiles)
        for f in range(NF):
            w = load_w(Wgu, "Wgu", f, 8, 256)
            bg = bring.next(); bu = bring.next()
            for kc in range(KC):
                mm(bg[:, 0:TB], w[:, kc, 0:128], hT[:, kc, 0:TB], kc == 0, kc == KC - 1, [w.b, hT.b], [bg.b])
            for kc in range(KC):
                mm(bu[:, 0:TB], w[:, kc, 128:256], hT[:, kc, 0:TB], kc == 0, kc == KC - 1, [w.b, hT.b], [bu.b])
            sg = tf.next()
            act(sg[:, 0:TB], bg[:, 0:TB], AF.Silu, [bg.b], [sg.b])
            tt("dve", actT[:, f, 0:TB], bu[:, 0:TB], sg[:, 0:TB], ALU.mult, [bu.b, sg.b], [actT.b] + ALIAS)
        if nextblk is not None:
            frontB(nextblk[0], nTB, ntiles, g1p, 0)
        for cg in range(2):
            ybk = [bring.next() for _ in tiles]
            for (k0, k1) in ((0, 8), (8, 16), (16, NF)):
                w = wring.next()
                dma("sp", w[:, 0:k1 - k0, :], Wd[cg, :, k0:k1, :], [], [w.b])
                for i, (r0, R) in enumerate(tiles):
                    for kc in range(k0, k1):
                        mm(ybk[i][0:R, :], actT[:, kc, r0:r0 + R], w[:, kc - k0, :], kc == 0, kc == NF - 1,
                           [w.b, actT.b], [ybk[i].b])
            for i, (r0, R) in enumerate(tiles):
                t_ = tf.next()
                tt("dve", t_[0:R, :], ybk[i][0:R, :], gateb[path][0:R, 1024 + cg * 512:1024 + (cg + 1) * 512], ALU.mult,
                   [ybk[i].b, gateb[path].b], [t_.b])
                tt("pool", xt[0:R, i, cg * 512:(cg + 1) * 512], xt[0:R, i, cg * 512:(cg + 1) * 512], t_[0:R, :], ALU.add,
                   [xt.b, t_.b], [xt.b])
        for i, (r0, R) in enumerate(tiles):
            dma("pool", yy[row0 + r0:row0 + r0 + R, :], xt[0:R, i, :], [xt.b], [])

    for b in range(NB):
        do_block(0, b, (0, b + 1) if b + 1 < NB else (1, 0))
    do_block(1, 0, None)
    for c in range(8):
        dma("pool", co_d.rearrange("t (c p) -> p c t", p=128)[:, c, :], utail[0][:, c, :], [utail[0].b], [],
            allow_slow_non_contiguous=True)
        dma("pool", cso_d.rearrange("t (c p) -> p c t", p=128)[:, c, :], utail[1][:, c, :], [utail[1].b], [],
            allow_slow_non_contiguous=True)
    sc.emit(nc)
    return nc


def _consts():
    c = np.zeros((128, 256), np.float32)
    c[:, 0:128] = np.eye(128, dtype=np.float32)
    for p in range(128):
        c[p, 128 + (p // 64) * 64:128 + (p // 64 + 1) * 64] = 1.0 / 64
    return c


def make_in_maps(inp, S, ncores):
    f = lambda a: np.ascontiguousarray(np.asarray(a, dtype=np.float32))
    fm8 = lambda v: f(np.asarray(v).reshape(8, 128).T)
    shared = {
        "w_ada": f(inp["w_ada"][0]), "b_ada": f(inp["b_ada"][0]).reshape(1, -1),
        "b_adaT": f(np.asarray(inp["b_ada"][0]).reshape(48, 128).T),
        "n1g": fm8(inp["norm1_g"][0]), "n2g": fm8(inp["norm2_g"][0]),
        "w_in": f(inp["w_in"][0]),
        "qg2": f(np.tile(np.asarray(inp["q_norm_g"][0]), 2).reshape(128, 1)),
        "kg2": f(np.tile(np.asarray(inp["k_norm_g"][0]), 2).reshape(128, 1)),
        "kgrow": f(np.tile(np.asarray(inp["k_norm_g"][0]), 16).reshape(1, D)),
        "lam4": f(np.concatenate([np.asarray(inp[k][0]) for k in ("lambda_q1", "lambda_k1", "lambda_q2", "lambda_k2")]).reshape(1, 256)),
        "subg": f(np.asarray(inp["sub_norm_g"][0]).reshape(128, 1)),
        "w_attn_out": f(inp["w_attn_out"][0]),
        "cwT": f(np.asarray(inp["conv_w"][0]).reshape(3, 8, 128).transpose(2, 1, 0)),
        "w_conv_out": f(inp["w_conv_out"][0]), "w_out": f(inp["w_out"][0]),
        "w_gate_up": f(inp["w_gate_up"][0]), "w_down": f(inp["w_down"][0]),
        "consts": _consts(),
    }
    maps = []
    for b in range(ncores):
        m = dict(shared)
        m["x"] = f(inp["x_prompt"][b]); m["xs"] = f(inp["x_sample"][b])
        m["ck"] = f(np.asarray(inp["cache_k"][0, b]).reshape(PAST, D))
        m["cv"] = f(np.asarray(inp["cache_v"][0, b]).reshape(PAST, D))
        m["sconvT"] = f(np.asarray(inp["state_conv"][0, b]).reshape(2, 8, 128).transpose(2, 1, 0))
        c2 = np.stack([np.asarray(inp["c_prompt"][b]), np.asarray(inp["c_sample"][b])])
        m["cT"] = f(c2.reshape(2, 8, 128).transpose(2, 1, 0))
        maps.append(m)
    return maps


_cache = {}


def run(inp, S, ncores):
    if S not in _cache:
        _cache[S] = build(S)
    nc = _cache[S]
    maps = make_in_maps(inp, S, ncores)
    res = run_bass_kernel_spmd(nc, maps, core_ids=list(range(ncores)))
    R = res.results
    st = lambda k: np.stack([np.asarray(R[b][k], dtype=np.float32) for b in range(ncores)])
    y = st("y"); ys = st("ys")
    kp = st("ko").reshape(1, ncores, S, NH, 2, 64); vp = st("vo").reshape(1, ncores, S, NH, 128)
    cp_ = st("co").reshape(1, ncores, 2, D)
    ks = st("kso").reshape(1, ncores, TS, NH, 2, 64); vs = st("vso").reshape(1, ncores, TS, NH, 128)
    cs = st("cso").reshape(1, ncores, 2, D)
    return (y, ys, kp, vp, cp_, ks, vs, cs)


def kernel(**inputs):
    S = int(np.asarray(inputs["x_prompt"]).shape[1])
    return run(inputs, S, 8)
```

```python
import numpy as np
import concourse.bass as bass
import concourse.mybir as mybir
from concourse.bass_utils import run_bass_kernel_spmd

F32 = mybir.dt.float32
BF = mybir.dt.bfloat16
AF = mybir.ActivationFunctionType
ALU = mybir.AluOpType
AX = mybir.AxisListType

D = 1024
KC = 8
NH = 8
DFF = 2816
NF = 22
EPS = 1e-6
LAM_INIT = 0.2
PAST = 1024
TS = 32
SAME_ENG_WINDOW = 8
NCHAN = 16


class Buf:
    __slots__ = ("name", "w", "r")

    def __init__(self, name=""):
        self.name = name
        self.w = None
        self.r = []


class Op:
    __slots__ = ("eng", "fn", "deps", "is_dma", "needs_inc", "sem", "val", "pos", "noself")

    def __init__(self, eng, fn, is_dma):
        self.eng = eng
        self.fn = fn
        self.deps = []
        self.is_dma = is_dma
        self.needs_inc = False
        self.sem = None
        self.val = 0
        self.pos = 0
        self.noself = False


class Sched:
    ENGS = ("pe", "act", "dve", "pool", "sp")

    def __init__(self):
        self.ops = []
        self.per_eng = {e: [] for e in self.ENGS}
        self.dma_hist = {e: [] for e in self.ENGS}
        self.barrier_deps = {e: [] for e in self.ENGS}

    def add(self, eng, fn, R=(), W=(), dma=False, noself=False):
        op = Op(eng, fn, dma)
        op.noself = noself
        deps = set()
        for b in R:
            if b.w is not None:
                deps.add(b.w)
        for b in W:
            if b.w is not None:
                deps.add(b.w)
            for r in b.r:
                deps.add(r)
        for d in self.barrier_deps[eng]:
            deps.add(d)
        self.barrier_deps[eng] = []
        if dma:
            h = self.dma_hist[eng]
            if len(h) >= NCHAN:
                deps.add(h[-NCHAN])
            h.append(op)
        deps.discard(op)
        op.deps = list(deps)
        for b in R:
            b.r.append(op)
        for b in W:
            b.w = op
            b.r = []
        op.pos = len(self.per_eng[eng])
        self.per_eng[eng].append(op)
        self.ops.append(op)
        return op

    def barrier(self):
        last = []
        for e in self.ENGS:
            ops = self.per_eng[e]
            nd = [o for o in ops[-200:] if not o.is_dma]
            if nd:
                last.append(nd[-1])
            last.extend(self.dma_hist[e][-NCHAN:])
        for e in self.ENGS:
            self.barrier_deps[e] = list(last)

    def emit(self, nc):
        for op in self.ops:
            real = []
            for d in op.deps:
                if not d.is_dma and d.eng == op.eng:
                    if op.eng == "pe" and not op.is_dma:
                        continue
                    if op.pos - d.pos >= SAME_ENG_WINDOW or op.noself:
                        continue
                real.append(d)
                d.needs_inc = True
            op.deps = real
        from contextlib import ExitStack
        with ExitStack() as es:
            csem = {e: es.enter_context(nc.semaphore("c_" + e)) for e in ("pe", "act", "dve", "pool")}
            dsem = {}
            for e in ("sp", "pool", "act"):
                dsem[e] = [es.enter_context(nc.semaphore("d_%s%d" % (e, i))) for i in range(NCHAN)]
            cnt = {e: 0 for e in csem}
            dcnt = {e: [0] * NCHAN for e in dsem}
            dn = {e: 0 for e in dsem}
            for op in self.ops:
                if op.is_dma:
                    i = dn[op.eng] % NCHAN
                    dn[op.eng] += 1
                    dcnt[op.eng][i] += 16
                    op.sem = dsem[op.eng][i]
                    op.val = dcnt[op.eng][i]
                elif op.needs_inc:
                    cnt[op.eng] += 1
                    op.sem = csem[op.eng]
                    op.val = cnt[op.eng]
            block = es.enter_context(nc.Block())

            def run_stream(ename, e):
                waited = {}
                for op in self.per_eng[ename]:
                    need = {}
                    for d in op.deps:
                        k = id(d.sem)
                        if waited.get(k, 0) >= d.val:
                            continue
                        if k not in need or need[k][1] < d.val:
                            need[k] = (d.sem, d.val)
                    for k, (s, v) in need.items():
                        e.wait_ge(s, v)
                        waited[k] = v
                    ins = op.fn(e)
                    if op.is_dma:
                        ins.then_inc(op.sem, 16)
                    elif op.needs_inc:
                        ins.then_inc(op.sem, 1)
                if ename in dsem:
                    for i in range(NCHAN):
                        if dcnt[ename][i] > 0:
                            e.wait_ge(dsem[ename][i], dcnt[ename][i])

            @block.sync
            def _(e):
                run_stream("sp", e)

            @block.tensor
            def _(e):
                run_stream("pe", e)

            @block.vector
            def _(e):
                run_stream("dve", e)

            @block.scalar
            def _(e):
                run_stream("act", e)

            @block.gpsimd
            def _(e):
                run_stream("pool", e)


class T:
    def __init__(self, nc, name, shape, dtype, psum=False, es=None):
        name = "t_" + name
        if psum:
            self.t = nc.alloc_psum_tensor(name, list(shape), dtype)
        elif es is not None:
            self.t = es.enter_context(nc.sbuf_tensor(name, list(shape), dtype))
        else:
            self.t = nc.alloc_sbuf_tensor(name, list(shape), dtype)
        self.b = Buf(name)

    def __getitem__(self, k):
        return self.t[k]


class Ring:
    def __init__(self, items):
        self.items = items
        self.i = 0

    def next(self):
        it = self.items[self.i % len(self.items)]
        self.i += 1
        return it


def build(S):
    assert S % 512 == 0
    NB = S // 512
    nc = bass.Bass("TRN2", target_bir_lowering=False)
    sc = Sched()

    def din(name, shape):
        return nc.dram_tensor(name, list(shape), F32, kind="ExternalInput").ap()

    def dout(name, shape):
        return nc.dram_tensor(name, list(shape), F32, kind="ExternalOutput").ap()

    def dscr(name, shape):
        return nc.dram_tensor(name, list(shape), BF, kind="Internal").ap()

    x_d = din("x", [S, D]); xs_d = din("xs", [TS, D])
    ck_d = din("ck", [PAST, D]); cv_d = din("cv", [PAST, D])
    sconvT_d = din("sconvT", [128, 8, 2]); cT_d = din("cT", [128, 8, 2])
    wada_d = din("w_ada", [D, 6 * D]); bada_d = din("b_ada", [1, 6 * D]); badaT_d = din("b_adaT", [128, 48])
    n1g_d = din("n1g", [128, 8]); n2g_d = din("n2g", [128, 8])
    win_d = din("w_in", [D, 8 * D])
    qg2_d = din("qg2", [128, 1]); kg2_d = din("kg2", [128, 1]); kgrow_d = din("kgrow", [1, D])
    lam4_d = din("lam4", [1, 256]); subg_d = din("subg", [128, 1])
    wao_d = din("w_attn_out", [D, D]); cwT_d = din("cwT", [128, 8, 3])
    wco_d = din("w_conv_out", [D, D]); wo_d = din("w_out", [D, D])
    wgu_d = din("w_gate_up", [D, 2 * DFF]); wd_d = din("w_down", [DFF, D])
    consts_d = din("consts", [128, 256])

    y_d = dout("y", [S, D]); ys_d = dout("ys", [TS, D])
    ko_d = dout("ko", [S, D]); vo_d = dout("vo", [S, D]); co_d = dout("co", [2, D])
    kso_d = dout("kso", [TS, D]); vso_d = dout("vso", [TS, D]); cso_d = dout("cso", [2, D])

    Wq = dscr("Wq", [2, 128, 8, 512]); Wk = dscr("Wk", [2, 128, 8, 512]); Wv = dscr("Wv", [2, 128, 8, 512])
    Wcv = dscr("Wcv", [8, 128, 8, 384]); Wmg = dscr("Wmg", [8, 128, 8, 512]); Wo = dscr("Wo", [2, 128, 8, 512])
    Wgu = dscr("Wgu", [NF, 128, 8, 256]); Wd = dscr("Wd", [2, 128, NF, 512])
    KVp = dscr("KVp", [NH, NB, 128, 1024]); KVs = dscr("KVs", [NH, 3, 128, 1024])
    dbuf = {}

    def DB(key):
        if key not in dbuf:
            dbuf[key] = Buf(str(key))
        return dbuf[key]

    def P(name, shape, dt=F32):
        return T(nc, name, shape, dt)

    cst = P("cst", [128, 256]); ident = P("ident", [128, 128], BF); bones = P("bones", [128, 128], BF)
    ones = P("ones", [128, 128], BF); ones128 = P("ones128", [128, 128], BF)
    identf = cst
    epsT = P("epsT", [128, 1])
    modT = P("modT", [128, 48, 2]); g1p = P("g1p", [128, 2, 8]); g2p = P("g2p", [128, 2, 8])
    n1g = P("n1g", [128, 8]); n2g = P("n2g", [128, 8])
    qg2 = P("qg2", [128, 1]); kg2 = P("kg2", [128, 1]); subg = P("subg", [128, 1]); subg2 = P("subg2", [128, 1])
    neglam = P("neglam", [128, 1]); lamb = P("lamb", [128, 256]); lprod = P("lprod", [128, 128]); lsum = P("lsum", [128, 2])
    lexp = P("lexp", [128, 2]); lam1 = P("lam1", [128, 1])
    cw = P("cw", [128, 8, 3]); utail = [P("utail0", [128, 8, 2]), P("utail1", [128, 8, 2])]
    gateb = [P("gateb0", [128, 2048]), P("gateb1", [128, 2048])]
    kgb = P("kgb", [128, D])
    badaT = P("badaT", [128, 48]); cT = P("cT", [128, 8, 2]); scT = P("scT", [128, 8, 2])
    ss = P("ss", [128, 4]); rs = P("rs", [128, 4]); ssk = P("ssk", [128, 8])

    pall = nc.alloc_psum_tensor("t_pall", [128, 4096], F32)

    class BankV:
        def __init__(self, i):
            self.ap = pall[:, i * 512:(i + 1) * 512]
            self.b = Buf("bank%d" % i)

        def __getitem__(self, k):
            return self.ap[k]

    banks = [BankV(i) for i in range(8)]
    bring = Ring(banks)
    onesf = P("onesf", [128, 128])

    def dma(q, out, in_, R, W, **kw):
        return sc.add(q, lambda e: e.dma_start(out=out, in_=in_, **kw), R, W, dma=True)

    def mm(out, lhsT, rhs, start, stop, R, W):
        return sc.add("pe", lambda e: e.matmul(out, lhsT=lhsT, rhs=rhs, start=start, stop=stop), R, W)

    def tr(out, in_, idn, R, W):
        return sc.add("pe", lambda e: e.transpose(out, in_, idn), R, W)

    def act(out, in_, func, R, W, **kw):
        return sc.add("act", lambda e: e.activation(out=out, in_=in_, func=func, **kw), R, W)

    def ts(eng, out, in0, s1, s2, op0, op1, R, W):
        if op1 is None:
            return sc.add(eng, lambda e: e.tensor_scalar(out=out, in0=in0, scalar1=s1, scalar2=None, op0=op0), R, W)
        return sc.add(eng, lambda e: e.tensor_scalar(out=out, in0=in0, scalar1=s1, scalar2=s2, op0=op0, op1=op1), R, W)

    def tt(eng, out, in0, in1, op, R, W):
        return sc.add(eng, lambda e: e.tensor_tensor(out=out, in0=in0, in1=in1, op=op), R, W)

    def stt(out, in0, scalar, in1, op0, op1, R, W):
        return sc.add("dve", lambda e: e.scalar_tensor_tensor(out=out, in0=in0, scalar=scalar, in1=in1, op0=op0, op1=op1), R, W)

    def cp(eng, out, in_, R, W):
        return sc.add(eng, lambda e: e.tensor_copy(out=out, in_=in_), R, W)

    def recip(out, in_, R, W):
        act(out, in_, AF.Ln, R, W)
        return act(out, out, AF.Exp, W, W, scale=-1.0)

    def mset(eng, ap, val, W):
        return sc.add(eng, lambda e: e.memset(ap, val), (), W)

    def rsqrt_inplace(ap, b):
        act(ap, ap, AF.Ln, [b], [b])
        act(ap, ap, AF.Exp, [b], [b], scale=-0.5)

    from contextlib import ExitStack
    dma("sp", cst[:], consts_d[:, :], [], [cst.b])
    cp("dve", ident[:], cst[:, 0:128], [cst.b], [ident.b])
    cp("dve", bones[:], cst[:, 128:256], [cst.b], [bones.b])
    mset("dve", ones[:], 1.0, [ones.b])
    mset("dve", ones128[:], 1.0 / 128, [ones128.b])
    mset("dve", onesf[:], 1.0, [onesf.b])
    mset("dve", epsT[:], EPS, [epsT.b])
    mset("dve", utail[0][:], 0.0, [utail[0].b])
    for (t_, d_) in ((n1g, n1g_d), (n2g, n2g_d), (qg2, qg2_d), (kg2, kg2_d), (subg, subg_d), (badaT, badaT_d)):
        dma("sp", t_[:], d_[:, :], [], [t_.b])
    dma("sp", cw[:], cwT_d[:, :, :], [], [cw.b])
    dma("sp", cT[:], cT_d[:, :, :], [], [cT.b])
    dma("sp", utail[1][:], sconvT_d[:, :, :], [], [utail[1].b])
    dma("sp", lamb[:], lam4_d[0].partition_broadcast(128), [], [lamb.b])
    dma("sp", kgb[:], kgrow_d[0].partition_broadcast(128), [], [kgb.b])
    tt("dve", lprod[:, 0:64], lamb[:, 0:64], lamb[:, 64:128], ALU.mult, [lamb.b], [lprod.b])
    tt("dve", lprod[:, 64:128], lamb[:, 128:192], lamb[:, 192:256], ALU.mult, [lamb.b, lprod.b], [lprod.b])
    sc.add("dve", lambda e: e.tensor_reduce(out=lsum[:, 0:2], in_=lprod[:].rearrange("p (g d) -> p g d", d=64),
                                            axis=AX.X, op=ALU.add), [lprod.b], [lsum.b])
    act(lexp[:], lsum[:], AF.Exp, [lsum.b], [lexp.b])
    tt("dve", lam1[:], lexp[:, 1:2], lexp[:, 0:1], ALU.subtract, [lexp.b], [lam1.b])
    ts("dve", neglam[:], lam1[:], -LAM_INIT, None, ALU.add, None, [lam1.b], [neglam.b])
    ts("dve", subg2[:], subg[:], 1.0 - LAM_INIT, None, ALU.mult, None, [subg.b], [subg2.b])
    act(scT[:], cT[:], AF.Silu, [cT.b], [scT.b])

    es = ExitStack()

    def PT_(name, shape, dt=F32):
        return T(nc, name, shape, dt, es=es)

    wslots = Ring([PT_("wadas%d" % i, [128, 8, 512]) for i in range(2)])
    scb = [PT_("scb%d" % r, [128, 8, 128]) for r in range(2)]
    badab = PT_("badab", [128, 2048])
    for r in range(2):
        cp("dve", scb[r][:], scT[:, :, r:r + 1].to_broadcast([128, 8, 128]), [scT.b], [scb[r].b])
    dma("sp", badab[:, 0:1024], bada_d[0, 2048:3072].partition_broadcast(128), [], [badab.b])
    dma("sp", badab[:, 1024:2048], bada_d[0, 5120:6144].partition_broadcast(128), [badab.b], [badab.b])
    modbank = bring.next()
    wada_v = wada_d.rearrange("(k p) c -> p k c", p=128)

    def mod_block(blk):
        wsl = wslots.next()
        dma("sp", wsl[:], wada_v[:, :, blk * 512:(blk + 1) * 512], [], [wsl.b])
        for m4 in range(4):
            m = blk * 4 + m4
            for kc in range(KC):
                mm(modbank[:, 2 * m:2 * m + 2], wsl[:, kc, m4 * 128:(m4 + 1) * 128], scT[:, kc, :],
                   kc == 0, kc == KC - 1, [wsl.b, scT.b], [modbank.b])
        if blk in (4, 5, 10, 11):
            gi = 0 if blk < 6 else 1
            cgi = blk % 2
            for r in range(2):
                bk = bring.next()
                if bk is modbank:
                    bk = bring.next()
                for kc in range(KC):
                    mm(bk[:, :], scb[r][:, kc, :], wsl[:, kc, :], kc == 0, kc == KC - 1, [wsl.b, scb[r].b], [bk.b])
                off = gi * 1024 + cgi * 512
                tt("dve", gateb[r][:, off:off + 512], bk[:, :], badab[:, off:off + 512], ALU.add,
                   [bk.b, badab.b], [gateb[r].b])

    prep_jobs = []
    stf = Ring([PT_("stf%d" % i, [128, 2048]) for i in range(6)])
    stb = Ring([PT_("stb%d" % i, [128, 2048], BF) for i in range(8)])
    casters = Ring(["dve", "act", "pool", "dve", "act", "dve"])

    def prep_piece(src, kc, c0, c1, mapping):
        f = stf.next(); b = stb.next()
        dma("sp", f[:, 0:c1 - c0], src[kc * 128:(kc + 1) * 128, c0:c1], [], [f.b])
        ce = casters.next()
        if ce == "act":
            act(b[:, 0:c1 - c0], f[:, 0:c1 - c0], AF.Copy, [f.b], [b.b])
        else:
            cp(ce, b[:, 0:c1 - c0], f[:, 0:c1 - c0], [f.b], [b.b])
        for (dst, srcfn) in mapping(kc, c0, c1):
            dma("act", dst, srcfn(b, c0), [b.b], [])

    def prep(src, K, N, mapping):
        nkc = K // 128
        for kc in range(nkc):
            for c0 in range(0, N, 2048):
                c1 = min(N, c0 + 2048)
                prep_jobs.append((src, kc, c0, c1, mapping))

    def map_simple(stream, sname, gw, col_lo, col_hi, src_off=0):
        def f(kc, c0, c1):
            out = []
            a = max(c0, col_lo); bnd = min(c1, col_hi)
            c = a
            while c < bnd:
                g = (c - col_lo) // gw
                e_ = min(bnd, col_lo + (g + 1) * gw)
                o = c - col_lo - g * gw
                out.append((stream[g, :, kc, o:o + (e_ - c)], lambda b, c0, c=c, e_=e_: b[:, c - c0:e_ - c0]))
                c = e_
            return out
        return f

    def map_chunks(specs):
        def f(kc, c0, c1):
            out = []
            for (col_lo, nch, stream, off) in specs:
                i0 = max(0, -(-(c0 - col_lo) // 128))
                i1 = min(nch, (c1 - col_lo) // 128)
                if i1 <= i0:
                    continue
                a0 = col_lo + i0 * 128; a1 = col_lo + i1 * 128
                out.append((stream[i0:i1, :, kc, off:off + 128].rearrange("g p c -> p g c"),
                            lambda b, c0, a0=a0, a1=a1: b[:, a0 - c0:a1 - c0].rearrange("p (g c) -> p g c", c=128)))
            return out
        return f

    def map_multi(fs):
        def f(kc, c0, c1):
            out = []
            for g in fs:
                out.extend(g(kc, c0, c1))
            return out
        return f

    prep(win_d, D, 8 * D, map_multi([
        map_simple(Wq, "Wq", 512, 0, 1024), map_simple(Wk, "Wk", 512, 1024, 2048), map_simple(Wv, "Wv", 512, 2048, 3072),
        map_chunks([(3072, 8, Wcv, 0), (4096, 8, Wcv, 128), (5120, 8, Wcv, 256),
                    (6144, 8, Wmg, 0), (7168, 8, Wmg, 128)])]))
    prep(wao_d, D, D, map_chunks([(0, 8, Wmg, 256)]))
    prep(wco_d, D, D, map_chunks([(0, 8, Wmg, 384)]))
    prep(wo_d, D, D, map_simple(Wo, "Wo", 512, 0, 1024))
    prep(wgu_d, D, 2 * DFF, map_chunks([(0, NF, Wgu, 0), (DFF, NF, Wgu, 128)]))
    prep(wd_d, DFF, D, map_simple(Wd, "Wd", 512, 0, 1024))
    nj = len(prep_jobs)
    per = max(1, nj // 12)
    mb = 0
    for ji, job in enumerate(prep_jobs):
        if ji % per == 0 and mb < 12:
            mod_block(mb); mb += 1
        prep_piece(*job)
    while mb < 12:
        mod_block(mb); mb += 1
    tt("dve", modT[:], modbank[:, 0:96].rearrange("p (m r) -> p m r", r=2),
       badaT[:].unsqueeze(2).to_broadcast([128, 48, 2]), ALU.add, [modbank.b, badaT.b], [modT.b])
    for r in range(2):
        stt(g1p[:, r, :], modT[:, 8:16, r], 1.0, n1g[:], ALU.add, ALU.mult, [modT.b, n1g.b], [g1p.b])
        stt(g2p[:, r, :], modT[:, 32:40, r], 1.0, n2g[:], ALU.add, ALU.mult, [modT.b, n2g.b], [g2p.b])


    ckf = Ring([PT_("ckf%d" % i, [128, D]) for i in range(2)])
    ckb = Ring([PT_("ckb%d" % i, [128, D], BF) for i in range(2)])
    kts_s = PT_("kts_s", [128, NH, 512], BF)
    vs_s = PT_("vs_s", [128, NH, 4, 128], BF)
    for kb in range(2):
        for j in range(4):
            t0 = (kb * 4 + j) * 128
            f = ckf.next(); b = ckb.next()
            dma("sp", f[:], ck_d[t0:t0 + 128, :], [], [f.b])
            cp("dve", b[:], f[:], [f.b], [b.b])
            bk = bring.next()
            bkb = bk[:].bitcast(BF)
            for h in range(NH):
                tr(bkb[:, h * 128:(h + 1) * 128], b[:, h * 128:(h + 1) * 128], ident[:], [b.b, ident.b], [bk.b])
            cp("dve", kts_s[:, :, j * 128:(j + 1) * 128], bkb[:, :].rearrange("p (h c) -> p h c", c=128), [bk.b], [kts_s.b])
            f = ckf.next()
            dma("sp", f[:], cv_d[t0:t0 + 128, :], [], [f.b])
            cp("pool", vs_s[:, :, j, :], f[:].rearrange("p (h d) -> p h d", d=128), [f.b], [vs_s.b])
        dma("pool", KVs[:, kb, :, 0:512].rearrange("h p c -> p h c"), kts_s[:], [kts_s.b], [DB(("KVs", kb, "k"))])
        dma("pool", KVs[:, kb, :, 512:1024].rearrange("h p (t d) -> p h t d", d=128), vs_s[:], [vs_s.b], [DB(("KVs", kb, "v"))])

    sc.barrier()
    es.close()

    xts = [P("xt0", [128, 4, D]), P("xt1", [128, 4, D])]; xn = P("xn", [128, 4, D], BF); sqj = xn
    hT = P("hT", [128, 8, 512], BF)
    QT = P("QT", [128, NH, 1, 512], BF); QTb = [Buf("QT%d" % h) for h in range(NH)]
    KTs = P("KTs", [128, NH, 512], BF); Vs = P("Vs", [128, NH, 4, 128], BF)
    big = P("big", [128, 3 * 8 * 512], BF)

    class View:
        def __init__(self, ap, name):
            self.ap = ap
            self.b = Buf(name)

        def __getitem__(self, k):
            return self.ap[k]

    zT = View(big[:, 0:4096].rearrange("p (c t) -> p c t", t=512), "zT")
    OT = View(big[:, 4096:8192].rearrange("p (c t) -> p c t", t=512), "OT"); OTb = [Buf("OT%d" % h) for h in range(NH)]
    mT = View(big[:, 8192:12288].rearrange("p (c t) -> p c t", t=512), "mT")
    actT = View(big[:, 0:NF * 512].rearrange("p (c t) -> p c t", t=512), "actT")
    ALIAS = [zT.b, mT.b] + OTb
    wring = Ring([P("wr%d" % i, [128, 8, 512], BF) for i in range(4)])
    kvring = Ring([P("kv%d" % i, [128, 1024], BF) for i in range(4)])
    ptring = Ring([P("pt%d" % i, [128, 2, 512], BF) for i in range(5)])
    accs = [P("acc%d" % i, [128, 512]) for i in range(2)]
    o1s = [P("o1s%d" % i, [128, 512]) for i in range(2)]
    o2s = [P("o2s%d" % i, [128, 512]) for i in range(2)]
    l2s = [P("l2s%d" % i, [128, 512]) for i in range(2)]
    accD = [Buf("accD%d" % i) for i in range(2)]
    accPb = [Buf("accP%d" % i) for i in range(2)]
    CS = 352
    tf = Ring([P("tf%d" % i, [128, 512]) for i in range(6)])
    tb = Ring([P("tb%d" % i, [128, 512], BF) for i in range(3)])
    utr = Ring([P("ut%d" % i, [128, 514]) for i in range(2)])
    print("sbuf bytes remaining:", nc.sbuf_bytes_remaining)
    mset("pool", QT[:], 0.0, [QT.b] + QTb)

    def frontA(xt, tiles):
        for i, (r0, R) in enumerate(tiles):
            act(sqj[0:R, i, :], xt[0:R, i, :], AF.Square, [xt.b], [xn.b, ss.b], accum_out=ss[0:R, i:i + 1])
            ts("dve", rs[0:R, i:i + 1], ss[0:R, i:i + 1], 1.0 / D, EPS, ALU.mult, ALU.add, [ss.b], [rs.b])
        rsqrt_inplace(rs[:, 0:len(tiles)] if tiles[0][1] == 128 else rs[0:tiles[0][1], 0:1], rs.b)
        for i, (r0, R) in enumerate(tiles):
            ts("dve", xn[0:R, i, :], xt[0:R, i, :], rs[0:R, i:i + 1], None, ALU.mult, None, [xt.b, rs.b], [xn.b])

    def frontB(path, TB, tiles, gp, sh_lo):
        usedbanks = {}
        for i, (r0, R) in enumerate(tiles):
            for kc in range(KC):
                off = kc * TB + i * 128
                bi = off // 1024
                if bi not in usedbanks:
                    usedbanks[bi] = bring.next()
                bk = usedbanks[bi]
                o = off % 1024
                tr(bk[:].bitcast(BF)[:, o:o + R], xn[0:R, i, kc * 128:(kc + 1) * 128], ident[0:R, 0:R],
                   [xn.b, ident.b], [bk.b])
        for kc in range(KC):
            off = kc * TB
            bk = usedbanks[off // 1024]
            o = off % 1024
            ts("dve", hT[:, kc, 0:TB], bk[:].bitcast(BF)[:, o:o + TB], gp[:, path, kc:kc + 1],
               modT[:, sh_lo + kc, path:path + 1], ALU.mult, ALU.add, [bk.b, gp.b, modT.b], [hT.b])

    def blk_params(path, b):
        prompt = path == 0
        TB = 512 if prompt else TS
        tiles = [(i * 128, 128) for i in range(4)] if prompt else [(0, TS)]
        xsrc = x_d if prompt else xs_d
        row0 = b * 512 if prompt else 0
        slot = (b % 2) if prompt else (NB % 2)
        return prompt, TB, tiles, xsrc, row0, xts[slot]

    def load_x(path, b):
        prompt, TB, tiles, xsrc, row0, xt = blk_params(path, b)
        for i, (r0, R) in enumerate(tiles):
            dma("sp", xt[0:R, i, :], xsrc[row0 + r0:row0 + r0 + R, :], [], [xt.b])

    def load_w(stream, sname, g, shape_kc, ncols, kc0=0):
        w = wring.next()
        dma("sp", w[:, 0:shape_kc, 0:ncols], stream[g, :, kc0:kc0 + shape_kc, :], [], [w.b])
        return w

    def headnorm_fm(bk, TB, outs):
        sq = tb.next()
        act(sq[:, 0:TB], bk[:, 0:TB], AF.Square, [bk.b], [sq.b])

        def part2():
            nb = bring.next()
            mm(nb[:, 0:TB], bones[:], sq[:, 0:TB], True, True, [sq.b, bones.b], [nb.b])
            sd = tf.next()
            act(sd[:, 0:TB], nb[:, 0:TB], AF.Ln, [nb.b, epsT.b], [sd.b], bias=epsT[:, 0:1], scale=1.0)
            act(sd[:, 0:TB], sd[:, 0:TB], AF.Exp, [sd.b], [sd.b], scale=-0.5)
            for (oap, p0, p1, g, wb) in outs:
                stt(oap, bk[p0:p1, 0:TB], g[p0:p1, 0:1], sd[p0:p1, 0:TB], ALU.mult, ALU.mult, [bk.b, sd.b, g.b], wb)
        return part2

    def do_block(path, b, nextblk):
        prompt, TB, tiles, xsrc, row0, xt = blk_params(path, b)
        KV = KVp if prompt else KVs
        kvname = "KVp" if prompt else "KVs"
        kb_new = b if prompt else 2
        yk, yv, yy = (ko_d, vo_d, y_d) if prompt else (kso_d, vso_d, ys_d)
        if prompt and b == 0:
            load_x(path, b)
            frontA(xt, tiles)
            frontB(path, TB, tiles, g1p, 0)
        pend = None
        for hh in range(2 * NH):
            isq = hh < NH
            h = hh % NH
            if h % 4 == 0:
                w = load_w(Wq if isq else Wk, "W", h // 4, 8, 512)
            bk = bring.next()
            for kc in range(KC):
                mm(bk[:, 0:TB], w[:, kc, (h % 4) * 128:(h % 4 + 1) * 128], hT[:, kc, 0:TB], kc == 0, kc == KC - 1,
                   [w.b, hT.b], [bk.b])
            if isq:
                nxt = headnorm_fm(bk, TB, [(QT[:, h, 0, 0:TB], 0, 128, qg2, [QTb[h]])])
            else:
                nxt = headnorm_fm(bk, TB, [(KTs[:, h, 0:TB], 0, 128, kg2, [KTs.b])])
            if pend is not None:
                pend()
            pend = nxt
        pend()
        dma("pool", KV[:, kb_new, :, 0:TB].rearrange("h p c -> p h c"), KTs[:, :, 0:TB], [KTs.b],
            [DB((kvname, kb_new, "k"))])
        if nextblk is not None:
            load_x(*nextblk)
        for cg in range(2):
            w = load_w(Wk, "Wk", cg, 8, 512)
            for i, (r0, R) in enumerate(tiles):
                bk = bring.next()
                for kc in range(KC):
                    mm(bk[0:R, :], hT[:, kc, r0:r0 + R], w[:, kc, :], kc == 0, kc == KC - 1, [w.b, hT.b], [bk.b])
                sq = tf.next()
                act(sq[0:R, :], bk[0:R, :], AF.Square, [bk.b], [sq.b])
                sc.add("dve", lambda e, sq=sq, R=R: e.tensor_reduce(
                    out=ssk[0:R, 0:8], in_=sq[0:R, :].rearrange("p (g d) -> p g d", d=64), axis=AX.X, op=ALU.add),
                    [sq.b], [ssk.b])
                ts("dve", ssk[0:R, :], ssk[0:R, :], 1.0 / 64, EPS, ALU.mult, ALU.add, [ssk.b], [ssk.b])
                rsqrt_inplace(ssk[0:R, :], ssk.b)
                ko = tf.next()
                tt("dve", ko[0:R, :].rearrange("p (g d) -> p g d", d=64), bk[0:R, :].rearrange("p (g d) -> p g d", d=64),
                   ssk[0:R, 0:8].unsqueeze(2).to_broadcast([R, 8, 64]), ALU.mult, [bk.b, ssk.b], [ko.b])
                tt("dve", ko[0:R, :], ko[0:R, :], kgb[0:R, cg * 512:(cg + 1) * 512], ALU.mult, [ko.b, kgb.b], [ko.b])
                dma("pool", yk[row0 + r0:row0 + r0 + R, cg * 512:(cg + 1) * 512], ko[0:R, :], [ko.b], [])
        for cg in range(2):
            w = load_w(Wv, "Wv", cg, 8, 512)
            for i, (r0, R) in enumerate(tiles):
                bk = bring.next()
                for kc in range(KC):
                    mm(bk[0:R, :], hT[:, kc, r0:r0 + R], w[:, kc, :], kc == 0, kc == KC - 1, [w.b, hT.b], [bk.b])
                vo = tf.next()
                act(vo[0:R, :], bk[0:R, :], AF.Copy, [bk.b], [vo.b])
                act(Vs[0:R, 4 * cg:4 * cg + 4, i, :], bk[0:R, :].rearrange("p (h d) -> p h d", d=128), AF.Copy, [bk.b], [Vs.b])
                dma("pool", yv[row0 + r0:row0 + r0 + R, cg * 512:(cg + 1) * 512], vo[0:R, :], [vo.b], [])
        if prompt:
            dma("pool", KV[:, kb_new, :, 512:1024].rearrange("h p (t d) -> p h t d", d=128), Vs[:], [Vs.b],
                [DB((kvname, kb_new, "v"))])
        else:
            dma("pool", KV[:, kb_new, 0:TS, 512:640].rearrange("h p d -> p h d"), Vs[0:TS, :, 0, :], [Vs.b],
                [DB((kvname, kb_new, "v"))])
        ut_ = utail[path]
        for c in range(8):
            w = load_w(Wcv, "Wcv", c, 8, 384)
            bx = bring.next(); bgb = bring.next(); bgc = bring.next()
            for (bk, q) in ((bx, 0), (bgb, 1), (bgc, 2)):
                for kc in range(KC):
                    mm(bk[:, 0:TB], w[:, kc, q * 128:(q + 1) * 128], hT[:, kc, 0:TB], kc == 0, kc == KC - 1, [w.b, hT.b], [bk.b])
            xsf = tf.next()
            act(xsf[:, 0:TB], bx[:, 0:TB], AF.Copy, [bx.b], [xsf.b])
            u = utr.next()
            cp("pool", u[:, 0:2], ut_[:, c, :], [ut_.b], [u.b])
            tt("dve", u[:, 2:2 + TB], bgc[:, 0:TB], xsf[:, 0:TB], ALU.mult, [bgc.b, xsf.b, u.b], [u.b])
            cp("pool", ut_[:, c, :], u[:, TB:TB + 2], [u.b], [ut_.b])
            cvt = tf.next()
            ts("dve", cvt[:, 0:TB], u[:, 2:2 + TB], cw[:, c, 2:3], None, ALU.mult, None, [u.b, cw.b], [cvt.b])
            stt(cvt[:, 0:TB], u[:, 1:1 + TB], cw[:, c, 1:2], cvt[:, 0:TB], ALU.mult, ALU.add, [u.b, cw.b, cvt.b], [cvt.b])
            stt(cvt[:, 0:TB], u[:, 0:TB], cw[:, c, 0:1], cvt[:, 0:TB], ALU.mult, ALU.add, [u.b, cw.b, cvt.b], [cvt.b])
            tt("dve", zT[:, c, 0:TB], bgb[:, 0:TB], cvt[:, 0:TB], ALU.mult, [bgb.b, cvt.b], [zT.b, actT.b])
        nkb = (b + 1) if prompt else 3
        Sp = [(banks[0], banks[1]), (banks[2], banks[3])]
        O1, O2, LB, B7 = banks[4], banks[5], banks[6], banks[7]
        jobs = []
        for h in range(NH):
            hj = []
            for kb in range(nkb):
                diag = prompt and kb == b
                ntile = 1 if (not prompt and kb == 2) else 4
                for j in range(ntile):
                    KR = TS if (not prompt and kb == 2) else 128
                    hj.append((kb, j, KR, 128 * j if diag else 0, diag))
            for n, (kb, j, KR, q0, diag) in enumerate(hj):
                jobs.append((h, n, len(hj), kb, j, KR, q0, diag))
        G = len(jobs)
        njh = G // NH
        dA = max(1, min(6, njh - 2))
        dB = max(1, min(5, njh - 1))
        kvrec = {}
        ptof = {}
        st = {"spair": 0}
        deferred = []

        def next_pair():
            p = st["spair"] % 2
            st["spair"] += 1
            return p

        def emit_S(g):
            (h, n, nj, kb, j, KR, q0, diag) = jobs[g]
            if (h, kb) not in kvrec:
                kv = kvring.next()
                dma("sp", kv[:], KV[h, kb, :, :], [DB((kvname, kb, "k")), DB((kvname, kb, "v"))], [kv.b])
                kvrec[(h, kb)] = kv
            kv = kvrec[(h, kb)]
            NQ = TB - q0
            p = next_pair()
            sa, sb_ = Sp[p]
            mm(sa[0:KR, 0:NQ], kv[0:64, j * 128:j * 128 + KR], QT[0:64, h, 0, q0:TB], True, True, [kv.b, QTb[h]], [sa.b])
            mm(sb_[0:KR, 0:NQ], kv[64:128, j * 128:j * 128 + KR], QT[64:128, h, 0, q0:TB], True, True, [kv.b, QTb[h]], [sb_.b])
            pt = ptring.next()
            ptof[g] = pt
            act(pt[0:KR, :, q0:TB], pall[0:KR, p * 1024:(p + 1) * 1024].rearrange("p (m c) -> p m c", m=2)[:, :, 0:NQ],
                AF.Exp, [sa.b, sb_.b], [pt.b], scale=0.125)
            if diag:
                mset("dve", pt[64:128, :, q0:q0 + 64], 0.0, [pt.b])
            acc = accs[h % 2]
            ab = accD[h % 2]
            if n == 0:
                sc.add("dve", lambda e, a=acc[0:KR, q0:TB], p_=pt[0:KR, 0, q0:TB]: e.tensor_copy(out=a, in_=p_), [pt.b], [ab])
            else:
                sc.add("dve", lambda e, a=acc[0:KR, q0:TB], p_=pt[0:KR, 0, q0:TB]: e.tensor_tensor(out=a, in0=a, in1=p_, op=ALU.add),
                       [pt.b, ab], [ab])

        def epiB(h, ta, rb, sq):
            nb = B7
            mm(nb[:, 0:TB], ones128[:], sq[:, 0:TB], True, True, [sq.b, ones128.b], [nb.b])
            act(rb[:, 0:TB], nb[:, 0:TB], AF.Ln, [nb.b, epsT.b], [rb.b], bias=epsT[:, 0:1], scale=1.0)
            act(rb[:, 0:TB], rb[:, 0:TB], AF.Exp, [rb.b], [rb.b], scale=-0.5)
            stt(OT[:, h, 0:TB], ta[:, 0:TB], subg2[:, 0:1], rb[:, 0:TB], ALU.mult, ALU.mult, [ta.b, rb.b, subg2.b], [OTb[h], actT.b])

        def epiA(h):
            par = h % 2
            acc = accs[par]
            mm(B7[:, 0:TB], onesf[:], acc[:, 0:TB], True, True, [accD[par], onesf.b], [B7.b])
            ra = tf.next(); rb = tf.next(); ta = tf.next(); tb_ = tf.next()
            recip(ra[:, 0:TB], B7[:, 0:TB], [B7.b], [ra.b])
            act(rb[:, 0:TB], l2s[par][:, 0:TB], AF.Exp, [l2s[par].b], [rb.b], scale=-1.0)
            tt("dve", ta[:, 0:TB], o1s[par][:, 0:TB], ra[:, 0:TB], ALU.mult, [o1s[par].b, ra.b], [ta.b])
            tt("dve", tb_[:, 0:TB], o2s[par][:, 0:TB], rb[:, 0:TB], ALU.mult, [o2s[par].b, rb.b], [tb_.b])
            stt(ta[:, 0:TB], tb_[:, 0:TB], neglam[:, 0:1], ta[:, 0:TB], ALU.mult, ALU.add, [tb_.b, ta.b, neglam.b], [ta.b])
            sq = tb.next()
            tt("pool", sq[:, 0:TB], ta[:, 0:TB], ta[:, 0:TB], ALU.mult, [ta.b], [sq.b])
            deferred.append([dB, lambda: epiB(h, ta, rb, sq)])

        def emit_PV(g):
            (h, n, nj, kb, j, KR, q0, diag) = jobs[g]
            kv = kvrec[(h, kb)]
            pt = ptof.pop(g)
            first = n == 0
            last = n == nj - 1
            vt = kv[0:KR, 512 + j * 128:512 + (j + 1) * 128]
            mm(O1[:, q0:TB], vt, pt[0:KR, 0, q0:TB], first, last, [kv.b, pt.b], [O1.b])
            mm(O2[:, q0:TB], vt, pt[0:KR, 1, q0:TB], first, last, [kv.b, pt.b], [O2.b])
            mm(LB[:, q0:TB], ones[0:KR, :], pt[0:KR, 1, q0:TB], first, last, [ones.b, pt.b], [LB.b])
            if last:
                par = h % 2
                act(o1s[par][:, 0:TB], O1[:, 0:TB], AF.Copy, [O1.b], [o1s[par].b])
                act(o2s[par][:, 0:TB], O2[:, 0:TB], AF.Copy, [O2.b], [o2s[par].b])
                act(l2s[par][:, 0:TB], LB[:, 0:TB], AF.Ln, [LB.b], [l2s[par].b])
                deferred.append([dA, lambda: epiA(h)])

        def tick():
            for d in deferred:
                d[0] -= 1
            while deferred and deferred[0][0] <= 0:
                deferred.pop(0)[1]()

        emit_S(0)
        if G > 1:
            emit_S(1)
        for g in range(G):
            if g + 2 < G:
                emit_S(g + 2)
            emit_PV(g)
            tick()
        while deferred:
            deferred.pop(0)[1]()
        for j in range(8):
            w = load_w(Wmg, "Wmg", j, 8, 512)
            bga = bring.next(); bgb = bring.next(); bya = bring.next(); byb = bring.next()
            for kc in range(KC):
                mm(bga[:, 0:TB], w[:, kc, 0:128], hT[:, kc, 0:TB], kc == 0, kc == KC - 1, [w.b, hT.b], [bga.b])
            for kc in range(KC):
                mm(bgb[:, 0:TB], w[:, kc, 128:256], hT[:, kc, 0:TB], kc == 0, kc == KC - 1, [w.b, hT.b], [bgb.b])
            for kc in range(KC):
                mm(bya[:, 0:TB], w[:, kc, 256:384], OT[:, kc, 0:TB], kc == 0, kc == KC - 1, [w.b, OTb[kc]], [bya.b])
            for kc in range(KC):
                mm(byb[:, 0:TB], w[:, kc, 384:512], zT[:, kc, 0:TB], kc == 0, kc == KC - 1, [w.b, zT.b], [byb.b])
            sga = tf.next(); sgb = tf.next()
            act(sga[:, 0:TB], bga[:, 0:TB], AF.Sigmoid, [bga.b], [sga.b])
            act(sgb[:, 0:TB], bgb[:, 0:TB], AF.Sigmoid, [bgb.b], [sgb.b])
            tt("dve", sga[:, 0:TB], bya[:, 0:TB], sga[:, 0:TB], ALU.mult, [bya.b, sga.b], [sga.b])
            tt("dve", sgb[:, 0:TB], byb[:, 0:TB], sgb[:, 0:TB], ALU.mult, [byb.b, sgb.b], [sgb.b])
            tt("pool", mT[:, j, 0:TB], sga[:, 0:TB], sgb[:, 0:TB], ALU.add, [sga.b, sgb.b], [mT.b, actT.b])
        for cg in range(2):
            w = load_w(Wo, "Wo", cg, 8, 512)
            for i, (r0, R) in enumerate(tiles):
                bk = bring.next()
                for kc in range(KC):
                    mm(bk[0:R, :], mT[:, kc, r0:r0 + R], w[:, kc, :], kc == 0, kc == KC - 1, [w.b, mT.b], [bk.b])
                t_ = tf.next()
                tt("dve", t_[0:R, :], bk[0:R, :], gateb[path][0:R, cg * 512:(cg + 1) * 512], ALU.mult,
                   [bk.b, gateb[path].b], [t_.b])
                tt("pool", xt[0:R, i, cg * 512:(cg + 1) * 512], xt[0:R, i, cg * 512:(cg + 1) * 512], t_[0:R, :], ALU.add,
                   [xt.b, t_.b], [xt.b])
        frontA(xt, tiles)
        frontB(path, TB, tiles, g2p, 24)
        if nextblk is not None:
            nprompt, nTB, ntiles, _, _, nxt_xt = blk_params(*nextblk)
            frontA(nxt_xt, ntiles)
        for f in range(NF):
            w = load_w(Wgu, "Wgu", f, 8, 256)
            bg = bring.next(); bu = bring.next()
            for kc in range(KC):
                mm(bg[:, 0:TB], w[:, kc, 0:128], hT[:, kc, 0:TB], kc == 0, kc == KC - 1, [w.b, hT.b], [bg.b])
            for kc in range(KC):
                mm(bu[:, 0:TB], w[:, kc, 128:256], hT[:, kc, 0:TB], kc == 0, kc == KC - 1, [w.b, hT.b], [bu.b])
            sg = tf.next()
            act(sg[:, 0:TB], bg[:, 0:TB], AF.Silu, [bg.b], [sg.b])
            tt("dve", actT[:, f, 0:TB], bu[:, 0:TB], sg[:, 0:TB], ALU.mult, [bu.b, sg.b], [actT.b] + ALIAS)
        if nextblk is not None:
            frontB(nextblk[0], nTB, ntiles, g1p, 0)
        for cg in range(2):
            ybk = [bring.next() for _ in tiles]
            for (k0, k1) in ((0, 8), (8, 16), (16, NF)):
                w = wring.next()
                dma("sp", w[:, 0:k1 - k0, :], Wd[cg, :, k0:k1, :], [], [w.b])
                for i, (r0, R) in enumerate(tiles):
                    for kc in range(k0, k1):
                        mm(ybk[i][0:R, :], actT[:, kc, r0:r0 + R], w[:, kc - k0, :], kc == 0, kc == NF - 1,
                           [w.b, actT.b], [ybk[i].b])
            for i, (r0, R) in enumerate(tiles):
                t_ = tf.next()
                tt("dve", t_[0:R, :], ybk[i][0:R, :], gateb[path][0:R, 1024 + cg * 512:1024 + (cg + 1) * 512], ALU.mult,
                   [ybk[i].b, gateb[path].b], [t_.b])
                tt("pool", xt[0:R, i, cg * 512:(cg + 1) * 512], xt[0:R, i, cg * 512:(cg + 1) * 512], t_[0:R, :], ALU.add,
                   [xt.b, t_.b], [xt.b])
        for i, (r0, R) in enumerate(tiles):
            dma("pool", yy[row0 + r0:row0 + r0 + R, :], xt[0:R, i, :], [xt.b], [])

    for b in range(NB):
        do_block(0, b, (0, b + 1) if b + 1 < NB else (1, 0))
    do_block(1, 0, None)
    for c in range(8):
        dma("pool", co_d.rearrange("t (c p) -> p c t", p=128)[:, c, :], utail[0][:, c, :], [utail[0].b], [],
            allow_slow_non_contiguous=True)
        dma("pool", cso_d.rearrange("t (c p) -> p c t", p=128)[:, c, :], utail[1][:, c, :], [utail[1].b], [],
            allow_slow_non_contiguous=True)
    sc.emit(nc)
    return nc


def _consts():
    c = np.zeros((128, 256), np.float32)
    c[:, 0:128] = np.eye(128, dtype=np.float32)
    for p in range(128):
        c[p, 128 + (p // 64) * 64:128 + (p // 64 + 1) * 64] = 1.0 / 64
    return c


def make_in_maps(inp, S, ncores):
    f = lambda a: np.ascontiguousarray(np.asarray(a, dtype=np.float32))
    fm8 = lambda v: f(np.asarray(v).reshape(8, 128).T)
    shared = {
        "w_ada": f(inp["w_ada"][0]), "b_ada": f(inp["b_ada"][0]).reshape(1, -1),
        "b_adaT": f(np.asarray(inp["b_ada"][0]).reshape(48, 128).T),
        "n1g": fm8(inp["norm1_g"][0]), "n2g": fm8(inp["norm2_g"][0]),
        "w_in": f(inp["w_in"][0]),
        "qg2": f(np.tile(np.asarray(inp["q_norm_g"][0]), 2).reshape(128, 1)),
        "kg2": f(np.tile(np.asarray(inp["k_norm_g"][0]), 2).reshape(128, 1)),
        "kgrow": f(np.tile(np.asarray(inp["k_norm_g"][0]), 16).reshape(1, D)),
        "lam4": f(np.concatenate([np.asarray(inp[k][0]) for k in ("lambda_q1", "lambda_k1", "lambda_q2", "lambda_k2")]).reshape(1, 256)),
        "subg": f(np.asarray(inp["sub_norm_g"][0]).reshape(128, 1)),
        "w_attn_out": f(inp["w_attn_out"][0]),
        "cwT": f(np.asarray(inp["conv_w"][0]).reshape(3, 8, 128).transpose(2, 1, 0)),
        "w_conv_out": f(inp["w_conv_out"][0]), "w_out": f(inp["w_out"][0]),
        "w_gate_up": f(inp["w_gate_up"][0]), "w_down": f(inp["w_down"][0]),
        "consts": _consts(),
    }
    maps = []
    for b in range(ncores):
        m = dict(shared)
        m["x"] = f(inp["x_prompt"][b]); m["xs"] = f(inp["x_sample"][b])
        m["ck"] = f(np.asarray(inp["cache_k"][0, b]).reshape(PAST, D))
        m["cv"] = f(np.asarray(inp["cache_v"][0, b]).reshape(PAST, D))
        m["sconvT"] = f(np.asarray(inp["state_conv"][0, b]).reshape(2, 8, 128).transpose(2, 1, 0))
        c2 = np.stack([np.asarray(inp["c_prompt"][b]), np.asarray(inp["c_sample"][b])])
        m["cT"] = f(c2.reshape(2, 8, 128).transpose(2, 1, 0))
        maps.append(m)
    return maps


_cache = {}


def run(inp, S, ncores):
    if S not in _cache:
        _cache[S] = build(S)
    nc = _cache[S]
    maps = make_in_maps(inp, S, ncores)
    res = run_bass_kernel_spmd(nc, maps, core_ids=list(range(ncores)))
    R = res.results
    st = lambda k: np.stack([np.asarray(R[b][k], dtype=np.float32) for b in range(ncores)])
    y = st("y"); ys = st("ys")
    kp = st("ko").reshape(1, ncores, S, NH, 2, 64); vp = st("vo").reshape(1, ncores, S, NH, 128)
    cp_ = st("co").reshape(1, ncores, 2, D)
    ks = st("kso").reshape(1, ncores, TS, NH, 2, 64); vs = st("vso").reshape(1, ncores, TS, NH, 128)
    cs = st("cso").reshape(1, ncores, 2, D)
    return (y, ys, kp, vp, cp_, ks, vs, cs)


def kernel(**inputs):
    S = int(np.asarray(inputs["x_prompt"]).shape[1])
    return run(inputs, S, 8)
```

```python
import numpy as np
import concourse.bass as bass
import concourse.mybir as mybir
from concourse.bass_utils import run_bass_kernel_spmd

F32 = mybir.dt.float32
BF = mybir.dt.bfloat16
AF = mybir.ActivationFunctionType
ALU = mybir.AluOpType
AX = mybir.AxisListType

D = 1024
KC = 8
NH = 8
DFF = 2816
NF = 22
EPS = 1e-6
LAM_INIT = 0.2
PAST = 1024
TS = 32
SAME_ENG_WINDOW = 8
NCHAN = 16


class Buf:
    __slots__ = ("name", "w", "r")

    def __init__(self, name=""):
        self.name = name
        self.w = None
        self.r = []


class Op:
    __slots__ = ("eng", "fn", "deps", "is_dma", "needs_inc", "sem", "val", "pos", "noself")

    def __init__(self, eng, fn, is_dma):
        self.eng = eng
        self.fn = fn
        self.deps = []
        self.is_dma = is_dma
        self.needs_inc = False
        self.sem = None
        self.val = 0
        self.pos = 0
        self.noself = False


class Sched:
    ENGS = ("pe", "act", "dve", "pool", "sp")

    def __init__(self):
        self.ops = []
        self.per_eng = {e: [] for e in self.ENGS}
        self.dma_hist = {e: [] for e in self.ENGS}
        self.barrier_deps = {e: [] for e in self.ENGS}

    def add(self, eng, fn, R=(), W=(), dma=False, noself=False):
        op = Op(eng, fn, dma)
        op.noself = noself
        deps = set()
        for b in R:
            if b.w is not None:
                deps.add(b.w)
        for b in W:
            if b.w is not None:
                deps.add(b.w)
            for r in b.r:
                deps.add(r)
        for d in self.barrier_deps[eng]:
            deps.add(d)
        self.barrier_deps[eng] = []
        if dma:
            h = self.dma_hist[eng]
            if len(h) >= NCHAN:
                deps.add(h[-NCHAN])
            h.append(op)
        deps.discard(op)
        op.deps = list(deps)
        for b in R:
            b.r.append(op)
        for b in W:
            b.w = op
            b.r = []
        op.pos = len(self.per_eng[eng])
        self.per_eng[eng].append(op)
        self.ops.append(op)
        return op

    def barrier(self):
        last = []
        for e in self.ENGS:
            ops = self.per_eng[e]
            nd = [o for o in ops[-200:] if not o.is_dma]
            if nd:
                last.append(nd[-1])
            last.extend(self.dma_hist[e][-NCHAN:])
        for e in self.ENGS:
            self.barrier_deps[e] = list(last)

    def emit(self, nc):
        for op in self.ops:
            real = []
            for d in op.deps:
                if not d.is_dma and d.eng == op.eng:
                    if op.eng == "pe" and not op.is_dma:
                        continue
                    if op.pos - d.pos >= SAME_ENG_WINDOW or op.noself:
                        continue
                real.append(d)
                d.needs_inc = True
            op.deps = real
        from contextlib import ExitStack
        with ExitStack() as es:
            csem = {e: es.enter_context(nc.semaphore("c_" + e)) for e in ("pe", "act", "dve", "pool")}
            dsem = {}
            for e in ("sp", "pool", "act"):
                dsem[e] = [es.enter_context(nc.semaphore("d_%s%d" % (e, i))) for i in range(NCHAN)]
            cnt = {e: 0 for e in csem}
            dcnt = {e: [0] * NCHAN for e in dsem}
            dn = {e: 0 for e in dsem}
            for op in self.ops:
                if op.is_dma:
                    i = dn[op.eng] % NCHAN
                    dn[op.eng] += 1
                    dcnt[op.eng][i] += 16
                    op.sem = dsem[op.eng][i]
                    op.val = dcnt[op.eng][i]
                elif op.needs_inc:
                    cnt[op.eng] += 1
                    op.sem = csem[op.eng]
                    op.val = cnt[op.eng]
            block = es.enter_context(nc.Block())

            def run_stream(ename, e):
                waited = {}
                for op in self.per_eng[ename]:
                    need = {}
                    for d in op.deps:
                        k = id(d.sem)
                        if waited.get(k, 0) >= d.val:
                            continue
                        if k not in need or need[k][1] < d.val:
                            need[k] = (d.sem, d.val)
                    for k, (s, v) in need.items():
                        e.wait_ge(s, v)
                        waited[k] = v
                    ins = op.fn(e)
                    if op.is_dma:
                        ins.then_inc(op.sem, 16)
                    elif op.needs_inc:
                        ins.then_inc(op.sem, 1)
                if ename in dsem:
                    for i in range(NCHAN):
                        if dcnt[ename][i] > 0:
                            e.wait_ge(dsem[ename][i], dcnt[ename][i])

            @block.sync
            def _(e):
                run_stream("sp", e)

            @block.tensor
            def _(e):
                run_stream("pe", e)

            @block.vector
            def _(e):
                run_stream("dve", e)

            @block.scalar
            def _(e):
                run_stream("act", e)

            @block.gpsimd
            def _(e):
                run_stream("pool", e)


class T:
    def __init__(self, nc, name, shape, dtype, psum=False, es=None):
        name = "t_" + name
        if psum:
            self.t = nc.alloc_psum_tensor(name, list(shape), dtype)
        elif es is not None:
            self.t = es.enter_context(nc.sbuf_tensor(name, list(shape), dtype))
        else:
            self.t = nc.alloc_sbuf_tensor(name, list(shape), dtype)
        self.b = Buf(name)

    def __getitem__(self, k):
        return self.t[k]


class Ring:
    def __init__(self, items):
        self.items = items
        self.i = 0

    def next(self):
        it = self.items[self.i % len(self.items)]
        self.i += 1
        return it


def build(S):
    assert S % 512 == 0
    NB = S // 512
    nc = bass.Bass("TRN2", target_bir_lowering=False)
    sc = Sched()

    def din(name, shape):
        return nc.dram_tensor(name, list(shape), F32, kind="ExternalInput").ap()

    def dout(name, shape):
        return nc.dram_tensor(name, list(shape), F32, kind="ExternalOutput").ap()

    def dscr(name, shape):
        return nc.dram_tensor(name, list(shape), BF, kind="Internal").ap()

    x_d = din("x", [S, D]); xs_d = din("xs", [TS, D])
    ck_d = din("ck", [PAST, D]); cv_d = din("cv", [PAST, D])
    sconvT_d = din("sconvT", [128, 8, 2]); cT_d = din("cT", [128, 8, 2])
    wada_d = din("w_ada", [D, 6 * D]); bada_d = din("b_ada", [1, 6 * D]); badaT_d = din("b_adaT", [128, 48])
    n1g_d = din("n1g", [128, 8]); n2g_d = din("n2g", [128, 8])
    win_d = din("w_in", [D, 8 * D])
    qg2_d = din("qg2", [128, 1]); kg2_d = din("kg2", [128, 1]); kgrow_d = din("kgrow", [1, D])
    lam4_d = din("lam4", [1, 256]); subg_d = din("subg", [128, 1])
    wao_d = din("w_attn_out", [D, D]); cwT_d = din("cwT", [128, 8, 3])
    wco_d = din("w_conv_out", [D, D]); wo_d = din("w_out", [D, D])
    wgu_d = din("w_gate_up", [D, 2 * DFF]); wd_d = din("w_down", [DFF, D])
    consts_d = din("consts", [128, 256])

    y_d = dout("y", [S, D]); ys_d = dout("ys", [TS, D])
    ko_d = dout("ko", [S, D]); vo_d = dout("vo", [S, D]); co_d = dout("co", [2, D])
    kso_d = dout("kso", [TS, D]); vso_d = dout("vso", [TS, D]); cso_d = dout("cso", [2, D])

    Wq = dscr("Wq", [2, 128, 8, 512]); Wk = dscr("Wk", [2, 128, 8, 512]); Wv = dscr("Wv", [2, 128, 8, 512])
    Wcv = dscr("Wcv", [8, 128, 8, 384]); Wmg = dscr("Wmg", [8, 128, 8, 512]); Wo = dscr("Wo", [2, 128, 8, 512])
    Wgu = dscr("Wgu", [NF, 128, 8, 256]); Wd = dscr("Wd", [2, 128, NF, 512])
    KVp = dscr("KVp", [NH, NB, 128, 1024]); KVs = dscr("KVs", [NH, 3, 128, 1024])
    dbuf = {}

    def DB(key):
        if key not in dbuf:
            dbuf[key] = Buf(str(key))
        return dbuf[key]

    def P(name, shape, dt=F32):
        return T(nc, name, shape, dt)

    cst = P("cst", [128, 256]); ident = P("ident", [128, 128], BF); bones = P("bones", [128, 128], BF)
    ones = P("ones", [128, 128], BF); ones128 = P("ones128", [128, 128], BF)
    identf = cst
    epsT = P("epsT", [128, 1])
    modT = P("modT", [128, 48, 2]); g1p = P("g1p", [128, 2, 8]); g2p = P("g2p", [128, 2, 8])
    n1g = P("n1g", [128, 8]); n2g = P("n2g", [128, 8])
    qg2 = P("qg2", [128, 1]); kg2 = P("kg2", [128, 1]); subg = P("subg", [128, 1]); subg2 = P("subg2", [128, 1])
    neglam = P("neglam", [128, 1]); lamb = P("lamb", [128, 256]); lprod = P("lprod", [128, 128]); lsum = P("lsum", [128, 2])
    lexp = P("lexp", [128, 2]); lam1 = P("lam1", [128, 1])
    cw = P("cw", [128, 8, 3]); utail = [P("utail0", [128, 8, 2]), P("utail1", [128, 8, 2])]
    gateb = [P("gateb0", [128, 2048]), P("gateb1", [128, 2048])]
    kgb = P("kgb", [128, D])
    badaT = P("badaT", [128, 48]); cT = P("cT", [128, 8, 2]); scT = P("scT", [128, 8, 2])
    ss = P("ss", [128, 4]); rs = P("rs", [128, 4]); ssk = P("ssk", [128, 8])

    pall = nc.alloc_psum_tensor("t_pall", [128, 4096], F32)

    class BankV:
        def __init__(self, i):
            self.ap = pall[:, i * 512:(i + 1) * 512]
            self.b = Buf("bank%d" % i)

        def __getitem__(self, k):
            return self.ap[k]

    banks = [BankV(i) for i in range(8)]
    bring = Ring(banks)
    onesf = P("onesf", [128, 128])

    def dma(q, out, in_, R, W, **kw):
        return sc.add(q, lambda e: e.dma_start(out=out, in_=in_, **kw), R, W, dma=True)

    def mm(out, lhsT, rhs, start, stop, R, W):
        return sc.add("pe", lambda e: e.matmul(out, lhsT=lhsT, rhs=rhs, start=start, stop=stop), R, W)

    def tr(out, in_, idn, R, W):
        return sc.add("pe", lambda e: e.transpose(out, in_, idn), R, W)

    def act(out, in_, func, R, W, **kw):
        return sc.add("act", lambda e: e.activation(out=out, in_=in_, func=func, **kw), R, W)

    def ts(eng, out, in0, s1, s2, op0, op1, R, W):
        if op1 is None:
            return sc.add(eng, lambda e: e.tensor_scalar(out=out, in0=in0, scalar1=s1, scalar2=None, op0=op0), R, W)
        return sc.add(eng, lambda e: e.tensor_scalar(out=out, in0=in0, scalar1=s1, scalar2=s2, op0=op0, op1=op1), R, W)

    def tt(eng, out, in0, in1, op, R, W):
        return sc.add(eng, lambda e: e.tensor_tensor(out=out, in0=in0, in1=in1, op=op), R, W)

    def stt(out, in0, scalar, in1, op0, op1, R, W):
        return sc.add("dve", lambda e: e.scalar_tensor_tensor(out=out, in0=in0, scalar=scalar, in1=in1, op0=op0, op1=op1), R, W)

    def cp(eng, out, in_, R, W):
        return sc.add(eng, lambda e: e.tensor_copy(out=out, in_=in_), R, W)

    def recip(out, in_, R, W):
        act(out, in_, AF.Ln, R, W)
        return act(out, out, AF.Exp, W, W, scale=-1.0)

    def mset(eng, ap, val, W):
        return sc.add(eng, lambda e: e.memset(ap, val), (), W)

    def rsqrt_inplace(ap, b):
        act(ap, ap, AF.Ln, [b], [b])
        act(ap, ap, AF.Exp, [b], [b], scale=-0.5)

    from contextlib import ExitStack
    dma("sp", cst[:], consts_d[:, :], [], [cst.b])
    cp("dve", ident[:], cst[:, 0:128], [cst.b], [ident.b])
    cp("dve", bones[:], cst[:, 128:256], [cst.b], [bones.b])
    mset("dve", ones[:], 1.0, [ones.b])
    mset("dve", ones128[:], 1.0 / 128, [ones128.b])
    mset("dve", onesf[:], 1.0, [onesf.b])
    mset("dve", epsT[:], EPS, [epsT.b])
    mset("dve", utail[0][:], 0.0, [utail[0].b])
    for (t_, d_) in ((n1g, n1g_d), (n2g, n2g_d), (qg2, qg2_d), (kg2, kg2_d), (subg, subg_d), (badaT, badaT_d)):
        dma("sp", t_[:], d_[:, :], [], [t_.b])
    dma("sp", cw[:], cwT_d[:, :, :], [], [cw.b])
    dma("sp", cT[:], cT_d[:, :, :], [], [cT.b])
    dma("sp", utail[1][:], sconvT_d[:, :, :], [], [utail[1].b])
    dma("sp", lamb[:], lam4_d[0].partition_broadcast(128), [], [lamb.b])
    dma("sp", kgb[:], kgrow_d[0].partition_broadcast(128), [], [kgb.b])
    tt("dve", lprod[:, 0:64], lamb[:, 0:64], lamb[:, 64:128], ALU.mult, [lamb.b], [lprod.b])
    tt("dve", lprod[:, 64:128], lamb[:, 128:192], lamb[:, 192:256], ALU.mult, [lamb.b, lprod.b], [lprod.b])
    sc.add("dve", lambda e: e.tensor_reduce(out=lsum[:, 0:2], in_=lprod[:].rearrange("p (g d) -> p g d", d=64),
                                            axis=AX.X, op=ALU.add), [lprod.b], [lsum.b])
    act(lexp[:], lsum[:], AF.Exp, [lsum.b], [lexp.b])
    tt("dve", lam1[:], lexp[:, 1:2], lexp[:, 0:1], ALU.subtract, [lexp.b], [lam1.b])
    ts("dve", neglam[:], lam1[:], -LAM_INIT, None, ALU.add, None, [lam1.b], [neglam.b])
    ts("dve", subg2[:], subg[:], 1.0 - LAM_INIT, None, ALU.mult, None, [subg.b], [subg2.b])
    act(scT[:], cT[:], AF.Silu, [cT.b], [scT.b])

    es = ExitStack()

    def PT_(name, shape, dt=F32):
        return T(nc, name, shape, dt, es=es)

    wslots = Ring([PT_("wadas%d" % i, [128, 8, 512]) for i in range(2)])
    scb = [PT_("scb%d" % r, [128, 8, 128]) for r in range(2)]
    badab = PT_("badab", [128, 2048])
    for r in range(2):
        cp("dve", scb[r][:], scT[:, :, r:r + 1].to_broadcast([128, 8, 128]), [scT.b], [scb[r].b])
    dma("sp", badab[:, 0:1024], bada_d[0, 2048:3072].partition_broadcast(128), [], [badab.b])
    dma("sp", badab[:, 1024:2048], bada_d[0, 5120:6144].partition_broadcast(128), [badab.b], [badab.b])
    modbank = bring.next()
    wada_v = wada_d.rearrange("(k p) c -> p k c", p=128)

    def mod_block(blk):
        wsl = wslots.next()
        dma("sp", wsl[:], wada_v[:, :, blk * 512:(blk + 1) * 512], [], [wsl.b])
        for m4 in range(4):
            m = blk * 4 + m4
            for kc in range(KC):
                mm(modbank[:, 2 * m:2 * m + 2], wsl[:, kc, m4 * 128:(m4 + 1) * 128], scT[:, kc, :],
                   kc == 0, kc == KC - 1, [wsl.b, scT.b], [modbank.b])
        if blk in (4, 5, 10, 11):
            gi = 0 if blk < 6 else 1
            cgi = blk % 2
            for r in range(2):
                bk = bring.next()
                if bk is modbank:
                    bk = bring.next()
                for kc in range(KC):
                    mm(bk[:, :], scb[r][:, kc, :], wsl[:, kc, :], kc == 0, kc == KC - 1, [wsl.b, scb[r].b], [bk.b])
                off = gi * 1024 + cgi * 512
                tt("dve", gateb[r][:, off:off + 512], bk[:, :], badab[:, off:off + 512], ALU.add,
                   [bk.b, badab.b], [gateb[r].b])

    prep_jobs = []
    stf = Ring([PT_("stf%d" % i, [128, 2048]) for i in range(6)])
    stb = Ring([PT_("stb%d" % i, [128, 2048], BF) for i in range(8)])
    casters = Ring(["dve", "act", "pool", "dve", "act", "dve"])

    def prep_piece(src, kc, c0, c1, mapping):
        f = stf.next(); b = stb.next()
        dma("sp", f[:, 0:c1 - c0], src[kc * 128:(kc + 1) * 128, c0:c1], [], [f.b])
        ce = casters.next()
        if ce == "act":
            act(b[:, 0:c1 - c0], f[:, 0:c1 - c0], AF.Copy, [f.b], [b.b])
        else:
            cp(ce, b[:, 0:c1 - c0], f[:, 0:c1 - c0], [f.b], [b.b])
        for (dst, srcfn) in mapping(kc, c0, c1):
            dma("act", dst, srcfn(b, c0), [b.b], [])

    def prep(src, K, N, mapping):
        nkc = K // 128
        for kc in range(nkc):
            for c0 in range(0, N, 2048):
                c1 = min(N, c0 + 2048)
                prep_jobs.append((src, kc, c0, c1, mapping))

    def map_simple(stream, sname, gw, col_lo, col_hi, src_off=0):
        def f(kc, c0, c1):
            out = []
            a = max(c0, col_lo); bnd = min(c1, col_hi)
            c = a
            while c < bnd:
                g = (c - col_lo) // gw
                e_ = min(bnd, col_lo + (g + 1) * gw)
                o = c - col_lo - g * gw
                out.append((stream[g, :, kc, o:o + (e_ - c)], lambda b, c0, c=c, e_=e_: b[:, c - c0:e_ - c0]))
                c = e_
            return out
        return f

    def map_chunks(specs):
        def f(kc, c0, c1):
            out = []
            for (col_lo, nch, stream, off) in specs:
                i0 = max(0, -(-(c0 - col_lo) // 128))
                i1 = min(nch, (c1 - col_lo) // 128)
                if i1 <= i0:
                    continue
                a0 = col_lo + i0 * 128; a1 = col_lo + i1 * 128
                out.append((stream[i0:i1, :, kc, off:off + 128].rearrange("g p c -> p g c"),
                            lambda b, c0, a0=a0, a1=a1: b[:, a0 - c0:a1 - c0].rearrange("p (g c) -> p g c", c=128)))
            return out
        return f

    def map_multi(fs):
        def f(kc, c0, c1):
            out = []
            for g in fs:
                out.extend(g(kc, c0, c1))
            return out
        return f

    prep(win_d, D, 8 * D, map_multi([
        map_simple(Wq, "Wq", 512, 0, 1024), map_simple(Wk, "Wk", 512, 1024, 2048), map_simple(Wv, "Wv", 512, 2048, 3072),
        map_chunks([(3072, 8, Wcv, 0), (4096, 8, Wcv, 128), (5120, 8, Wcv, 256),
                    (6144, 8, Wmg, 0), (7168, 8, Wmg, 128)])]))
    prep(wao_d, D, D, map_chunks([(0, 8, Wmg, 256)]))
    prep(wco_d, D, D, map_chunks([(0, 8, Wmg, 384)]))
    prep(wo_d, D, D, map_simple(Wo, "Wo", 512, 0, 1024))
    prep(wgu_d, D, 2 * DFF, map_chunks([(0, NF, Wgu, 0), (DFF, NF, Wgu, 128)]))
    prep(wd_d, DFF, D, map_simple(Wd, "Wd", 512, 0, 1024))
    nj = len(prep_jobs)
    per = max(1, nj // 12)
    mb = 0
    for ji, job in enumerate(prep_jobs):
        if ji % per == 0 and mb < 12:
            mod_block(mb); mb += 1
        prep_piece(*job)
    while mb < 12:
        mod_block(mb); mb += 1
    tt("dve", modT[:], modbank[:, 0:96].rearrange("p (m r) -> p m r", r=2),
       badaT[:].unsqueeze(2).to_broadcast([128, 48, 2]), ALU.add, [modbank.b, badaT.b], [modT.b])
    for r in range(2):
        stt(g1p[:, r, :], modT[:, 8:16, r], 1.0, n1g[:], ALU.add, ALU.mult, [modT.b, n1g.b], [g1p.b])
        stt(g2p[:, r, :], modT[:, 32:40, r], 1.0, n2g[:], ALU.add, ALU.mult, [modT.b, n2g.b], [g2p.b])


    ckf = Ring([PT_("ckf%d" % i, [128, D]) for i in range(2)])
    ckb = Ring([PT_("ckb%d" % i, [128, D], BF) for i in range(2)])
    kts_s = PT_("kts_s", [128, NH, 512], BF)
    vs_s = PT_("vs_s", [128, NH, 4, 128], BF)
    for kb in range(2):
        for j in range(4):
            t0 = (kb * 4 + j) * 128
            f = ckf.next(); b = ckb.next()
            dma("sp", f[:], ck_d[t0:t0 + 128, :], [], [f.b])
            cp("dve", b[:], f[:], [f.b], [b.b])
            bk = bring.next()
            bkb = bk[:].bitcast(BF)
            for h in range(NH):
                tr(bkb[:, h * 128:(h + 1) * 128], b[:, h * 128:(h + 1) * 128], ident[:], [b.b, ident.b], [bk.b])
            cp("dve", kts_s[:, :, j * 128:(j + 1) * 128], bkb[:, :].rearrange("p (h c) -> p h c", c=128), [bk.b], [kts_s.b])
            f = ckf.next()
            dma("sp", f[:], cv_d[t0:t0 + 128, :], [], [f.b])
            cp("pool", vs_s[:, :, j, :], f[:].rearrange("p (h d) -> p h d", d=128), [f.b], [vs_s.b])
        dma("pool", KVs[:, kb, :, 0:512].rearrange("h p c -> p h c"), kts_s[:], [kts_s.b], [DB(("KVs", kb, "k"))])
        dma("pool", KVs[:, kb, :, 512:1024].rearrange("h p (t d) -> p h t d", d=128), vs_s[:], [vs_s.b], [DB(("KVs", kb, "v"))])

    sc.barrier()
    es.close()

    xts = [P("xt0", [128, 4, D]), P("xt1", [128, 4, D])]; xn = P("xn", [128, 4, D], BF); sqj = xn
    hT = P("hT", [128, 8, 512], BF)
    QT = P("QT", [128, NH, 1, 512], BF); QTb = [Buf("QT%d" % h) for h in range(NH)]
    KTs = P("KTs", [128, NH, 512], BF); Vs = P("Vs", [128, NH, 4, 128], BF)
    big = P("big", [128, 3 * 8 * 512], BF)

    class View:
        def __init__(self, ap, name):
            self.ap = ap
            self.b = Buf(name)

        def __getitem__(self, k):
            return self.ap[k]

    zT = View(big[:, 0:4096].rearrange("p (c t) -> p c t", t=512), "zT")
    OT = View(big[:, 4096:8192].rearrange("p (c t) -> p c t", t=512), "OT"); OTb = [Buf("OT%d" % h) for h in range(NH)]
    mT = View(big[:, 8192:12288].rearrange("p (c t) -> p c t", t=512), "mT")
    actT = View(big[:, 0:NF * 512].rearrange("p (c t) -> p c t", t=512), "actT")
    ALIAS = [zT.b, mT.b] + OTb
    wring = Ring([P("wr%d" % i, [128, 8, 512], BF) for i in range(4)])
    kvring = Ring([P("kv%d" % i, [128, 1024], BF) for i in range(4)])
    ptring = Ring([P("pt%d" % i, [128, 2, 512], BF) for i in range(5)])
    accs = [P("acc%d" % i, [128, 512]) for i in range(2)]
    o12s = [P("o12s%d" % i, [128, 2, 512]) for i in range(2)]
    l2s = [P("l2s%d" % i, [128, 512]) for i in range(2)]
    accD = [Buf("accD%d" % i) for i in range(2)]
    accPb = [Buf("accP%d" % i) for i in range(2)]
    CS = 352
    tf = Ring([P("tf%d" % i, [128, 512]) for i in range(6)])
    tb = Ring([P("tb%d" % i, [128, 512], BF) for i in range(3)])
    utr = Ring([P("ut%d" % i, [128, 514]) for i in range(2)])
    print("sbuf bytes remaining:", nc.sbuf_bytes_remaining)
    mset("pool", QT[:], 0.0, [QT.b] + QTb)

    def frontA(xt, tiles):
        for i, (r0, R) in enumerate(tiles):
            act(sqj[0:R, i, :], xt[0:R, i, :], AF.Square, [xt.b], [xn.b, ss.b], accum_out=ss[0:R, i:i + 1])
            ts("dve", rs[0:R, i:i + 1], ss[0:R, i:i + 1], 1.0 / D, EPS, ALU.mult, ALU.add, [ss.b], [rs.b])
        rsqrt_inplace(rs[:, 0:len(tiles)] if tiles[0][1] == 128 else rs[0:tiles[0][1], 0:1], rs.b)
        for i, (r0, R) in enumerate(tiles):
            ts("dve", xn[0:R, i, :], xt[0:R, i, :], rs[0:R, i:i + 1], None, ALU.mult, None, [xt.b, rs.b], [xn.b])

    def frontB(path, TB, tiles, gp, sh_lo):
        usedbanks = {}
        for i, (r0, R) in enumerate(tiles):
            for kc in range(KC):
                off = kc * TB + i * 128
                bi = off // 1024
                if bi not in usedbanks:
                    usedbanks[bi] = bring.next()
                bk = usedbanks[bi]
                o = off % 1024
                tr(bk[:].bitcast(BF)[:, o:o + R], xn[0:R, i, kc * 128:(kc + 1) * 128], ident[0:R, 0:R],
                   [xn.b, ident.b], [bk.b])
        for kc in range(KC):
            off = kc * TB
            bk = usedbanks[off // 1024]
            o = off % 1024
            ts("dve", hT[:, kc, 0:TB], bk[:].bitcast(BF)[:, o:o + TB], gp[:, path, kc:kc + 1],
               modT[:, sh_lo + kc, path:path + 1], ALU.mult, ALU.add, [bk.b, gp.b, modT.b], [hT.b])

    def blk_params(path, b):
        prompt = path == 0
        TB = 512 if prompt else TS
        tiles = [(i * 128, 128) for i in range(4)] if prompt else [(0, TS)]
        xsrc = x_d if prompt else xs_d
        row0 = b * 512 if prompt else 0
        slot = (b % 2) if prompt else (NB % 2)
        return prompt, TB, tiles, xsrc, row0, xts[slot]

    def load_x(path, b):
        prompt, TB, tiles, xsrc, row0, xt = blk_params(path, b)
        for i, (r0, R) in enumerate(tiles):
            dma("sp", xt[0:R, i, :], xsrc[row0 + r0:row0 + r0 + R, :], [], [xt.b])

    def load_w(stream, sname, g, shape_kc, ncols, kc0=0):
        w = wring.next()
        dma("sp", w[:, 0:shape_kc, 0:ncols], stream[g, :, kc0:kc0 + shape_kc, :], [], [w.b])
        return w

    def headnorm_fm(bk, TB, outs):
        sq = tb.next()
        act(sq[:, 0:TB], bk[:, 0:TB], AF.Square, [bk.b], [sq.b])

        def part2():
            nb = bring.next()
            mm(nb[:, 0:TB], bones[:], sq[:, 0:TB], True, True, [sq.b, bones.b], [nb.b])
            sd = tf.next()
            act(sd[:, 0:TB], nb[:, 0:TB], AF.Ln, [nb.b, epsT.b], [sd.b], bias=epsT[:, 0:1], scale=1.0)
            act(sd[:, 0:TB], sd[:, 0:TB], AF.Exp, [sd.b], [sd.b], scale=-0.5)
            for (oap, p0, p1, g, wb) in outs:
                stt(oap, bk[p0:p1, 0:TB], g[p0:p1, 0:1], sd[p0:p1, 0:TB], ALU.mult, ALU.mult, [bk.b, sd.b, g.b], wb)
        return part2

    def do_block(path, b, nextblk):
        prompt, TB, tiles, xsrc, row0, xt = blk_params(path, b)
        KV = KVp if prompt else KVs
        kvname = "KVp" if prompt else "KVs"
        kb_new = b if prompt else 2
        yk, yv, yy = (ko_d, vo_d, y_d) if prompt else (kso_d, vso_d, ys_d)
        if prompt and b == 0:
            load_x(path, b)
            frontA(xt, tiles)
            frontB(path, TB, tiles, g1p, 0)
        pend = None
        for hh in range(2 * NH):
            isq = hh < NH
            h = hh % NH
            if h % 4 == 0:
                w = load_w(Wq if isq else Wk, "W", h // 4, 8, 512)
            bk = bring.next()
            for kc in range(KC):
                mm(bk[:, 0:TB], w[:, kc, (h % 4) * 128:(h % 4 + 1) * 128], hT[:, kc, 0:TB], kc == 0, kc == KC - 1,
                   [w.b, hT.b], [bk.b])
            if isq:
                nxt = headnorm_fm(bk, TB, [(QT[:, h, 0, 0:TB], 0, 128, qg2, [QTb[h]])])
            else:
                nxt = headnorm_fm(bk, TB, [(KTs[:, h, 0:TB], 0, 128, kg2, [KTs.b])])
            if pend is not None:
                pend()
            pend = nxt
        pend()
        dma("pool", KV[:, kb_new, :, 0:TB].rearrange("h p c -> p h c"), KTs[:, :, 0:TB], [KTs.b],
            [DB((kvname, kb_new, "k"))])
        if nextblk is not None:
            load_x(*nextblk)
        for cg in range(2):
            w = load_w(Wk, "Wk", cg, 8, 512)
            for i, (r0, R) in enumerate(tiles):
                bk = bring.next()
                for kc in range(KC):
                    mm(bk[0:R, :], hT[:, kc, r0:r0 + R], w[:, kc, :], kc == 0, kc == KC - 1, [w.b, hT.b], [bk.b])
                sq = tf.next()
                act(sq[0:R, :], bk[0:R, :], AF.Square, [bk.b], [sq.b])
                sc.add("dve", lambda e, sq=sq, R=R: e.tensor_reduce(
                    out=ssk[0:R, 0:8], in_=sq[0:R, :].rearrange("p (g d) -> p g d", d=64), axis=AX.X, op=ALU.add),
                    [sq.b], [ssk.b])
                ts("dve", ssk[0:R, :], ssk[0:R, :], 1.0 / 64, EPS, ALU.mult, ALU.add, [ssk.b], [ssk.b])
                rsqrt_inplace(ssk[0:R, :], ssk.b)
                ko = tf.next()
                tt("dve", ko[0:R, :].rearrange("p (g d) -> p g d", d=64), bk[0:R, :].rearrange("p (g d) -> p g d", d=64),
                   ssk[0:R, 0:8].unsqueeze(2).to_broadcast([R, 8, 64]), ALU.mult, [bk.b, ssk.b], [ko.b])
                tt("dve", ko[0:R, :], ko[0:R, :], kgb[0:R, cg * 512:(cg + 1) * 512], ALU.mult, [ko.b, kgb.b], [ko.b])
                dma("pool", yk[row0 + r0:row0 + r0 + R, cg * 512:(cg + 1) * 512], ko[0:R, :], [ko.b], [])
        for cg in range(2):
            w = load_w(Wv, "Wv", cg, 8, 512)
            for i, (r0, R) in enumerate(tiles):
                bk = bring.next()
                for kc in range(KC):
                    mm(bk[0:R, :], hT[:, kc, r0:r0 + R], w[:, kc, :], kc == 0, kc == KC - 1, [w.b, hT.b], [bk.b])
                vo = tf.next()
                act(vo[0:R, :], bk[0:R, :], AF.Copy, [bk.b], [vo.b])
                act(Vs[0:R, 4 * cg:4 * cg + 4, i, :], bk[0:R, :].rearrange("p (h d) -> p h d", d=128), AF.Copy, [bk.b], [Vs.b])
                dma("pool", yv[row0 + r0:row0 + r0 + R, cg * 512:(cg + 1) * 512], vo[0:R, :], [vo.b], [])
        if prompt:
            dma("pool", KV[:, kb_new, :, 512:1024].rearrange("h p (t d) -> p h t d", d=128), Vs[:], [Vs.b],
                [DB((kvname, kb_new, "v"))])
        else:
            dma("pool", KV[:, kb_new, 0:TS, 512:640].rearrange("h p d -> p h d"), Vs[0:TS, :, 0, :], [Vs.b],
                [DB((kvname, kb_new, "v"))])
        ut_ = utail[path]
        for c in range(8):
            w = load_w(Wcv, "Wcv", c, 8, 384)
            bx = bring.next(); bgb = bring.next(); bgc = bring.next()
            for (bk, q) in ((bx, 0), (bgb, 1), (bgc, 2)):
                for kc in range(KC):
                    mm(bk[:, 0:TB], w[:, kc, q * 128:(q + 1) * 128], hT[:, kc, 0:TB], kc == 0, kc == KC - 1, [w.b, hT.b], [bk.b])
            xsf = tf.next()
            act(xsf[:, 0:TB], bx[:, 0:TB], AF.Copy, [bx.b], [xsf.b])
            u = utr.next()
            cp("pool", u[:, 0:2], ut_[:, c, :], [ut_.b], [u.b])
            tt("dve", u[:, 2:2 + TB], bgc[:, 0:TB], xsf[:, 0:TB], ALU.mult, [bgc.b, xsf.b, u.b], [u.b])
            cp("pool", ut_[:, c, :], u[:, TB:TB + 2], [u.b], [ut_.b])
            cvt = tf.next()
            ts("dve", cvt[:, 0:TB], u[:, 2:2 + TB], cw[:, c, 2:3], None, ALU.mult, None, [u.b, cw.b], [cvt.b])
            stt(cvt[:, 0:TB], u[:, 1:1 + TB], cw[:, c, 1:2], cvt[:, 0:TB], ALU.mult, ALU.add, [u.b, cw.b, cvt.b], [cvt.b])
            stt(cvt[:, 0:TB], u[:, 0:TB], cw[:, c, 0:1], cvt[:, 0:TB], ALU.mult, ALU.add, [u.b, cw.b, cvt.b], [cvt.b])
            tt("dve", zT[:, c, 0:TB], bgb[:, 0:TB], cvt[:, 0:TB], ALU.mult, [bgb.b, cvt.b], [zT.b, actT.b])
        nkb = (b + 1) if prompt else 3
        Sp = [(banks[0], banks[1]), (banks[2], banks[3])]
        O1, O2, LB, B7 = banks[4], banks[5], banks[6], banks[7]
        jobs = []
        for h in range(NH):
            hj = []
            for kb in range(nkb):
                diag = prompt and kb == b
                ntile = 1 if (not prompt and kb == 2) else 4
                for j in range(ntile):
                    KR = TS if (not prompt and kb == 2) else 128
                    hj.append((kb, j, KR, 128 * j if diag else 0, diag))
            for n, (kb, j, KR, q0, diag) in enumerate(hj):
                jobs.append((h, n, len(hj), kb, j, KR, q0, diag))
        G = len(jobs)
        njh = G // NH
        dA = max(1, min(6, njh - 2))
        dB = max(1, min(5, njh - 1))
        kvrec = {}
        ptof = {}
        st = {"spair": 0}
        deferred = []

        def next_pair():
            p = st["spair"] % 2
            st["spair"] += 1
            return p

        def emit_S(g):
            (h, n, nj, kb, j, KR, q0, diag) = jobs[g]
            if (h, kb) not in kvrec:
                kv = kvring.next()
                dma("sp", kv[:], KV[h, kb, :, :], [DB((kvname, kb, "k")), DB((kvname, kb, "v"))], [kv.b])
                kvrec[(h, kb)] = kv
            kv = kvrec[(h, kb)]
            NQ = TB - q0
            p = next_pair()
            sa, sb_ = Sp[p]
            mm(sa[0:KR, 0:NQ], kv[0:64, j * 128:j * 128 + KR], QT[0:64, h, 0, q0:TB], True, True, [kv.b, QTb[h]], [sa.b])
            mm(sb_[0:KR, 0:NQ], kv[64:128, j * 128:j * 128 + KR], QT[64:128, h, 0, q0:TB], True, True, [kv.b, QTb[h]], [sb_.b])
            pt = ptring.next()
            ptof[g] = pt
            act(pt[0:KR, :, q0:TB], pall[0:KR, p * 1024:(p + 1) * 1024].rearrange("p (m c) -> p m c", m=2)[:, :, 0:NQ],
                AF.Exp, [sa.b, sb_.b], [pt.b], scale=0.125)
            if diag:
                mset("dve", pt[64:128, :, q0:q0 + 64], 0.0, [pt.b])
            acc = accs[h % 2]
            ab = accD[h % 2]
            if n == 0:
                sc.add("dve", lambda e, a=acc[0:KR, q0:TB], p_=pt[0:KR, 0, q0:TB]: e.tensor_copy(out=a, in_=p_), [pt.b], [ab])
            else:
                sc.add("dve", lambda e, a=acc[0:KR, q0:TB], p_=pt[0:KR, 0, q0:TB]: e.tensor_tensor(out=a, in0=a, in1=p_, op=ALU.add),
                       [pt.b, ab], [ab])

        def epiB(h, ta, rb, sq):
            nb = B7
            mm(nb[:, 0:TB], ones128[:], sq[:, 0:TB], True, True, [sq.b, ones128.b], [nb.b])
            act(rb[:, 0:TB], nb[:, 0:TB], AF.Ln, [nb.b, epsT.b], [rb.b], bias=epsT[:, 0:1], scale=1.0)
            act(rb[:, 0:TB], rb[:, 0:TB], AF.Exp, [rb.b], [rb.b], scale=-0.5)
            stt(OT[:, h, 0:TB], ta[:, 0:TB], subg2[:, 0:1], rb[:, 0:TB], ALU.mult, ALU.mult, [ta.b, rb.b, subg2.b], [OTb[h], actT.b])

        def epiA(h):
            par = h % 2
            acc = accs[par]
            mm(B7[:, 0:TB], onesf[:], acc[:, 0:TB], True, True, [accD[par], onesf.b], [B7.b])
            ra = tf.next(); rb = tf.next(); ta = tf.next(); tb_ = tf.next()
            recip(ra[:, 0:TB], B7[:, 0:TB], [B7.b], [ra.b])
            act(rb[:, 0:TB], l2s[par][:, 0:TB], AF.Exp, [l2s[par].b], [rb.b], scale=-1.0)
            tt("dve", ta[:, 0:TB], o12s[par][:, 0, 0:TB], ra[:, 0:TB], ALU.mult, [o12s[par].b, ra.b], [ta.b])
            tt("dve", tb_[:, 0:TB], o12s[par][:, 1, 0:TB], rb[:, 0:TB], ALU.mult, [o12s[par].b, rb.b], [tb_.b])
            stt(ta[:, 0:TB], tb_[:, 0:TB], neglam[:, 0:1], ta[:, 0:TB], ALU.mult, ALU.add, [tb_.b, ta.b, neglam.b], [ta.b])
            sq = tb.next()
            tt("pool", sq[:, 0:TB], ta[:, 0:TB], ta[:, 0:TB], ALU.mult, [ta.b], [sq.b])
            deferred.append([dB, lambda: epiB(h, ta, rb, sq)])

        def emit_PV(g):
            (h, n, nj, kb, j, KR, q0, diag) = jobs[g]
            kv = kvrec[(h, kb)]
            pt = ptof.pop(g)
            first = n == 0
            last = n == nj - 1
            vt = kv[0:KR, 512 + j * 128:512 + (j + 1) * 128]
            mm(O1[:, q0:TB], vt, pt[0:KR, 0, q0:TB], first, last, [kv.b, pt.b], [O1.b])
            mm(O2[:, q0:TB], vt, pt[0:KR, 1, q0:TB], first, last, [kv.b, pt.b], [O2.b])
            mm(LB[:, q0:TB], ones[0:KR, :], pt[0:KR, 1, q0:TB], first, last, [ones.b, pt.b], [LB.b])
            if last:
                par = h % 2
                act(o12s[par][:, :, 0:TB], pall[:, 4 * 512:6 * 512].rearrange("p (m c) -> p m c", m=2)[:, :, 0:TB],
                    AF.Copy, [O1.b, O2.b], [o12s[par].b])
                act(l2s[par][:, 0:TB], LB[:, 0:TB], AF.Ln, [LB.b], [l2s[par].b])
                deferred.append([dA, lambda: epiA(h)])

        def tick():
            for d in deferred:
                d[0] -= 1
            while deferred and deferred[0][0] <= 0:
                deferred.pop(0)[1]()

        emit_S(0)
        if G > 1:
            emit_S(1)
        for g in range(G):
            if g + 2 < G:
                emit_S(g + 2)
            emit_PV(g)
            tick()
        while deferred:
            deferred.pop(0)[1]()
        for j in range(8):
            w = load_w(Wmg, "Wmg", j, 8, 512)
            bga = bring.next(); bgb = bring.next(); bya = bring.next(); byb = bring.next()
            for kc in range(KC):
                mm(bga[:, 0:TB], w[:, kc, 0:128], hT[:, kc, 0:TB], kc == 0, kc == KC - 1, [w.b, hT.b], [bga.b])
            for kc in range(KC):
                mm(bgb[:, 0:TB], w[:, kc, 128:256], hT[:, kc, 0:TB], kc == 0, kc == KC - 1, [w.b, hT.b], [bgb.b])
            for kc in range(KC):
                mm(bya[:, 0:TB], w[:, kc, 256:384], OT[:, kc, 0:TB], kc == 0, kc == KC - 1, [w.b, OTb[kc]], [bya.b])
            for kc in range(KC):
                mm(byb[:, 0:TB], w[:, kc, 384:512], zT[:, kc, 0:TB], kc == 0, kc == KC - 1, [w.b, zT.b], [byb.b])
            sga = tf.next(); sgb = tf.next()
            act(sga[:, 0:TB], bga[:, 0:TB], AF.Sigmoid, [bga.b], [sga.b])
            act(sgb[:, 0:TB], bgb[:, 0:TB], AF.Sigmoid, [bgb.b], [sgb.b])
            tt("dve", sga[:, 0:TB], bya[:, 0:TB], sga[:, 0:TB], ALU.mult, [bya.b, sga.b], [sga.b])
            tt("dve", sgb[:, 0:TB], byb[:, 0:TB], sgb[:, 0:TB], ALU.mult, [byb.b, sgb.b], [sgb.b])
            tt("pool", mT[:, j, 0:TB], sga[:, 0:TB], sgb[:, 0:TB], ALU.add, [sga.b, sgb.b], [mT.b, actT.b])
        for cg in range(2):
            w = load_w(Wo, "Wo", cg, 8, 512)
            for i, (r0, R) in enumerate(tiles):
                bk = bring.next()
                for kc in range(KC):
                    mm(bk[0:R, :], mT[:, kc, r0:r0 + R], w[:, kc, :], kc == 0, kc == KC - 1, [w.b, mT.b], [bk.b])
                t_ = tf.next()
                tt("dve", t_[0:R, :], bk[0:R, :], gateb[path][0:R, cg * 512:(cg + 1) * 512], ALU.mult,
                   [bk.b, gateb[path].b], [t_.b])
                tt("pool", xt[0:R, i, cg * 512:(cg + 1) * 512], xt[0:R, i, cg * 512:(cg + 1) * 512], t_[0:R, :], ALU.add,
                   [xt.b, t_.b], [xt.b])
        frontA(xt, tiles)
        frontB(path, TB, tiles, g2p, 24)
        if nextblk is not None:
            nprompt, nTB, ntiles, _, _, nxt_xt = blk_params(*nextblk)
            frontA(nxt_xt, ntiles)
        for f in range(NF):
            w = load_w(Wgu, "Wgu", f, 8, 256)
            bg = bring.next(); bu = bring.next()
            for kc in range(KC):
                mm(bg[:, 0:TB], w[:, kc, 0:128], hT[:, kc, 0:TB], kc == 0, kc == KC - 1, [w.b, hT.b], [bg.b])
            for kc in range(KC):
                mm(bu[:, 0:TB], w[:, kc, 128:256], hT[:, kc, 0:TB], kc == 0, kc == KC - 1, [w.b, hT.b], [bu.b])
            sg = tf.next()
            act(sg[:, 0:TB], bg[:, 0:TB], AF.Silu, [bg.b], [sg.b])
            tt("dve", actT[:, f, 0:TB], bu[:, 0:TB], sg[:, 0:TB], ALU.mult, [bu.b, sg.b], [actT.b] + ALIAS)
        if nextblk is not None:
            frontB(nextblk[0], nTB, ntiles, g1p, 0)
        for cg in range(2):
            ybk = [bring.next() for _ in tiles]
            for (k0, k1) in ((0, 8), (8, 16), (16, NF)):
                w = wring.next()
                dma("sp", w[:, 0:k1 - k0, :], Wd[cg, :, k0:k1, :], [], [w.b])
                for i, (r0, R) in enumerate(tiles):
                    for kc in range(k0, k1):
                        mm(ybk[i][0:R, :], actT[:, kc, r0:r0 + R], w[:, kc - k0, :], kc == 0, kc == NF - 1,
                           [w.b, actT.b], [ybk[i].b])
            for i, (r0, R) in enumerate(tiles):
                t_ = tf.next()
                tt("dve", t_[0:R, :], ybk[i][0:R, :], gateb[path][0:R, 1024 + cg * 512:1024 + (cg + 1) * 512], ALU.mult,
                   [ybk[i].b, gateb[path].b], [t_.b])
                tt("pool", xt[0:R, i, cg * 512:(cg + 1) * 512], xt[0:R, i, cg * 512:(cg + 1) * 512], t_[0:R, :], ALU.add,
                   [xt.b, t_.b], [xt.b])
        for i, (r0, R) in enumerate(tiles):
            dma("pool", yy[row0 + r0:row0 + r0 + R, :], xt[0:R, i, :], [xt.b], [])

    for b in range(NB):
        do_block(0, b, (0, b + 1) if b + 1 < NB else (1, 0))
    do_block(1, 0, None)
    for c in range(8):
        dma("pool", co_d.rearrange("t (c p) -> p c t", p=128)[:, c, :], utail[0][:, c, :], [utail[0].b], [],
            allow_slow_non_contiguous=True)
        dma("pool", cso_d.rearrange("t (c p) -> p c t", p=128)[:, c, :], utail[1][:, c, :], [utail[1].b], [],
            allow_slow_non_contiguous=True)
    sc.emit(nc)
    return nc


def _consts():
    c = np.zeros((128, 256), np.float32)
    c[:, 0:128] = np.eye(128, dtype=np.float32)
    for p in range(128):
        c[p, 128 + (p // 64) * 64:128 + (p // 64 + 1) * 64] = 1.0 / 64
    return c


def make_in_maps(inp, S, ncores):
    f = lambda a: np.ascontiguousarray(np.asarray(a, dtype=np.float32))
    fm8 = lambda v: f(np.asarray(v).reshape(8, 128).T)
    shared = {
        "w_ada": f(inp["w_ada"][0]), "b_ada": f(inp["b_ada"][0]).reshape(1, -1),
        "b_adaT": f(np.asarray(inp["b_ada"][0]).reshape(48, 128).T),
        "n1g": fm8(inp["norm1_g"][0]), "n2g": fm8(inp["norm2_g"][0]),
        "w_in": f(inp["w_in"][0]),
        "qg2": f(np.tile(np.asarray(inp["q_norm_g"][0]), 2).reshape(128, 1)),
        "kg2": f(np.tile(np.asarray(inp["k_norm_g"][0]), 2).reshape(128, 1)),
        "kgrow": f(np.tile(np.asarray(inp["k_norm_g"][0]), 16).reshape(1, D)),
        "lam4": f(np.concatenate([np.asarray(inp[k][0]) for k in ("lambda_q1", "lambda_k1", "lambda_q2", "lambda_k2")]).reshape(1, 256)),
        "subg": f(np.asarray(inp["sub_norm_g"][0]).reshape(128, 1)),
        "w_attn_out": f(inp["w_attn_out"][0]),
        "cwT": f(np.asarray(inp["conv_w"][0]).reshape(3, 8, 128).transpose(2, 1, 0)),
        "w_conv_out": f(inp["w_conv_out"][0]), "w_out": f(inp["w_out"][0]),
        "w_gate_up": f(inp["w_gate_up"][0]), "w_down": f(inp["w_down"][0]),
        "consts": _consts(),
    }
    maps = []
    for b in range(ncores):
        m = dict(shared)
        m["x"] = f(inp["x_prompt"][b]); m["xs"] = f(inp["x_sample"][b])
        m["ck"] = f(np.asarray(inp["cache_k"][0, b]).reshape(PAST, D))
        m["cv"] = f(np.asarray(inp["cache_v"][0, b]).reshape(PAST, D))
        m["sconvT"] = f(np.asarray(inp["state_conv"][0, b]).reshape(2, 8, 128).transpose(2, 1, 0))
        c2 = np.stack([np.asarray(inp["c_prompt"][b]), np.asarray(inp["c_sample"][b])])
        m["cT"] = f(c2.reshape(2, 8, 128).transpose(2, 1, 0))
        maps.append(m)
    return maps


_cache = {}


def run(inp, S, ncores):
    if S not in _cache:
        _cache[S] = build(S)
    nc = _cache[S]
    maps = make_in_maps(inp, S, ncores)
    res = run_bass_kernel_spmd(nc, maps, core_ids=list(range(ncores)))
    R = res.results
    st = lambda k: np.stack([np.asarray(R[b][k], dtype=np.float32) for b in range(ncores)])
    y = st("y"); ys = st("ys")
    kp = st("ko").reshape(1, ncores, S, NH, 2, 64); vp = st("vo").reshape(1, ncores, S, NH, 128)
    cp_ = st("co").reshape(1, ncores, 2, D)
    ks = st("kso").reshape(1, ncores, TS, NH, 2, 64); vs = st("vso").reshape(1, ncores, TS, NH, 128)
    cs = st("cso").reshape(1, ncores, 2, D)
    return (y, ys, kp, vp, cp_, ks, vs, cs)


def kernel(**inputs):
    S = int(np.asarray(inputs["x_prompt"]).shape[1])
    return run(inputs, S, 8)
```

```python
import numpy as np
import concourse.bass as bass
import concourse.mybir as mybir
from concourse.bass_utils import run_bass_kernel_spmd

F32 = mybir.dt.float32
BF = mybir.dt.bfloat16
AF = mybir.ActivationFunctionType
ALU = mybir.AluOpType
AX = mybir.AxisListType

D = 1024
KC = 8
NH = 8
DFF = 2816
NF = 22
EPS = 1e-6
LAM_INIT = 0.2
PAST = 1024
TS = 32
SAME_ENG_WINDOW = 8
NCHAN = 16


class Buf:
    __slots__ = ("name", "w", "r")

    def __init__(self, name=""):
        self.name = name
        self.w = None
        self.r = []


class Op:
    __slots__ = ("eng", "fn", "deps", "is_dma", "needs_inc", "sem", "val", "pos", "noself")

    def __init__(self, eng, fn, is_dma):
        self.eng = eng
        self.fn = fn
        self.deps = []
        self.is_dma = is_dma
        self.needs_inc = False
        self.sem = None
        self.val = 0
        self.pos = 0
        self.noself = False


class Sched:
    ENGS = ("pe", "act", "dve", "pool", "sp")

    def __init__(self):
        self.ops = []
        self.per_eng = {e: [] for e in self.ENGS}
        self.dma_hist = {e: [] for e in self.ENGS}
        self.barrier_deps = {e: [] for e in self.ENGS}

    def add(self, eng, fn, R=(), W=(), dma=False, noself=False):
        op = Op(eng, fn, dma)
        op.noself = noself
        deps = set()
        for b in R:
            if b.w is not None:
                deps.add(b.w)
        for b in W:
            if b.w is not None:
                deps.add(b.w)
            for r in b.r:
                deps.add(r)
        for d in self.barrier_deps[eng]:
            deps.add(d)
        self.barrier_deps[eng] = []
        if dma:
            h = self.dma_hist[eng]
            if len(h) >= NCHAN:
                deps.add(h[-NCHAN])
            h.append(op)
        deps.discard(op)
        op.deps = list(deps)
        for b in R:
            b.r.append(op)
        for b in W:
            b.w = op
            b.r = []
        op.pos = len(self.per_eng[eng])
        self.per_eng[eng].append(op)
        self.ops.append(op)
        return op

    def barrier(self):
        last = []
        for e in self.ENGS:
            ops = self.per_eng[e]
            nd = [o for o in ops[-200:] if not o.is_dma]
            if nd:
                last.append(nd[-1])
            last.extend(self.dma_hist[e][-NCHAN:])
        for e in self.ENGS:
            self.barrier_deps[e] = list(last)

    def emit(self, nc):
        for op in self.ops:
            real = []
            for d in op.deps:
                if not d.is_dma and d.eng == op.eng:
                    if op.eng == "pe" and not op.is_dma:
                        continue
                    if op.pos - d.pos >= SAME_ENG_WINDOW or op.noself:
                        continue
                real.append(d)
                d.needs_inc = True
            op.deps = real
        from contextlib import ExitStack
        with ExitStack() as es:
            csem = {e: es.enter_context(nc.semaphore("c_" + e)) for e in ("pe", "act", "dve", "pool")}
            dsem = {}
            for e in ("sp", "pool", "act"):
                dsem[e] = [es.enter_context(nc.semaphore("d_%s%d" % (e, i))) for i in range(NCHAN)]
            cnt = {e: 0 for e in csem}
            dcnt = {e: [0] * NCHAN for e in dsem}
            dn = {e: 0 for e in dsem}
            for op in self.ops:
                if op.is_dma:
                    i = dn[op.eng] % NCHAN
                    dn[op.eng] += 1
                    dcnt[op.eng][i] += 16
                    op.sem = dsem[op.eng][i]
                    op.val = dcnt[op.eng][i]
                elif op.needs_inc:
                    cnt[op.eng] += 1
                    op.sem = csem[op.eng]
                    op.val = cnt[op.eng]
            block = es.enter_context(nc.Block())

            def run_stream(ename, e):
                waited = {}
                for op in self.per_eng[ename]:
                    need = {}
                    for d in op.deps:
                        k = id(d.sem)
                        if waited.get(k, 0) >= d.val:
                            continue
                        if k not in need or need[k][1] < d.val:
                            need[k] = (d.sem, d.val)
                    for k, (s, v) in need.items():
                        e.wait_ge(s, v)
                        waited[k] = v
                    ins = op.fn(e)
                    if op.is_dma:
                        ins.then_inc(op.sem, 16)
                    elif op.needs_inc:
                        ins.then_inc(op.sem, 1)
                if ename in dsem:
                    for i in range(NCHAN):
                        if dcnt[ename][i] > 0:
                            e.wait_ge(dsem[ename][i], dcnt[ename][i])

            @block.sync
            def _(e):
                run_stream("sp", e)

            @block.tensor
            def _(e):
                run_stream("pe", e)

            @block.vector
            def _(e):
                run_stream("dve", e)

            @block.scalar
            def _(e):
                run_stream("act", e)

            @block.gpsimd
            def _(e):
                run_stream("pool", e)


class T:
    def __init__(self, nc, name, shape, dtype, psum=False, es=None):
        name = "t_" + name
        if psum:
            self.t = nc.alloc_psum_tensor(name, list(shape), dtype)
        elif es is not None:
            self.t = es.enter_context(nc.sbuf_tensor(name, list(shape), dtype))
        else:
            self.t = nc.alloc_sbuf_tensor(name, list(shape), dtype)
        self.b = Buf(name)

    def __getitem__(self, k):
        return self.t[k]


class Ring:
    def __init__(self, items):
        self.items = items
        self.i = 0

    def next(self):
        it = self.items[self.i % len(self.items)]
        self.i += 1
        return it


def build(S):
    assert S % 512 == 0
    NB = S // 512
    nc = bass.Bass("TRN2", target_bir_lowering=False)
    sc = Sched()

    def din(name, shape):
        return nc.dram_tensor(name, list(shape), F32, kind="ExternalInput").ap()

    def dout(name, shape):
        return nc.dram_tensor(name, list(shape), F32, kind="ExternalOutput").ap()

    def dscr(name, shape):
        return nc.dram_tensor(name, list(shape), BF, kind="Internal").ap()

    x_d = din("x", [S, D]); xs_d = din("xs", [TS, D])
    ck_d = din("ck", [PAST, D]); cv_d = din("cv", [PAST, D])
    sconvT_d = din("sconvT", [128, 8, 2]); cT_d = din("cT", [128, 8, 2])
    wada_d = din("w_ada", [D, 6 * D]); bada_d = din("b_ada", [1, 6 * D]); badaT_d = din("b_adaT", [128, 48])
    n1g_d = din("n1g", [128, 8]); n2g_d = din("n2g", [128, 8])
    win_d = din("w_in", [D, 8 * D])
    qg2_d = din("qg2", [128, 1]); kg2_d = din("kg2", [128, 1]); kgrow_d = din("kgrow", [1, D])
    lam4_d = din("lam4", [1, 256]); subg_d = din("subg", [128, 1])
    wao_d = din("w_attn_out", [D, D]); cwT_d = din("cwT", [128, 8, 3])
    wco_d = din("w_conv_out", [D, D]); wo_d = din("w_out", [D, D])
    wgu_d = din("w_gate_up", [D, 2 * DFF]); wd_d = din("w_down", [DFF, D])
    consts_d = din("consts", [128, 256])

    y_d = dout("y", [S, D]); ys_d = dout("ys", [TS, D])
    ko_d = dout("ko", [S, D]); vo_d = dout("vo", [S, D]); co_d = dout("co", [2, D])
    kso_d = dout("kso", [TS, D]); vso_d = dout("vso", [TS, D]); cso_d = dout("cso", [2, D])

    Wq = dscr("Wq", [2, 128, 8, 512]); Wk = dscr("Wk", [2, 128, 8, 512]); Wv = dscr("Wv", [2, 128, 8, 512])
    Wcv = dscr("Wcv", [8, 128, 8, 384]); Wmg = dscr("Wmg", [8, 128, 8, 512]); Wo = dscr("Wo", [2, 128, 8, 512])
    Wgu = dscr("Wgu", [NF, 128, 8, 256]); Wd = dscr("Wd", [2, 128, NF, 512])
    KVp = dscr("KVp", [NH, NB, 128, 1024]); KVs = dscr("KVs", [NH, 3, 128, 1024])
    dbuf = {}

    def DB(key):
        if key not in dbuf:
            dbuf[key] = Buf(str(key))
        return dbuf[key]

    def P(name, shape, dt=F32):
        return T(nc, name, shape, dt)

    cst = P("cst", [128, 256]); ident = P("ident", [128, 128], BF); bones = P("bones", [128, 128], BF)
    ones = P("ones", [128, 128], BF); ones128 = P("ones128", [128, 128], BF)
    identf = cst
    epsT = P("epsT", [128, 1])
    modT = P("modT", [128, 48, 2]); g1p = P("g1p", [128, 2, 8]); g2p = P("g2p", [128, 2, 8])
    n1g = P("n1g", [128, 8]); n2g = P("n2g", [128, 8])
    qg2 = P("qg2", [128, 1]); kg2 = P("kg2", [128, 1]); subg = P("subg", [128, 1]); subg2 = P("subg2", [128, 1])
    neglam = P("neglam", [128, 1]); lamb = P("lamb", [128, 256]); lprod = P("lprod", [128, 128]); lsum = P("lsum", [128, 2])
    lexp = P("lexp", [128, 2]); lam1 = P("lam1", [128, 1])
    cw = P("cw", [128, 8, 3]); utail = [P("utail0", [128, 8, 2]), P("utail1", [128, 8, 2])]
    gateb = [P("gateb0", [128, 2048]), P("gateb1", [128, 2048])]
    kgb = P("kgb", [128, D])
    badaT = P("badaT", [128, 48]); cT = P("cT", [128, 8, 2]); scT = P("scT", [128, 8, 2])
    ss = P("ss", [128, 4]); rs = P("rs", [128, 4]); ssk = P("ssk", [128, 8])

    pall = nc.alloc_psum_tensor("t_pall", [128, 4096], F32)

    class BankV:
        def __init__(self, i):
            self.ap = pall[:, i * 512:(i + 1) * 512]
            self.b = Buf("bank%d" % i)

        def __getitem__(self, k):
            return self.ap[k]

    banks = [BankV(i) for i in range(8)]
    bring = Ring(banks)
    onesf = P("onesf", [128, 128])

    def dma(q, out, in_, R, W, **kw):
        return sc.add(q, lambda e: e.dma_start(out=out, in_=in_, **kw), R, W, dma=True)

    def mm(out, lhsT, rhs, start, stop, R, W):
        return sc.add("pe", lambda e: e.matmul(out, lhsT=lhsT, rhs=rhs, start=start, stop=stop), R, W)

    def tr(out, in_, idn, R, W):
        return sc.add("pe", lambda e: e.transpose(out, in_, idn), R, W)

    def act(out, in_, func, R, W, **kw):
        return sc.add("act", lambda e: e.activation(out=out, in_=in_, func=func, **kw), R, W)

    def ts(eng, out, in0, s1, s2, op0, op1, R, W):
        if op1 is None:
            return sc.add(eng, lambda e: e.tensor_scalar(out=out, in0=in0, scalar1=s1, scalar2=None, op0=op0), R, W)
        return sc.add(eng, lambda e: e.tensor_scalar(out=out, in0=in0, scalar1=s1, scalar2=s2, op0=op0, op1=op1), R, W)

    def tt(eng, out, in0, in1, op, R, W):
        return sc.add(eng, lambda e: e.tensor_tensor(out=out, in0=in0, in1=in1, op=op), R, W)

    def stt(out, in0, scalar, in1, op0, op1, R, W):
        return sc.add("dve", lambda e: e.scalar_tensor_tensor(out=out, in0=in0, scalar=scalar, in1=in1, op0=op0, op1=op1), R, W)

    def cp(eng, out, in_, R, W):
        return sc.add(eng, lambda e: e.tensor_copy(out=out, in_=in_), R, W)

    def recip(out, in_, R, W):
        act(out, in_, AF.Ln, R, W)
        return act(out, out, AF.Exp, W, W, scale=-1.0)

    def mset(eng, ap, val, W):
        return sc.add(eng, lambda e: e.memset(ap, val), (), W)

    def rsqrt_inplace(ap, b):
        act(ap, ap, AF.Ln, [b], [b])
        act(ap, ap, AF.Exp, [b], [b], scale=-0.5)

    from contextlib import ExitStack
    dma("sp", cst[:], consts_d[:, :], [], [cst.b])
    cp("dve", ident[:], cst[:, 0:128], [cst.b], [ident.b])
    cp("dve", bones[:], cst[:, 128:256], [cst.b], [bones.b])
    mset("dve", ones[:], 1.0, [ones.b])
    mset("dve", ones128[:], 1.0 / 128, [ones128.b])
    mset("dve", onesf[:], 1.0, [onesf.b])
    mset("dve", epsT[:], EPS, [epsT.b])
    mset("dve", utail[0][:], 0.0, [utail[0].b])
    for (t_, d_) in ((n1g, n1g_d), (n2g, n2g_d), (qg2, qg2_d), (kg2, kg2_d), (subg, subg_d), (badaT, badaT_d)):
        dma("sp", t_[:], d_[:, :], [], [t_.b])
    dma("sp", cw[:], cwT_d[:, :, :], [], [cw.b])
    dma("sp", cT[:], cT_d[:, :, :], [], [cT.b])
    dma("sp", utail[1][:], sconvT_d[:, :, :], [], [utail[1].b])
    dma("sp", lamb[:], lam4_d[0].partition_broadcast(128), [], [lamb.b])
    dma("sp", kgb[:], kgrow_d[0].partition_broadcast(128), [], [kgb.b])
    tt("dve", lprod[:, 0:64], lamb[:, 0:64], lamb[:, 64:128], ALU.mult, [lamb.b], [lprod.b])
    tt("dve", lprod[:, 64:128], lamb[:, 128:192], lamb[:, 192:256], ALU.mult, [lamb.b, lprod.b], [lprod.b])
    sc.add("dve", lambda e: e.tensor_reduce(out=lsum[:, 0:2], in_=lprod[:].rearrange("p (g d) -> p g d", d=64),
                                            axis=AX.X, op=ALU.add), [lprod.b], [lsum.b])
    act(lexp[:], lsum[:], AF.Exp, [lsum.b], [lexp.b])
    tt("dve", lam1[:], lexp[:, 1:2], lexp[:, 0:1], ALU.subtract, [lexp.b], [lam1.b])
    ts("dve", neglam[:], lam1[:], -LAM_INIT, None, ALU.add, None, [lam1.b], [neglam.b])
    ts("dve", subg2[:], subg[:], 1.0 - LAM_INIT, None, ALU.mult, None, [subg.b], [subg2.b])
    act(scT[:], cT[:], AF.Silu, [cT.b], [scT.b])

    es = ExitStack()

    def PT_(name, shape, dt=F32):
        return T(nc, name, shape, dt, es=es)

    wslots = Ring([PT_("wadas%d" % i, [128, 8, 512]) for i in range(2)])
    scb = [PT_("scb%d" % r, [128, 8, 128]) for r in range(2)]
    badab = PT_("badab", [128, 2048])
    for r in range(2):
        cp("dve", scb[r][:], scT[:, :, r:r + 1].to_broadcast([128, 8, 128]), [scT.b], [scb[r].b])
    dma("sp", badab[:, 0:1024], bada_d[0, 2048:3072].partition_broadcast(128), [], [badab.b])
    dma("sp", badab[:, 1024:2048], bada_d[0, 5120:6144].partition_broadcast(128), [badab.b], [badab.b])
    modbank = bring.next()
    wada_v = wada_d.rearrange("(k p) c -> p k c", p=128)

    def mod_block(blk):
        wsl = wslots.next()
        dma("sp", wsl[:], wada_v[:, :, blk * 512:(blk + 1) * 512], [], [wsl.b])
        for m4 in range(4):
            m = blk * 4 + m4
            for kc in range(KC):
                mm(modbank[:, 2 * m:2 * m + 2], wsl[:, kc, m4 * 128:(m4 + 1) * 128], scT[:, kc, :],
                   kc == 0, kc == KC - 1, [wsl.b, scT.b], [modbank.b])
        if blk in (4, 5, 10, 11):
            gi = 0 if blk < 6 else 1
            cgi = blk % 2
            for r in range(2):
                bk = bring.next()
                if bk is modbank:
                    bk = bring.next()
                for kc in range(KC):
                    mm(bk[:, :], scb[r][:, kc, :], wsl[:, kc, :], kc == 0, kc == KC - 1, [wsl.b, scb[r].b], [bk.b])
                off = gi * 1024 + cgi * 512
                tt("dve", gateb[r][:, off:off + 512], bk[:, :], badab[:, off:off + 512], ALU.add,
                   [bk.b, badab.b], [gateb[r].b])

    prep_jobs = []
    stf = Ring([PT_("stf%d" % i, [128, 2048]) for i in range(6)])
    stb = Ring([PT_("stb%d" % i, [128, 2048], BF) for i in range(8)])
    casters = Ring(["dve", "act", "pool", "dve", "act", "dve"])

    def prep_piece(src, kc, c0, c1, mapping):
        f = stf.next(); b = stb.next()
        dma("sp", f[:, 0:c1 - c0], src[kc * 128:(kc + 1) * 128, c0:c1], [], [f.b])
        ce = casters.next()
        if ce == "act":
            act(b[:, 0:c1 - c0], f[:, 0:c1 - c0], AF.Copy, [f.b], [b.b])
        else:
            cp(ce, b[:, 0:c1 - c0], f[:, 0:c1 - c0], [f.b], [b.b])
        for (dst, srcfn) in mapping(kc, c0, c1):
            dma("act", dst, srcfn(b, c0), [b.b], [])

    def prep(src, K, N, mapping):
        nkc = K // 128
        for kc in range(nkc):
            for c0 in range(0, N, 2048):
                c1 = min(N, c0 + 2048)
                prep_jobs.append((src, kc, c0, c1, mapping))

    def map_simple(stream, sname, gw, col_lo, col_hi, src_off=0):
        def f(kc, c0, c1):
            out = []
            a = max(c0, col_lo); bnd = min(c1, col_hi)
            c = a
            while c < bnd:
                g = (c - col_lo) // gw
                e_ = min(bnd, col_lo + (g + 1) * gw)
                o = c - col_lo - g * gw
                out.append((stream[g, :, kc, o:o + (e_ - c)], lambda b, c0, c=c, e_=e_: b[:, c - c0:e_ - c0]))
                c = e_
            return out
        return f

    def map_chunks(specs):
        def f(kc, c0, c1):
            out = []
            for (col_lo, nch, stream, off) in specs:
                i0 = max(0, -(-(c0 - col_lo) // 128))
                i1 = min(nch, (c1 - col_lo) // 128)
                if i1 <= i0:
                    continue
                a0 = col_lo + i0 * 128; a1 = col_lo + i1 * 128
                out.append((stream[i0:i1, :, kc, off:off + 128].rearrange("g p c -> p g c"),
                            lambda b, c0, a0=a0, a1=a1: b[:, a0 - c0:a1 - c0].rearrange("p (g c) -> p g c", c=128)))
            return out
        return f

    def map_multi(fs):
        def f(kc, c0, c1):
            out = []
            for g in fs:
                out.extend(g(kc, c0, c1))
            return out
        return f

    prep(win_d, D, 8 * D, map_multi([
        map_simple(Wq, "Wq", 512, 0, 1024), map_simple(Wk, "Wk", 512, 1024, 2048), map_simple(Wv, "Wv", 512, 2048, 3072),
        map_chunks([(3072, 8, Wcv, 0), (4096, 8, Wcv, 128), (5120, 8, Wcv, 256),
                    (6144, 8, Wmg, 0), (7168, 8, Wmg, 128)])]))
    prep(wao_d, D, D, map_chunks([(0, 8, Wmg, 256)]))
    prep(wco_d, D, D, map_chunks([(0, 8, Wmg, 384)]))
    prep(wo_d, D, D, map_simple(Wo, "Wo", 512, 0, 1024))
    prep(wgu_d, D, 2 * DFF, map_chunks([(0, NF, Wgu, 0), (DFF, NF, Wgu, 128)]))
    prep(wd_d, DFF, D, map_simple(Wd, "Wd", 512, 0, 1024))
    nj = len(prep_jobs)
    per = max(1, nj // 12)
    mb = 0
    for ji, job in enumerate(prep_jobs):
        if ji % per == 0 and mb < 12:
            mod_block(mb); mb += 1
        prep_piece(*job)
    while mb < 12:
        mod_block(mb); mb += 1
    tt("dve", modT[:], modbank[:, 0:96].rearrange("p (m r) -> p m r", r=2),
       badaT[:].unsqueeze(2).to_broadcast([128, 48, 2]), ALU.add, [modbank.b, badaT.b], [modT.b])
    for r in range(2):
        stt(g1p[:, r, :], modT[:, 8:16, r], 1.0, n1g[:], ALU.add, ALU.mult, [modT.b, n1g.b], [g1p.b])
        stt(g2p[:, r, :], modT[:, 32:40, r], 1.0, n2g[:], ALU.add, ALU.mult, [modT.b, n2g.b], [g2p.b])


    ckf = Ring([PT_("ckf%d" % i, [128, D]) for i in range(2)])
    ckb = Ring([PT_("ckb%d" % i, [128, D], BF) for i in range(2)])
    kts_s = PT_("kts_s", [128, NH, 512], BF)
    vs_s = PT_("vs_s", [128, NH, 4, 128], BF)
    for kb in range(2):
        for j in range(4):
            t0 = (kb * 4 + j) * 128
            f = ckf.next(); b = ckb.next()
            dma("sp", f[:], ck_d[t0:t0 + 128, :], [], [f.b])
            cp("dve", b[:], f[:], [f.b], [b.b])
            bk = bring.next()
            bkb = bk[:].bitcast(BF)
            for h in range(NH):
                tr(bkb[:, h * 128:(h + 1) * 128], b[:, h * 128:(h + 1) * 128], ident[:], [b.b, ident.b], [bk.b])
            cp("dve", kts_s[:, :, j * 128:(j + 1) * 128], bkb[:, :].rearrange("p (h c) -> p h c", c=128), [bk.b], [kts_s.b])
            f = ckf.next()
            dma("sp", f[:], cv_d[t0:t0 + 128, :], [], [f.b])
            cp("pool", vs_s[:, :, j, :], f[:].rearrange("p (h d) -> p h d", d=128), [f.b], [vs_s.b])
        dma("pool", KVs[:, kb, :, 0:512].rearrange("h p c -> p h c"), kts_s[:], [kts_s.b], [DB(("KVs", kb, "k"))])
        dma("pool", KVs[:, kb, :, 512:1024].rearrange("h p (t d) -> p h t d", d=128), vs_s[:], [vs_s.b], [DB(("KVs", kb, "v"))])

    sc.barrier()
    es.close()

    xts = [P("xt0", [128, 4, D]), P("xt1", [128, 4, D])]; xn = P("xn", [128, 4, D], BF); sqj = xn
    hT = P("hT", [128, 8, 512], BF)
    QT = P("QT", [128, NH, 1, 512], BF); QTb = [Buf("QT%d" % h) for h in range(NH)]
    KTs = P("KTs", [128, NH, 512], BF); Vs = P("Vs", [128, NH, 4, 128], BF)
    big = P("big", [128, 3 * 8 * 512], BF)

    class View:
        def __init__(self, ap, name):
            self.ap = ap
            self.b = Buf(name)

        def __getitem__(self, k):
            return self.ap[k]

    zT = View(big[:, 0:4096].rearrange("p (c t) -> p c t", t=512), "zT")
    OT = View(big[:, 4096:8192].rearrange("p (c t) -> p c t", t=512), "OT"); OTb = [Buf("OT%d" % h) for h in range(NH)]
    mT = View(big[:, 8192:12288].rearrange("p (c t) -> p c t", t=512), "mT")
    actT = View(big[:, 0:NF * 512].rearrange("p (c t) -> p c t", t=512), "actT")
    ALIAS = [zT.b, mT.b] + OTb
    wring = Ring([P("wr%d" % i, [128, 8, 512], BF) for i in range(4)])
    kvring = Ring([P("kv%d" % i, [128, 1024], BF) for i in range(4)])
    ptring = Ring([P("pt%d" % i, [128, 2, 512], BF) for i in range(5)])
    accs = [P("acc%d" % i, [128, 512]) for i in range(2)]
    o12s = [P("o12s%d" % i, [128, 2, 512]) for i in range(2)]
    l2s = [P("l2s%d" % i, [128, 512]) for i in range(2)]
    accD = [Buf("accD%d" % i) for i in range(2)]
    accPb = [Buf("accP%d" % i) for i in range(2)]
    CS = 352
    tf = Ring([P("tf%d" % i, [128, 512]) for i in range(6)])
    tb = Ring([P("tb%d" % i, [128, 512], BF) for i in range(3)])
    utr = Ring([P("ut%d" % i, [128, 514]) for i in range(2)])
    print("sbuf bytes remaining:", nc.sbuf_bytes_remaining)
    mset("pool", QT[:], 0.0, [QT.b] + QTb)

    def frontA(xt, tiles):
        for i, (r0, R) in enumerate(tiles):
            act(sqj[0:R, i, :], xt[0:R, i, :], AF.Square, [xt.b], [xn.b, ss.b], accum_out=ss[0:R, i:i + 1])
            ts("dve", rs[0:R, i:i + 1], ss[0:R, i:i + 1], 1.0 / D, EPS, ALU.mult, ALU.add, [ss.b], [rs.b])
        rsqrt_inplace(rs[:, 0:len(tiles)] if tiles[0][1] == 128 else rs[0:tiles[0][1], 0:1], rs.b)
        for i, (r0, R) in enumerate(tiles):
            ts("dve", xn[0:R, i, :], xt[0:R, i, :], rs[0:R, i:i + 1], None, ALU.mult, None, [xt.b, rs.b], [xn.b])

    def frontB(path, TB, tiles, gp, sh_lo):
        usedbanks = {}
        for i, (r0, R) in enumerate(tiles):
            for kc in range(KC):
                off = kc * TB + i * 128
                bi = off // 1024
                if bi not in usedbanks:
                    usedbanks[bi] = bring.next()
                bk = usedbanks[bi]
                o = off % 1024
                tr(bk[:].bitcast(BF)[:, o:o + R], xn[0:R, i, kc * 128:(kc + 1) * 128], ident[0:R, 0:R],
                   [xn.b, ident.b], [bk.b])
        for kc in range(KC):
            off = kc * TB
            bk = usedbanks[off // 1024]
            o = off % 1024
            ts("dve", hT[:, kc, 0:TB], bk[:].bitcast(BF)[:, o:o + TB], gp[:, path, kc:kc + 1],
               modT[:, sh_lo + kc, path:path + 1], ALU.mult, ALU.add, [bk.b, gp.b, modT.b], [hT.b])

    def blk_params(path, b):
        prompt = path == 0
        TB = 512 if prompt else TS
        tiles = [(i * 128, 128) for i in range(4)] if prompt else [(0, TS)]
        xsrc = x_d if prompt else xs_d
        row0 = b * 512 if prompt else 0
        slot = (b % 2) if prompt else (NB % 2)
        return prompt, TB, tiles, xsrc, row0, xts[slot]

    def load_x(path, b):
        prompt, TB, tiles, xsrc, row0, xt = blk_params(path, b)
        for i, (r0, R) in enumerate(tiles):
            dma("sp", xt[0:R, i, :], xsrc[row0 + r0:row0 + r0 + R, :], [], [xt.b])

    def load_w(stream, sname, g, shape_kc, ncols, kc0=0):
        w = wring.next()
        dma("sp", w[:, 0:shape_kc, 0:ncols], stream[g, :, kc0:kc0 + shape_kc, :], [], [w.b])
        return w

    def headnorm_fm(bk, TB, outs):
        sq = tb.next()
        act(sq[:, 0:TB], bk[:, 0:TB], AF.Square, [bk.b], [sq.b])

        def part2():
            nb = bring.next()
            mm(nb[:, 0:TB], bones[:], sq[:, 0:TB], True, True, [sq.b, bones.b], [nb.b])
            sd = tf.next()
            act(sd[:, 0:TB], nb[:, 0:TB], AF.Ln, [nb.b, epsT.b], [sd.b], bias=epsT[:, 0:1], scale=1.0)
            act(sd[:, 0:TB], sd[:, 0:TB], AF.Exp, [sd.b], [sd.b], scale=-0.5)
            for (oap, p0, p1, g, wb) in outs:
                stt(oap, bk[p0:p1, 0:TB], g[p0:p1, 0:1], sd[p0:p1, 0:TB], ALU.mult, ALU.mult, [bk.b, sd.b, g.b], wb)
        return part2

    def do_block(path, b, nextblk):
        prompt, TB, tiles, xsrc, row0, xt = blk_params(path, b)
        KV = KVp if prompt else KVs
        kvname = "KVp" if prompt else "KVs"
        kb_new = b if prompt else 2
        yk, yv, yy = (ko_d, vo_d, y_d) if prompt else (kso_d, vso_d, ys_d)
        if prompt and b == 0:
            load_x(path, b)
            frontA(xt, tiles)
            frontB(path, TB, tiles, g1p, 0)
        pend = None
        for hh in range(2 * NH):
            isq = hh < NH
            h = hh % NH
            if h % 4 == 0:
                w = load_w(Wq if isq else Wk, "W", h // 4, 8, 512)
            bk = bring.next()
            for kc in range(KC):
                mm(bk[:, 0:TB], w[:, kc, (h % 4) * 128:(h % 4 + 1) * 128], hT[:, kc, 0:TB], kc == 0, kc == KC - 1,
                   [w.b, hT.b], [bk.b])
            if isq:
                nxt = headnorm_fm(bk, TB, [(QT[:, h, 0, 0:TB], 0, 128, qg2, [QTb[h]])])
            else:
                nxt = headnorm_fm(bk, TB, [(KTs[:, h, 0:TB], 0, 128, kg2, [KTs.b])])
            if pend is not None:
                pend()
            pend = nxt
        pend()
        dma("pool", KV[:, kb_new, :, 0:TB].rearrange("h p c -> p h c"), KTs[:, :, 0:TB], [KTs.b],
            [DB((kvname, kb_new, "k"))])
        if nextblk is not None:
            load_x(*nextblk)
        for cg in range(2):
            w = load_w(Wk, "Wk", cg, 8, 512)
            for i, (r0, R) in enumerate(tiles):
                bk = bring.next()
                for kc in range(KC):
                    mm(bk[0:R, :], hT[:, kc, r0:r0 + R], w[:, kc, :], kc == 0, kc == KC - 1, [w.b, hT.b], [bk.b])
                sq = tf.next()
                act(sq[0:R, :], bk[0:R, :], AF.Square, [bk.b], [sq.b])
                sc.add("dve", lambda e, sq=sq, R=R: e.tensor_reduce(
                    out=ssk[0:R, 0:8], in_=sq[0:R, :].rearrange("p (g d) -> p g d", d=64), axis=AX.X, op=ALU.add),
                    [sq.b], [ssk.b])
                ts("dve", ssk[0:R, :], ssk[0:R, :], 1.0 / 64, EPS, ALU.mult, ALU.add, [ssk.b], [ssk.b])
                rsqrt_inplace(ssk[0:R, :], ssk.b)
                ko = tf.next()
                tt("dve", ko[0:R, :].rearrange("p (g d) -> p g d", d=64), bk[0:R, :].rearrange("p (g d) -> p g d", d=64),
                   ssk[0:R, 0:8].unsqueeze(2).to_broadcast([R, 8, 64]), ALU.mult, [bk.b, ssk.b], [ko.b])
                tt("dve", ko[0:R, :], ko[0:R, :], kgb[0:R, cg * 512:(cg + 1) * 512], ALU.mult, [ko.b, kgb.b], [ko.b])
                dma("pool", yk[row0 + r0:row0 + r0 + R, cg * 512:(cg + 1) * 512], ko[0:R, :], [ko.b], [])
        for cg in range(2):
            w = load_w(Wv, "Wv", cg, 8, 512)
            for i, (r0, R) in enumerate(tiles):
                bk = bring.next()
                for kc in range(KC):
                    mm(bk[0:R, :], hT[:, kc, r0:r0 + R], w[:, kc, :], kc == 0, kc == KC - 1, [w.b, hT.b], [bk.b])
                vo = tf.next()
                act(vo[0:R, :], bk[0:R, :], AF.Copy, [bk.b], [vo.b])
                act(Vs[0:R, 4 * cg:4 * cg + 4, i, :], bk[0:R, :].rearrange("p (h d) -> p h d", d=128), AF.Copy, [bk.b], [Vs.b])
                dma("pool", yv[row0 + r0:row0 + r0 + R, cg * 512:(cg + 1) * 512], vo[0:R, :], [vo.b], [])
        if prompt:
            dma("pool", KV[:, kb_new, :, 512:1024].rearrange("h p (t d) -> p h t d", d=128), Vs[:], [Vs.b],
                [DB((kvname, kb_new, "v"))])
        else:
            dma("pool", KV[:, kb_new, 0:TS, 512:640].rearrange("h p d -> p h d"), Vs[0:TS, :, 0, :], [Vs.b],
                [DB((kvname, kb_new, "v"))])
        ut_ = utail[path]
        for c in range(8):
            w = load_w(Wcv, "Wcv", c, 8, 384)
            bx = bring.next(); bgb = bring.next(); bgc = bring.next()
            for (bk, q) in ((bx, 0), (bgb, 1), (bgc, 2)):
                for kc in range(KC):
                    mm(bk[:, 0:TB], w[:, kc, q * 128:(q + 1) * 128], hT[:, kc, 0:TB], kc == 0, kc == KC - 1, [w.b, hT.b], [bk.b])
            xsf = tf.next()
            act(xsf[:, 0:TB], bx[:, 0:TB], AF.Copy, [bx.b], [xsf.b])
            u = utr.next()
            cp("pool", u[:, 0:2], ut_[:, c, :], [ut_.b], [u.b])
            tt("dve", u[:, 2:2 + TB], bgc[:, 0:TB], xsf[:, 0:TB], ALU.mult, [bgc.b, xsf.b, u.b], [u.b])
            cp("pool", ut_[:, c, :], u[:, TB:TB + 2], [u.b], [ut_.b])
            cvt = tf.next()
            ts("dve", cvt[:, 0:TB], u[:, 2:2 + TB], cw[:, c, 2:3], None, ALU.mult, None, [u.b, cw.b], [cvt.b])
            stt(cvt[:, 0:TB], u[:, 1:1 + TB], cw[:, c, 1:2], cvt[:, 0:TB], ALU.mult, ALU.add, [u.b, cw.b, cvt.b], [cvt.b])
            stt(cvt[:, 0:TB], u[:, 0:TB], cw[:, c, 0:1], cvt[:, 0:TB], ALU.mult, ALU.add, [u.b, cw.b, cvt.b], [cvt.b])
            tt("dve", zT[:, c, 0:TB], bgb[:, 0:TB], cvt[:, 0:TB], ALU.mult, [bgb.b, cvt.b], [zT.b, actT.b])
        nkb = (b + 1) if prompt else 3
        Sp = [(banks[0], banks[1]), (banks[2], banks[3])]
        O1, O2, LB, B7 = banks[4], banks[5], banks[6], banks[7]
        jobs = []
        for h in range(NH):
            hj = []
            for kb in range(nkb):
                diag = prompt and kb == b
                ntile = 1 if (not prompt and kb == 2) else 4
                for j in range(ntile):
                    KR = TS if (not prompt and kb == 2) else 128
                    hj.append((kb, j, KR, 128 * j if diag else 0, diag))
            for n, (kb, j, KR, q0, diag) in enumerate(hj):
                jobs.append((h, n, len(hj), kb, j, KR, q0, diag))
        G = len(jobs)
        njh = G // NH
        dA = max(1, min(8, njh - 2))
        dB = max(1, min(5, njh - 1))
        kvrec = {}
        ptof = {}
        st = {"spair": 0}
        deferred = []

        def next_pair():
            p = st["spair"] % 2
            st["spair"] += 1
            return p

        def emit_S(g):
            (h, n, nj, kb, j, KR, q0, diag) = jobs[g]
            if (h, kb) not in kvrec:
                kv = kvring.next()
                dma("sp", kv[:], KV[h, kb, :, :], [DB((kvname, kb, "k")), DB((kvname, kb, "v"))], [kv.b])
                kvrec[(h, kb)] = kv
            kv = kvrec[(h, kb)]
            NQ = TB - q0
            p = next_pair()
            sa, sb_ = Sp[p]
            mm(sa[0:KR, 0:NQ], kv[0:64, j * 128:j * 128 + KR], QT[0:64, h, 0, q0:TB], True, True, [kv.b, QTb[h]], [sa.b])
            mm(sb_[0:KR, 0:NQ], kv[64:128, j * 128:j * 128 + KR], QT[64:128, h, 0, q0:TB], True, True, [kv.b, QTb[h]], [sb_.b])
            pt = ptring.next()
            ptof[g] = pt
            act(pt[0:KR, :, q0:TB], pall[0:KR, p * 1024:(p + 1) * 1024].rearrange("p (m c) -> p m c", m=2)[:, :, 0:NQ],
                AF.Exp, [sa.b, sb_.b], [pt.b], scale=0.125)
            if diag:
                mset("dve", pt[64:128, :, q0:q0 + 64], 0.0, [pt.b])
            acc = accs[h % 2]
            ab = accD[h % 2]
            if n == 0:
                sc.add("dve", lambda e, a=acc[0:KR, q0:TB], p_=pt[0:KR, 0, q0:TB]: e.tensor_copy(out=a, in_=p_), [pt.b], [ab])
            else:
                sc.add("dve", lambda e, a=acc[0:KR, q0:TB], p_=pt[0:KR, 0, q0:TB]: e.tensor_tensor(out=a, in0=a, in1=p_, op=ALU.add),
                       [pt.b, ab], [ab])

        def epiB(h, ta, rb, sq):
            nb = B7
            mm(nb[:, 0:TB], ones128[:], sq[:, 0:TB], True, True, [sq.b, ones128.b], [nb.b])
            act(rb[:, 0:TB], nb[:, 0:TB], AF.Ln, [nb.b, epsT.b], [rb.b], bias=epsT[:, 0:1], scale=1.0)
            act(rb[:, 0:TB], rb[:, 0:TB], AF.Exp, [rb.b], [rb.b], scale=-0.5)
            stt(OT[:, h, 0:TB], ta[:, 0:TB], subg2[:, 0:1], rb[:, 0:TB], ALU.mult, ALU.mult, [ta.b, rb.b, subg2.b], [OTb[h], actT.b])

        def epiA(h):
            par = h % 2
            acc = accs[par]
            mm(B7[:, 0:TB], onesf[:], acc[:, 0:TB], True, True, [accD[par], onesf.b], [B7.b])
            ra = tf.next(); rb = tf.next(); ta = tf.next(); tb_ = tf.next()
            recip(ra[:, 0:TB], B7[:, 0:TB], [B7.b], [ra.b])
            act(rb[:, 0:TB], l2s[par][:, 0:TB], AF.Exp, [l2s[par].b], [rb.b], scale=-1.0)
            tt("dve", ta[:, 0:TB], o12s[par][:, 0, 0:TB], ra[:, 0:TB], ALU.mult, [o12s[par].b, ra.b], [ta.b])
            tt("dve", tb_[:, 0:TB], o12s[par][:, 1, 0:TB], rb[:, 0:TB], ALU.mult, [o12s[par].b, rb.b], [tb_.b])
            stt(ta[:, 0:TB], tb_[:, 0:TB], neglam[:, 0:1], ta[:, 0:TB], ALU.mult, ALU.add, [tb_.b, ta.b, neglam.b], [ta.b])
            sq = tb.next()
            tt("pool", sq[:, 0:TB], ta[:, 0:TB], ta[:, 0:TB], ALU.mult, [ta.b], [sq.b])
            deferred.append([dB, lambda: epiB(h, ta, rb, sq)])

        def emit_PV(g):
            (h, n, nj, kb, j, KR, q0, diag) = jobs[g]
            kv = kvrec[(h, kb)]
            pt = ptof.pop(g)
            first = n == 0
            last = n == nj - 1
            vt = kv[0:KR, 512 + j * 128:512 + (j + 1) * 128]
            mm(O1[:, q0:TB], vt, pt[0:KR, 0, q0:TB], first, last, [kv.b, pt.b], [O1.b])
            mm(O2[:, q0:TB], vt, pt[0:KR, 1, q0:TB], first, last, [kv.b, pt.b], [O2.b])
            mm(LB[:, q0:TB], ones[0:KR, :], pt[0:KR, 1, q0:TB], first, last, [ones.b, pt.b], [LB.b])
            if last:
                par = h % 2
                act(o12s[par][:, :, 0:TB], pall[:, 4 * 512:6 * 512].rearrange("p (m c) -> p m c", m=2)[:, :, 0:TB],
                    AF.Copy, [O1.b, O2.b], [o12s[par].b])
                act(l2s[par][:, 0:TB], LB[:, 0:TB], AF.Ln, [LB.b], [l2s[par].b])
                deferred.append([dA, lambda: epiA(h)])

        def tick():
            for d in deferred:
                d[0] -= 1
            while deferred and deferred[0][0] <= 0:
                deferred.pop(0)[1]()

        emit_S(0)
        if G > 1:
            emit_S(1)
        for g in range(G):
            if g + 2 < G:
                emit_S(g + 2)
            emit_PV(g)
            tick()
        while deferred:
            deferred.pop(0)[1]()
        for j in range(8):
            w = load_w(Wmg, "Wmg", j, 8, 512)
            bga = bring.next(); bgb = bring.next(); bya = bring.next(); byb = bring.next()
            for kc in range(KC):
                mm(bga[:, 0:TB], w[:, kc, 0:128], hT[:, kc, 0:TB], kc == 0, kc == KC - 1, [w.b, hT.b], [bga.b])
            for kc in range(KC):
                mm(bgb[:, 0:TB], w[:, kc, 128:256], hT[:, kc, 0:TB], kc == 0, kc == KC - 1, [w.b, hT.b], [bgb.b])
            for kc in range(KC):
                mm(bya[:, 0:TB], w[:, kc, 256:384], OT[:, kc, 0:TB], kc == 0, kc == KC - 1, [w.b, OTb[kc]], [bya.b])
            for kc in range(KC):
                mm(byb[:, 0:TB], w[:, kc, 384:512], zT[:, kc, 0:TB], kc == 0, kc == KC - 1, [w.b, zT.b], [byb.b])
            sga = tf.next(); sgb = tf.next()
            act(sga[:, 0:TB], bga[:, 0:TB], AF.Sigmoid, [bga.b], [sga.b])
            act(sgb[:, 0:TB], bgb[:, 0:TB], AF.Sigmoid, [bgb.b], [sgb.b])
            tt("dve", sga[:, 0:TB], bya[:, 0:TB], sga[:, 0:TB], ALU.mult, [bya.b, sga.b], [sga.b])
            tt("dve", sgb[:, 0:TB], byb[:, 0:TB], sgb[:, 0:TB], ALU.mult, [byb.b, sgb.b], [sgb.b])
            tt("pool", mT[:, j, 0:TB], sga[:, 0:TB], sgb[:, 0:TB], ALU.add, [sga.b, sgb.b], [mT.b, actT.b])
        for cg in range(2):
            w = load_w(Wo, "Wo", cg, 8, 512)
            for i, (r0, R) in enumerate(tiles):
                bk = bring.next()
                for kc in range(KC):
                    mm(bk[0:R, :], mT[:, kc, r0:r0 + R], w[:, kc, :], kc == 0, kc == KC - 1, [w.b, mT.b], [bk.b])
                t_ = tf.next()
                tt("dve", t_[0:R, :], bk[0:R, :], gateb[path][0:R, cg * 512:(cg + 1) * 512], ALU.mult,
                   [bk.b, gateb[path].b], [t_.b])
                tt("pool", xt[0:R, i, cg * 512:(cg + 1) * 512], xt[0:R, i, cg * 512:(cg + 1) * 512], t_[0:R, :], ALU.add,
                   [xt.b, t_.b], [xt.b])
        frontA(xt, tiles)
        frontB(path, TB, tiles, g2p, 24)
        if nextblk is not None:
            nprompt, nTB, ntiles, _, _, nxt_xt = blk_params(*nextblk)
            frontA(nxt_xt, ntiles)
        for f in range(NF):
            w = load_w(Wgu, "Wgu", f, 8, 256)
            bg = bring.next(); bu = bring.next()
            for kc in range(KC):
                mm(bg[:, 0:TB], w[:, kc, 0:128], hT[:, kc, 0:TB], kc == 0, kc == KC - 1, [w.b, hT.b], [bg.b])
            for kc in range(KC):
                mm(bu[:, 0:TB], w[:, kc, 128:256], hT[:, kc, 0:TB], kc == 0, kc == KC - 1, [w.b, hT.b], [bu.b])
            sg = tf.next()
            act(sg[:, 0:TB], bg[:, 0:TB], AF.Silu, [bg.b], [sg.b])
            tt("dve", actT[:, f, 0:TB], bu[:, 0:TB], sg[:, 0:TB], ALU.mult, [bu.b, sg.b], [actT.b] + ALIAS)
        if nextblk is not None:
            frontB(nextblk[0], nTB, ntiles, g1p, 0)
        for cg in range(2):
            ybk = [bring.next() for _ in tiles]
            for (k0, k1) in ((0, 8), (8, 16), (16, NF)):
                w = wring.next()
                dma("sp", w[:, 0:k1 - k0, :], Wd[cg, :, k0:k1, :], [], [w.b])
                for i, (r0, R) in enumerate(tiles):
                    for kc in range(k0, k1):
                        mm(ybk[i][0:R, :], actT[:, kc, r0:r0 + R], w[:, kc - k0, :], kc == 0, kc == NF - 1,
                           [w.b, actT.b], [ybk[i].b])
            for i, (r0, R) in enumerate(tiles):
                t_ = tf.next()
                tt("dve", t_[0:R, :], ybk[i][0:R, :], gateb[path][0:R, 1024 + cg * 512:1024 + (cg + 1) * 512], ALU.mult,
                   [ybk[i].b, gateb[path].b], [t_.b])
                tt("pool", xt[0:R, i, cg * 512:(cg + 1) * 512], xt[0:R, i, cg * 512:(cg + 1) * 512], t_[0:R, :], ALU.add,
                   [xt.b, t_.b], [xt.b])
        for i, (r0, R) in enumerate(tiles):
            dma("pool", yy[row0 + r0:row0 + r0 + R, :], xt[0:R, i, :], [xt.b], [])

    for b in range(NB):
        do_block(0, b, (0, b + 1) if b + 1 < NB else (1, 0))
    do_block(1, 0, None)
    for c in range(8):
        dma("pool", co_d.rearrange("t (c p) -> p c t", p=128)[:, c, :], utail[0][:, c, :], [utail[0].b], [],
            allow_slow_non_contiguous=True)
        dma("pool", cso_d.rearrange("t (c p) -> p c t", p=128)[:, c, :], utail[1][:, c, :], [utail[1].b], [],
            allow_slow_non_contiguous=True)
    sc.emit(nc)
    return nc


def _consts():
    c = np.zeros((128, 256), np.float32)
    c[:, 0:128] = np.eye(128, dtype=np.float32)
    for p in range(128):
        c[p, 128 + (p // 64) * 64:128 + (p // 64 + 1) * 64] = 1.0 / 64
    return c


def make_in_maps(inp, S, ncores):
    f = lambda a: np.ascontiguousarray(np.asarray(a, dtype=np.float32))
    fm8 = lambda v: f(np.asarray(v).reshape(8, 128).T)
    shared = {
        "w_ada": f(inp["w_ada"][0]), "b_ada": f(inp["b_ada"][0]).reshape(1, -1),
        "b_adaT": f(np.asarray(inp["b_ada"][0]).reshape(48, 128).T),
        "n1g": fm8(inp["norm1_g"][0]), "n2g": fm8(inp["norm2_g"][0]),
        "w_in": f(inp["w_in"][0]),
        "qg2": f(np.tile(np.asarray(inp["q_norm_g"][0]), 2).reshape(128, 1)),
        "kg2": f(np.tile(np.asarray(inp["k_norm_g"][0]), 2).reshape(128, 1)),
        "kgrow": f(np.tile(np.asarray(inp["k_norm_g"][0]), 16).reshape(1, D)),
        "lam4": f(np.concatenate([np.asarray(inp[k][0]) for k in ("lambda_q1", "lambda_k1", "lambda_q2", "lambda_k2")]).reshape(1, 256)),
        "subg": f(np.asarray(inp["sub_norm_g"][0]).reshape(128, 1)),
        "w_attn_out": f(inp["w_attn_out"][0]),
        "cwT": f(np.asarray(inp["conv_w"][0]).reshape(3, 8, 128).transpose(2, 1, 0)),
        "w_conv_out": f(inp["w_conv_out"][0]), "w_out": f(inp["w_out"][0]),
        "w_gate_up": f(inp["w_gate_up"][0]), "w_down": f(inp["w_down"][0]),
        "consts": _consts(),
    }
    maps = []
    for b in range(ncores):
        m = dict(shared)
        m["x"] = f(inp["x_prompt"][b]); m["xs"] = f(inp["x_sample"][b])
        m["ck"] = f(np.asarray(inp["cache_k"][0, b]).reshape(PAST, D))
        m["cv"] = f(np.asarray(inp["cache_v"][0, b]).reshape(PAST, D))
        m["sconvT"] = f(np.asarray(inp["state_conv"][0, b]).reshape(2, 8, 128).transpose(2, 1, 0))
        c2 = np.stack([np.asarray(inp["c_prompt"][b]), np.asarray(inp["c_sample"][b])])
        m["cT"] = f(c2.reshape(2, 8, 128).transpose(2, 1, 0))
        maps.append(m)
    return maps


_cache = {}


def run(inp, S, ncores):
    if S not in _cache:
        _cache[S] = build(S)
    nc = _cache[S]
    maps = make_in_maps(inp, S, ncores)
    res = run_bass_kernel_spmd(nc, maps, core_ids=list(range(ncores)))
    R = res.results
    st = lambda k: np.stack([np.asarray(R[b][k], dtype=np.float32) for b in range(ncores)])
    y = st("y"); ys = st("ys")
    kp = st("ko").reshape(1, ncores, S, NH, 2, 64); vp = st("vo").reshape(1, ncores, S, NH, 128)
    cp_ = st("co").reshape(1, ncores, 2, D)
    ks = st("kso").reshape(1, ncores, TS, NH, 2, 64); vs = st("vso").reshape(1, ncores, TS, NH, 128)
    cs = st("cso").reshape(1, ncores, 2, D)
    return (y, ys, kp, vp, cp_, ks, vs, cs)


def kernel(**inputs):
    S = int(np.asarray(inputs["x_prompt"]).shape[1])
    return run(inputs, S, 8)
```
